# Optimizing a Trainium2 kernel written in Bass

```python
import math
import jax, jax.numpy as jnp
from jax import lax
import numpy as np

D_MODEL = 1024
BATCH = 8
SEQ = 8192
DEPTH = 2

HEAD_DIM = 64
FOX_HEADS = 4
FOX_WIDTH = FOX_HEADS * HEAD_DIM
S5_WIDTH = 256
S5_GROUP = 16
S5_GROUPS = S5_WIDTH // S5_GROUP
S5_STATE = 64
MOBA_HEADS = 4
MOBA_WIDTH = MOBA_HEADS * HEAD_DIM
MOBA_BLOCK = 256
MOBA_TOPK = 3
MOBA_Q_CHUNK = 64
MLA_HEADS = 4
MLA_NOPE = 64
MLA_ROPE = 32
MLA_V = 64
MLA_Q_RANK = 384
MLA_KV_RANK = 128
MLA_WIDTH = MLA_HEADS * MLA_V
D_MIX = FOX_WIDTH + S5_WIDTH + MOBA_WIDTH + MLA_WIDTH
Q_BLOCK = 128
ROPE_THETA = 10000.0
EPS = 1e-6
NEG = -1e30
SPLITS = (FOX_WIDTH, FOX_WIDTH, FOX_WIDTH, FOX_WIDTH, FOX_HEADS,
          S5_WIDTH, S5_WIDTH,
          MOBA_WIDTH, MOBA_WIDTH, MOBA_WIDTH, MOBA_WIDTH,
          MLA_Q_RANK, MLA_KV_RANK, MLA_ROPE, MLA_WIDTH)
D_IN = sum(SPLITS)
SPLIT_POINTS = [int(v) for v in np.cumsum(SPLITS)[:-1]]

kernel_name = 'hybrid_fox_s5_moba_mla_trunk'


def rmsnorm(x, g):
    xf = x.astype(jnp.float32)
    y = xf * lax.rsqrt(jnp.mean(xf * xf, axis=-1, keepdims=True) + EPS) * g.astype(jnp.float32)
    return y.astype(x.dtype)


def rope(x, pos):
    d = x.shape[-1]
    half = d // 2
    inv = jnp.power(ROPE_THETA, -jnp.arange(half, dtype=jnp.float32) / half)
    ang = pos.astype(jnp.float32)[:, None] * inv[None, :]
    cos = jnp.cos(ang)[None, :, None, :]
    sin = jnp.sin(ang)[None, :, None, :]
    x1 = x[..., :half].astype(jnp.float32)
    x2 = x[..., half:].astype(jnp.float32)
    return jnp.concatenate([x1 * cos - x2 * sin, x1 * sin + x2 * cos], axis=-1).astype(x.dtype)


def dense_causal_attention(q, k, v, log_f_cum=None):
    Bsz, S, H, Dk = q.shape
    nq = S // Q_BLOCK
    scale = Dk ** -0.5
    kpos = jnp.arange(S)
    qb = q.reshape(Bsz, nq, Q_BLOCK, H, Dk).swapaxes(0, 1)
    xs = (jnp.arange(nq), qb)
    if log_f_cum is not None:
        cum_k = log_f_cum.astype(jnp.float32).transpose(0, 2, 1)
        xs = xs + (cum_k.reshape(Bsz, H, nq, Q_BLOCK).transpose(2, 0, 1, 3),)

    def one_block(args):
        i, q_blk = args[0], args[1]
        s = jnp.einsum('bqhd,bkhd->bhqk', q_blk, k).astype(jnp.float32) * scale
        if log_f_cum is not None:
            s = s + args[2][..., None] - cum_k[:, :, None, :]
        qpos = i * Q_BLOCK + jnp.arange(Q_BLOCK)
        s = jnp.where(kpos[None, :] <= qpos[:, None], s, NEG)
        p = jax.nn.softmax(s, axis=-1)
        return jnp.einsum('bhqk,bkhd->bqhd', p.astype(v.dtype), v)

    out = lax.map(one_block, xs)
    return out.swapaxes(0, 1).reshape(Bsz, S, H, v.shape[-1])


def moba_attention(q, k, v):
    Bsz, S, H, D = q.shape
    nblk = -(-S // MOBA_BLOCK)
    pad = nblk * MOBA_BLOCK - S
    padw = ((0, 0), (0, pad), (0, 0), (0, 0))
    kb = jnp.pad(k, padw).reshape(Bsz, nblk, MOBA_BLOCK, H, D).transpose(0, 3, 1, 2, 4)
    vb = jnp.pad(v, padw).reshape(Bsz, nblk, MOBA_BLOCK, H, D).transpose(0, 3, 1, 2, 4)
    kmean = kb.astype(jnp.float32).mean(axis=3)
    topk = min(MOBA_TOPK, nblk)
    nchunk = S // MOBA_Q_CHUNK
    scale = D ** -0.5
    qc = q.reshape(Bsz, nchunk, MOBA_Q_CHUNK, H, D).transpose(1, 0, 3, 2, 4)
    blk_ids = jnp.arange(nblk)
    gather = jax.vmap(jax.vmap(lambda t, idx: t[idx]))

    def one_chunk(args):
        c, q_c = args
        start = c * MOBA_Q_CHUNK
        qpos = start + jnp.arange(MOBA_Q_CHUNK)
        blk = start // MOBA_BLOCK
        gate = jnp.einsum('bhqd,bhnd->bhqn', q_c.astype(jnp.float32), kmean)
        gate = jnp.where(blk_ids < blk, gate, -jnp.inf)
        _, sel = lax.top_k(gate, topk)
        valid = sel < blk
        k_sel = gather(kb, sel)
        v_sel = gather(vb, sel)
        s_sel = jnp.einsum('bhqd,bhqnkd->bhqnk', q_c, k_sel).astype(jnp.float32) * scale
        s_sel = jnp.where(valid[..., None], s_sel, NEG).reshape(Bsz, H, MOBA_Q_CHUNK, topk * MOBA_BLOCK)
        k_own = lax.dynamic_slice_in_dim(kb, blk, 1, axis=2)[:, :, 0]
        v_own = lax.dynamic_slice_in_dim(vb, blk, 1, axis=2)[:, :, 0]
        s_own = jnp.einsum('bhqd,bhkd->bhqk', q_c, k_own).astype(jnp.float32) * scale
        kpos_own = blk * MOBA_BLOCK + jnp.arange(MOBA_BLOCK)
        s_own = jnp.where(kpos_own[None, :] <= qpos[:, None], s_own, NEG)
        p = jax.nn.softmax(jnp.concatenate([s_sel, s_own], axis=-1), axis=-1).astype(v.dtype)
        p_sel = p[..., :topk * MOBA_BLOCK].reshape(Bsz, H, MOBA_Q_CHUNK, topk, MOBA_BLOCK)
        p_own = p[..., topk * MOBA_BLOCK:]
        return (jnp.einsum('bhqnk,bhqnkd->bhqd', p_sel, v_sel)
                + jnp.einsum('bhqk,bhkd->bhqd', p_own, v_own))

    out = lax.map(one_chunk, (jnp.arange(nchunk), qc))
    return out.transpose(1, 0, 3, 2, 4).reshape(Bsz, S, H, D)


def s5_mixer(u, a_re, a_im, log_dt, b_re, b_im, c_re, c_im, d, glu_w, glu_b):
    f32 = jnp.float32
    Bsz, S, _ = u.shape
    ug = u.astype(f32).reshape(Bsz, S, S5_GROUPS, S5_GROUP)
    lam = lax.complex(a_re.astype(f32), a_im.astype(f32))
    dt = jnp.exp(log_dt.astype(f32))[:, None]
    lam_bar = jnp.exp(lam * dt)
    b = lax.complex(b_re.astype(f32), b_im.astype(f32))
    b_bar = ((lam_bar - 1.0) / lam)[..., None] * b
    bu = jnp.einsum('bsgc,gpc->bsgp', ug.astype(jnp.complex64), b_bar)
    lam_seq = jnp.broadcast_to(lam_bar, bu.shape)

    def combine(left, right):
        a_l, x_l = left
        a_r, x_r = right
        return a_r * a_l, a_r * x_l + x_r

    _, states = lax.associative_scan(combine, (lam_seq, bu), axis=1)
    c = lax.complex(c_re.astype(f32), c_im.astype(f32))
    y = jnp.einsum('bsgp,gcp->bsgc', states, c).real.reshape(Bsz, S, S5_WIDTH)
    y = y + d.astype(f32) * u.astype(f32)
    y = jax.nn.gelu(y)
    y = y * jax.nn.sigmoid(y @ glu_w.astype(f32) + glu_b.astype(f32))
    return y.astype(u.dtype)


def hybrid_layer(x, pos, norm_g, w_in, fox_fb, s5_a_re, s5_a_im, s5_log_dt, s5_b_re, s5_b_im,
                 s5_c_re, s5_c_im, s5_d, s5_glu_w, s5_glu_b, mla_q_norm, mla_w_uq, mla_kv_norm,
                 mla_w_ukv, w_out):
    Bsz, S, _ = x.shape
    h = rmsnorm(x, norm_g)
    z = h @ w_in
    (fq, fk, fv, fg, ff, su, sg, mq, mk, mv, mg, cq, ckv, kr, lg) = jnp.split(z, SPLIT_POINTS, axis=-1)
    heads = lambda t, n: t.reshape(Bsz, S, n, -1)

    log_f = jax.nn.log_sigmoid(ff.astype(jnp.float32) + fox_fb.astype(jnp.float32))
    cum = jnp.cumsum(log_f, axis=1)
    a_out = dense_causal_attention(heads(fq, FOX_HEADS), heads(fk, FOX_HEADS), heads(fv, FOX_HEADS), cum)
    a_out = a_out.reshape(Bsz, S, FOX_WIDTH) * jax.nn.silu(fg)

    b_out = s5_mixer(su, s5_a_re, s5_a_im, s5_log_dt, s5_b_re, s5_b_im, s5_c_re, s5_c_im,
                     s5_d, s5_glu_w, s5_glu_b) * jax.nn.silu(sg)

    c_out = moba_attention(rope(heads(mq, MOBA_HEADS), pos), rope(heads(mk, MOBA_HEADS), pos),
                           heads(mv, MOBA_HEADS))
    c_out = c_out.reshape(Bsz, S, MOBA_WIDTH) * jax.nn.silu(mg)

    qf = (rmsnorm(cq, mla_q_norm) @ mla_w_uq).reshape(Bsz, S, MLA_HEADS, MLA_NOPE + MLA_ROPE)
    q_nope, q_r = qf[..., :MLA_NOPE], rope(qf[..., MLA_NOPE:], pos)
    kv = (rmsnorm(ckv, mla_kv_norm) @ mla_w_ukv).reshape(Bsz, S, MLA_HEADS, MLA_NOPE + MLA_V)
    k_nope, v_d = kv[..., :MLA_NOPE], kv[..., MLA_NOPE:]
    k_r = jnp.broadcast_to(rope(kr.reshape(Bsz, S, 1, MLA_ROPE), pos), (Bsz, S, MLA_HEADS, MLA_ROPE))
    d_out = dense_causal_attention(jnp.concatenate([q_nope, q_r], axis=-1),
                                   jnp.concatenate([k_nope, k_r], axis=-1), v_d)
    d_out = d_out.reshape(Bsz, S, MLA_WIDTH) * jax.nn.silu(lg)

    mix = jnp.concatenate([a_out, b_out, c_out, d_out], axis=-1)
    return x + (mix @ w_out).astype(x.dtype)


def setup_inputs(seed: int = 0) -> dict:
    key = jax.random.key(seed)
    ks = jax.random.split(key, 20)
    f32 = jnp.float32
    nrm = lambda k, shape, s: s * jax.random.normal(k, shape, f32)
    L, G, P, C = DEPTH, S5_GROUPS, S5_STATE, S5_GROUP
    x = jax.random.normal(ks[0], (BATCH, SEQ, D_MODEL), f32)
    norm_g = 1.0 + nrm(ks[1], (L, D_MODEL), 0.02)
    w_in = nrm(ks[2], (L, D_MODEL, D_IN), D_MODEL ** -0.5)
    fox_fb = 3.0 + nrm(ks[3], (L, FOX_HEADS), 0.1)
    s5_a_re = -0.5 + nrm(ks[4], (L, G, P), 0.01)
    s5_a_im = jnp.pi * jnp.arange(P, dtype=f32) + nrm(ks[5], (L, G, P), 0.01)
    s5_log_dt = jax.random.uniform(ks[6], (L, G), f32, math.log(1e-3), math.log(1e-1))
    s5_b_re = nrm(ks[7], (L, G, P, C), (2 * C) ** -0.5)
    s5_b_im = nrm(ks[8], (L, G, P, C), (2 * C) ** -0.5)
    s5_c_re = nrm(ks[9], (L, G, C, P), (2 * P) ** -0.5)
    s5_c_im = nrm(ks[10], (L, G, C, P), (2 * P) ** -0.5)
    s5_d = nrm(ks[11], (L, S5_WIDTH), 1.0)
    s5_glu_w = nrm(ks[12], (L, S5_WIDTH, S5_WIDTH), S5_WIDTH ** -0.5)
    s5_glu_b = nrm(ks[13], (L, S5_WIDTH), 0.02)
    mla_q_norm = 1.0 + nrm(ks[14], (L, MLA_Q_RANK), 0.02)
    mla_w_uq = nrm(ks[15], (L, MLA_Q_RANK, MLA_HEADS * (MLA_NOPE + MLA_ROPE)), MLA_Q_RANK ** -0.5)
    mla_kv_norm = 1.0 + nrm(ks[16], (L, MLA_KV_RANK), 0.02)
    mla_w_ukv = nrm(ks[17], (L, MLA_KV_RANK, MLA_HEADS * (MLA_NOPE + MLA_V)), MLA_KV_RANK ** -0.5)
    w_out = nrm(ks[18], (L, D_MIX, D_MODEL), D_MIX ** -0.5)
    final_g = 1.0 + nrm(ks[19], (D_MODEL,), 0.02)
    return {'x': x, 'norm_g': norm_g, 'w_in': w_in, 'fox_fb': fox_fb,
            's5_a_re': s5_a_re, 's5_a_im': s5_a_im, 's5_log_dt': s5_log_dt,
            's5_b_re': s5_b_re, 's5_b_im': s5_b_im, 's5_c_re': s5_c_re, 's5_c_im': s5_c_im,
            's5_d': s5_d, 's5_glu_w': s5_glu_w, 's5_glu_b': s5_glu_b,
            'mla_q_norm': mla_q_norm, 'mla_w_uq': mla_w_uq, 'mla_kv_norm': mla_kv_norm,
            'mla_w_ukv': mla_w_ukv, 'w_out': w_out, 'final_g': final_g}


def reference(x, norm_g, w_in, fox_fb, s5_a_re, s5_a_im, s5_log_dt, s5_b_re, s5_b_im, s5_c_re,
              s5_c_im, s5_d, s5_glu_w, s5_glu_b, mla_q_norm, mla_w_uq, mla_kv_norm, mla_w_ukv,
              w_out, final_g):
    pos = jnp.arange(x.shape[1])
    for l in range(DEPTH):
        x = hybrid_layer(x, pos, norm_g[l], w_in[l], fox_fb[l], s5_a_re[l], s5_a_im[l], s5_log_dt[l],
                         s5_b_re[l], s5_b_im[l], s5_c_re[l], s5_c_im[l], s5_d[l], s5_glu_w[l],
                         s5_glu_b[l], mla_q_norm[l], mla_w_uq[l], mla_kv_norm[l], mla_w_ukv[l], w_out[l])
    return rmsnorm(x, final_g)
```

```python
import math
from contextlib import ExitStack
import numpy as np
import ml_dtypes
import concourse.bass as bass
import concourse.mybir as mybir
from concourse.bass_utils import run_bass_kernel_spmd

F32 = mybir.dt.float32
BF = mybir.dt.bfloat16
AF = mybir.ActivationFunctionType
ALU = mybir.AluOpType
AX = mybir.AxisListType

D = 1024
DIN = 3364
DEPTH = 2
EPS = 1e-6
O_FQ, O_FK, O_FV, O_FG, O_FF = 0, 256, 512, 768, 1024
O_SU, O_SG = 1028, 1284
O_MQ, O_MK, O_MV, O_MG = 1540, 1796, 2052, 2308
O_CQ, O_CKV, O_KR, O_LG = 2564, 2948, 3076, 3108
NEGB = -30000.0
KDIM = [70] * 4 + [96] * 4 + [96] * 4
SEM_LIMIT = 20000
DMA_RING = 12
import os
CUT = int(os.environ.get('P1CUT', '99'))
SUB = int(os.environ.get('P1SUB', '99'))


class Buf:
    __slots__ = ("w", "r", "rd")

    def __init__(self):
        self.w = None
        self.r = {}
        self.rd = []


class Op:
    __slots__ = ("eng", "fn", "deps", "marked", "sem", "val", "dma")


class Prog:
    ENG = ("pe", "act", "dve", "pool", "sp")

    def __init__(self, nc):
        self.nc = nc
        self.streams = {e: [] for e in self.ENG}
        self.dmas = []
        self.nsem = 0

    def op(self, eng, fn, r=(), w=(), dma=False):
        o = Op()
        o.eng, o.fn, o.dma, o.marked, o.sem, o.val = eng, fn, dma, dma, None, 0
        deps = set()
        for b in r:
            if b.w is not None:
                deps.add(b.w)
        for b in w:
            if b.w is not None:
                deps.add(b.w)
            deps.update(b.r.values())
            deps.update(b.rd)
        if eng == "pe" and not dma:
            deps = {d for d in deps if not (d.eng == "pe" and not d.dma)}
        o.deps = list(deps)
        for d in o.deps:
            d.marked = True
        for b in r:
            if dma:
                b.rd.append(o)
            else:
                b.r[eng] = o
        for b in w:
            b.w = o
            b.r = {}
            b.rd = []
        self.streams[eng].append(o)
        if dma:
            self.dmas.append(o)
        return o

    def barrier(self):
        lasts = []
        for s in self.streams.values():
            for o in reversed(s):
                if o.fn is not None:
                    lasts.append(o)
                    break
        pend = list(self.dmas)
        self.dmas = []
        for e in self.ENG:
            o = Op()
            o.eng, o.fn, o.dma, o.marked, o.sem, o.val = e, None, False, False, None, 0
            o.deps = list(set(lasts + pend))
            for d in o.deps:
                d.marked = True
            self.streams[e].append(o)

    def mm(self, out, lhsT, rhs, start, stop, r, w):
        return self.op("pe", lambda e: e.matmul(out, lhsT, rhs, start=start, stop=stop), r, w)

    def tr(self, out, in_, ident, r, w):
        return self.op("pe", lambda e: e.transpose(out, in_, ident), r, w)

    def act(self, out, in_, func, r, w, bias=None, scale=None, accum=None):
        kw = {}
        if bias is not None:
            kw["bias"] = bias
        if scale is not None:
            kw["scale"] = scale
        if accum is not None:
            kw["accum_out"] = accum
        return self.op("act", lambda e: e.activation(out, in_, func, **kw), r, w)

    def ts(self, out, in0, s1, s2, op0, op1, r, w, eng="dve"):
        if op1 is None:
            return self.op(eng, lambda e: e.tensor_scalar(out, in0, s1, None, op0), r, w)
        return self.op(eng, lambda e: e.tensor_scalar(out, in0, s1, s2, op0, op1), r, w)

    def tt(self, out, in0, in1, op, r, w, eng="dve"):
        return self.op(eng, lambda e: e.tensor_tensor(out, in0, in1, op), r, w)

    def stt(self, out, in0, sc, in1, op0, op1, r, w):
        return self.op("dve", lambda e: e.scalar_tensor_tensor(out, in0, sc, in1, op0, op1), r, w)

    def cp(self, out, in_, r, w, eng="dve"):
        if eng == "act":
            return self.op("act", lambda e: e.copy(out, in_), r, w)
        return self.op(eng, lambda e: e.tensor_copy(out, in_), r, w)

    def ms(self, ap, val, w, eng="dve"):
        return self.op(eng, lambda e: e.memset(ap, val), (), w)

    def dma(self, out, in_, r, w, eng="sp"):
        return self.op(eng, lambda e: e.dma_start(out=out, in_=in_), r, w, dma=True)

    def finalize(self, final_deps):
        nc = self.nc
        sems = []

        def newsem():
            s = nc.alloc_semaphore("s%d" % len(sems))
            sems.append(s)
            return len(sems) - 1

        for e in self.ENG:
            fo = Op()
            fo.eng, fo.fn, fo.dma, fo.marked, fo.sem, fo.val = e, None, False, False, None, 0
            fo.deps = list(final_deps) if e == "sp" else []
            self.streams[e].append(fo)
        for e in self.ENG:
            cnt, sem = 0, None
            ring, rval, rlast, nd = [], [], [], 0
            for o in self.streams[e]:
                if o.fn is None:
                    continue
                if o.dma:
                    if len(ring) < DMA_RING:
                        ring.append(newsem())
                        rval.append(0)
                        rlast.append(None)
                    slot = nd % DMA_RING
                    nd += 1
                    if rlast[slot] is not None:
                        o.deps.append(rlast[slot])
                    rval[slot] += 16
                    o.sem, o.val = ring[slot], rval[slot]
                    rlast[slot] = o
                elif o.marked:
                    if sem is None or cnt >= SEM_LIMIT:
                        sem, cnt = newsem(), 0
                    cnt += 1
                    o.sem, o.val = sem, cnt
        self.nsem = len(sems)
        streams = self.streams

        def mk(eng):
            def body(e):
                waited = {}
                for o in streams[eng]:
                    for d in o.deps:
                        if d.sem is None:
                            continue
                        if waited.get(d.sem, 0) < d.val:
                            e.wait_ge(sems[d.sem], d.val)
                            waited[d.sem] = d.val
                    if o.fn is None:
                        continue
                    ins = o.fn(e)
                    if o.sem is not None:
                        ins.then_inc(sems[o.sem], 16 if o.dma else 1)
            return body

        with nc.Block() as block:
            block.tensor(mk("pe"))
            block.scalar(mk("act"))
            block.vector(mk("dve"))
            block.gpsimd(mk("pool"))
            block.sync(mk("sp"))


class Tile:
    __slots__ = ("t", "b")

    def __init__(self, t):
        self.t = t
        self.b = Buf()

    def __getitem__(self, k):
        return self.t[k]


class View:
    __slots__ = ("t", "b")

    def __init__(self, ap):
        self.t = ap
        self.b = Buf()

    def __getitem__(self, k):
        return self.t[k]


class Rot:
    def __init__(self, tiles):
        self.tiles = tiles
        self.i = 0

    def next(self):
        t = self.tiles[self.i % len(self.tiles)]
        self.i += 1
        return t


class NS:
    pass


def build_fused(S, depth=DEPTH, dbg=None, stop=None):
    g = NS()
    nc = bass.Bass("TRN2", target_bir_lowering=False)
    g.nc, g.P = nc, Prog(nc)
    P = g.P
    g.S, g.NT, g.NCH = S, S // 128, S // 512
    NCH = g.NCH
    g.es = ExitStack()
    din = lambda name, shape, dt=F32: nc.dram_tensor(name, list(shape), dt, kind="ExternalInput").ap()
    dsc = lambda name, shape, dt: nc.dram_tensor(name, list(shape), dt, kind=("ExternalOutput" if dbg else "Internal")).ap()
    g.sb = lambda name, shape, dt=F32: Tile(g.es.enter_context(nc.sbuf_tensor("sb_" + name, list(shape), dt)))
    g.pbig = g.es.enter_context(nc.psum_tensor("pbig", [128, 3072], F32))
    g.banks = [View(g.pbig[:, i * 512:(i + 1) * 512]) for i in range(6)]
    g.banks += [Tile(g.es.enter_context(nc.psum_tensor("bank%d" % i, [128, 512], F32))) for i in range(6, 8)]
    x_in = din("x", [S, D])
    out_d = nc.dram_tensor("out", [S, D], F32, kind="ExternalOutput").ap()
    W = NS()
    W.w_in = din("w_in", [depth, D, DIN])
    W.w_out = din("w_out", [depth, D, D])
    W.norm_g = din("norm_g", [depth, 128, 8])
    W.final_g = din("final_g", [128, D])
    W.fox_fb = din("fox_fb", [depth, 4, 1])
    W.wuq = din("mla_w_uq", [depth, 384, 384])
    W.qng = din("mla_q_norm", [depth, 128, 3])
    W.wukv = din("mla_w_ukv", [depth, 128, 512])
    W.kvng = din("mla_kv_norm", [depth, 128, 1])
    W.s5a = din("s5_a", [depth, 3, 128, 16])
    W.s5a16 = din("s5_a16", [depth, 3, 16, 1024])
    W.s5b = din("s5_b", [depth, 2, 16, 1024])
    W.s5c = din("s5_c", [depth, 2, 128, 256])
    W.s5d = din("s5_d", [depth, 128, 2])
    W.gluw = din("s5_glu_w", [depth, 256, 256])
    W.glub = din("s5_glu_b", [depth, 128, 2])
    g.rope_d = din("rope", [2, 2, 128, S])
    g.ident_d = din("ident", [128, 128], BF)
    g.negm_d = din("negm", [128, 4, 512], BF)
    g.blkind_d = din("blkind", [32, S], BF)
    g.cmask_d = din("cmask", [NCH, 128, 512])
    g.notown_d = din("notown", [NCH, 128, 512])
    g.idf_d = din("identf", [128, 128])
    g.jsw_d = din("swapj", [128, 128])
    g.QA = dsc("QA", [12, 96, S], BF)
    g.KA = dsc("KA", [12, 96, S], BF)
    g.VS = dsc("VS", [12, 128, g.NT, 65], BF)
    g.GS = dsc("GS", [1024, S], BF)
    g.UT = dsc("UT", [256, S], BF)
    g.MIX = dsc("MIX", [1024, S], BF)
    g.XRES = dsc("XRES", [S, D], F32)
    g.RL = dsc("RL", [12 * NCH, 512], F32)
    g.bQA = [[Buf() for _ in range(NCH)] for _ in range(12)]
    g.bKA = [[Buf() for _ in range(NCH)] for _ in range(12)]
    g.bVS = [[Buf() for _ in range(NCH)] for _ in range(12)]
    g.bGS = [[Buf() for _ in range(NCH)] for _ in range(8)]
    g.bUT = [Buf() for _ in range(NCH)]
    g.ones_bf = g.sb("ones_bf", [128, 512], BF)
    g.ones_f = g.sb("ones_f", [128, 512], F32)
    g.epsc = g.sb("epsc", [128, 1], F32)
    g.onec = g.sb("onec", [128, 1], F32)
    g.ident = g.sb("ident", [128, 128], BF)
    zt = g.sb("zt", [32, 512], BF)
    P.ms(g.ones_bf[:], 1.0, [g.ones_bf.b])
    P.ms(g.ones_f[:], 1.0, [g.ones_f.b])
    P.ms(g.epsc[:], EPS, [g.epsc.b])
    P.ms(g.onec[:], 1.0, [g.onec.b])
    P.ms(zt[:], 0.0, [zt.b])
    P.dma(g.ident[:], g.ident_d, (), [g.ident.b])
    for h in range(4):
        for c in range(NCH):
            cs = slice(c * 512, (c + 1) * 512)
            P.dma(g.QA[h, 67:70, cs], g.ones_bf[0:3, :], [g.ones_bf.b], [g.bQA[h][c]])
            P.dma(g.KA[h, 64:67, cs], g.ones_bf[0:3, :], [g.ones_bf.b], [g.bKA[h][c]])
            P.dma(g.QA[h, 70:96, cs], zt[0:26, :], [zt.b], [g.bQA[h][c]])
            P.dma(g.KA[h, 70:96, cs], zt[0:26, :], [zt.b], [g.bKA[h][c]])
    P.barrier()
    for l in range(depth):
        last = l == depth - 1
        g.uid = l
        g.x_in = x_in if l == 0 else g.XRES
        g.w_in_d, g.normg_d, g.foxfb_d = W.w_in[l], W.norm_g[l], W.fox_fb[l]
        g.wuq_d, g.qng_d, g.wukv_d, g.kvng_d = W.wuq[l], W.qng[l], W.wukv[l], W.kvng[l]
        phase1(g)
        P.barrier()
        if dbg == (l, 1) or stop == (l, 1):
            break
        phase2f(g)
        P.barrier()
        if dbg == (l, 2) or stop == (l, 2):
            break
        phase3f(g, W.s5a[l], W.s5a16[l], W.s5b[l], W.s5c[l], W.s5d[l], W.gluw[l], W.glub[l])
        P.barrier()
        if dbg == (l, 3) or stop == (l, 3):
            break
        phase4f(g, W.w_out[l], W.final_g, g.x_in, (out_d if last else g.XRES), last)
        P.barrier()
    finals = [o for s in P.streams.values() for o in s if o.dma]
    P.finalize(finals)
    g.es.close()
    return nc


def phase2f(g):
    nc, P, banks, S, NT, NCH = g.nc, g.P, g.banks, g.S, g.NT, g.NCH
    UT_ = 3
    with ExitStack() as es:
        sb = lambda name, shape, dt=F32: Tile(es.enter_context(nc.sbuf_tensor("p2_%d_" % g.uid + name, list(shape), dt)))
        negm = sb("negm", [128, 4, 512], BF)
        P.dma(negm[:], g.negm_d, (), [negm.b])
        Qs = [sb("Qt%d" % i, [96, S], BF) for i in range(2)]
        Ks = [sb("Kt%d" % i, [96, S], BF) for i in range(2)]
        Vs = [sb("Vt%d" % i, [128, NT, 65], BF) for i in range(2)]
        Gs = [sb("Gt%d" % i, [64, S], BF) for i in range(2)]
        scb = [View(g.pbig[:, 0:1536]), View(g.pbig[:, 1536:3072])]
        oacc = Rot(banks[6:8])
        pts = Rot([sb("pt%d" % i, [128, 512 * UT_], BF) for i in range(4)])
        rl = Rot([sb("rl%d" % i, [128, 512]) for i in range(5)])
        rlb = Rot([sb("rlb%d" % i, [64, 512]) for i in range(5)])
        osb = Rot([sb("osb%d" % i, [65, 512]) for i in range(5)])
        ot = Rot([sb("ot%d" % i, [64, 512]) for i in range(2)])
        om = Rot([sb("om%d" % i, [64, 512], BF) for i in range(2)])
        grow = lambda h: (64 * h if h < 4 else 256 + 64 * h)

        def load(h):
            Qt, Kt, Vt, Gt = Qs[h % 2], Ks[h % 2], Vs[h % 2], Gs[h % 2]
            moba = 4 <= h < 8
            for c in range(NCH):
                cs = slice(c * 512, (c + 1) * 512)
                P.dma(Qt[:, cs], g.QA[h, :, cs], (), [Qt.b], eng="pool")
                if moba:
                    P.dma(Kt[0:64, cs], g.KA[h, 0:64, cs], (), [Kt.b], eng="pool")
                    P.dma(Kt[64:96, cs], g.blkind_d[:, cs], (), [Kt.b], eng="pool")
                else:
                    P.dma(Kt[:, cs], g.KA[h, :, cs], (), [Kt.b], eng="pool")
            P.dma(Vt[:], g.VS[h], (), [Vt.b], eng="pool")
            P.dma(Gt[:], g.GS[grow(h):grow(h) + 64, :], (), [Gt.b], eng="pool")

        units = []
        for h in range(12):
            for qc in range(NCH):
                nk = 4 * qc + 4
                k0 = 0
                while k0 < nk:
                    n_ = min(UT_, nk - k0)
                    units.append((h, qc, k0, n_, k0 + n_ == nk))
                    k0 += n_
        state = {"oa": None}
        sbuf_of = {}
        pt_of = {}

        def emit_qk(n):
            h, qc, k0, n_, _ = units[n]
            Qt, Kt = Qs[h % 2], Ks[h % 2]
            qs = slice(qc * 512, (qc + 1) * 512)
            sp = scb[n % 2]
            sbuf_of[n] = sp
            for t in range(n_):
                kt = k0 + t
                j = kt - 4 * qc
                ks = slice(kt * 128, (kt + 1) * 128)
                dst = sp[:, t * 512:(t + 1) * 512]
                if j >= 0:
                    P.mm(dst, g.ident[:], negm[:, j, :], True, False, [g.ident.b, negm.b], [sp.b])
                    P.mm(dst, Kt[:, ks], Qt[:, qs], False, True, [Kt.b, Qt.b], [sp.b])
                else:
                    P.mm(dst, Kt[:, ks], Qt[:, qs], True, True, [Kt.b, Qt.b], [sp.b])

        def emit_exp(n):
            n_ = units[n][3]
            sp = sbuf_of.pop(n)
            pt = pts.next()
            pt_of[n] = pt
            P.act(pt[:, 0:512 * n_], sp[:, 0:512 * n_], AF.Exp, [sp.b], [pt.b], scale=0.125)

        pend = []

        def epi_head(h, qc, oa):
            a, rb, o1 = rl.next(), rlb.next(), osb.next()
            P.cp(o1[:], oa[0:65, :], [oa.b], [o1.b])
            P.op("dve", lambda e, a=a, o1=o1: e.reciprocal(a[64:65, :], o1[64:65, :]), [o1.b], [a.b])
            rw = g.RL[h * NCH + qc:h * NCH + qc + 1, :]
            bw = Buf()
            P.dma(rw, a[64:65, :], [a.b], [bw])
            P.dma(rb[:], rw.to_broadcast([64, 512]), [bw], [rb.b])
            pend.append([0, h, qc, rb, o1])

        def epi_tail(h, qc, rb, o1):
            Gt = Gs[h % 2]
            qs = slice(qc * 512, (qc + 1) * 512)
            g0 = grow(h)
            o2, o3 = ot.next(), om.next()
            P.tt(o2[:], o1[0:64, :], rb[:], ALU.mult, [o1.b, rb.b], [o2.b])
            P.tt(o3[:], o2[:], Gt[:, qs], ALU.mult, [o2.b, Gt.b], [o3.b])
            P.dma(g.MIX[g0:g0 + 64, qs], o3[:], [o3.b], [Buf()])

        def flush(min_age):
            while pend and pend[0][0] >= min_age:
                _, h2, qc2, rb, o1 = pend.pop(0)
                epi_tail(h2, qc2, rb, o1)

        def emit_pv(n):
            h, qc, k0, n_, lastu = units[n]
            Vt = Vs[h % 2]
            nk = 4 * qc + 4
            if k0 == 0:
                state["oa"] = oacc.next()
            oa = state["oa"]
            pt = pt_of.pop(n)
            for t in range(n_):
                kt = k0 + t
                P.mm(oa[0:65, :], Vt[:, kt, :], pt[:, t * 512:(t + 1) * 512], kt == 0, kt == nk - 1, [Vt.b, pt.b], [oa.b])
            if lastu:
                epi_head(h, qc, oa)
            for p_ in pend:
                p_[0] += 1
            flush(6)

        load(0)
        N = len(units)
        emit_qk(0)
        for n in range(N):
            h, qc, k0 = units[n][0], units[n][1], units[n][2]
            if h + 1 < 12 and ((NCH > 3 and qc == 3 and k0 == 0) or (NCH <= 3 and qc == 0 and k0 == 0)):
                if NCH <= 3:
                    flush(0)
                load(h + 1)
            emit_exp(n)
            if n + 1 < N:
                emit_qk(n + 1)
            emit_pv(n)
        flush(0)


def phase4f(g, wo_d, fg_d, x_d, dst_d, last):
    nc, P, banks, NCH = g.nc, g.P, g.banks, g.NCH
    with ExitStack() as es:
        sb = lambda name, shape, dt=F32: Tile(es.enter_context(nc.sbuf_tensor("p4_%d_" % g.uid + name, list(shape), dt)))
        wo = sb("wo", [128, 8, 1024], BF)
        wst = [sb("wst%d" % i, [128, 1024]) for i in range(2)]
        mixs = [sb("mix%d" % i, [128, 8, 512], BF) for i in range(2)]
        xts = [sb("xt%d" % i, [128, 4, 1024]) for i in range(2)]
        xns = [sb("xn%d" % i, [128, 4, 1024]) for i in range(2)]
        yo = sb("yo", [128, 4, 1024])
        gB = sb("gB", [128, 1024])
        junk = sb("junk", [128, 1024], BF)
        ss, sd, rstd = sb("ss", [128, 4]), sb("sd", [128, 4]), sb("rstd", [128, 4])
        P.dma(gB[:], fg_d, (), [gB.b])
        for kt in range(8):
            st = wst[kt % 2]
            P.dma(st[:], wo_d[kt * 128:(kt + 1) * 128, :], (), [st.b])
            P.cp(wo[:, kt, :], st[:], [st.b], [wo.b], eng=("dve" if kt % 2 == 0 else "act"))
        pr = Rot(banks[0:4])

        def load(c):
            cs = slice(c * 512, (c + 1) * 512)
            P.dma(mixs[c % 2][:], g.MIX[:, cs].rearrange("(kt p) t -> p kt t", p=128), (), [mixs[c % 2].b])
            P.dma(xts[c % 2][:], x_d[cs, :].rearrange("(j p) d -> p j d", p=128), (), [xts[c % 2].b])

        load(0)
        for c in range(NCH):
            if c + 1 < NCH:
                load(c + 1)
            cs = slice(c * 512, (c + 1) * 512)
            mix, xt, xn = mixs[c % 2], xts[c % 2], xns[c % 2]
            for j in range(4):
                for half in range(2):
                    ps = pr.next()
                    for kt in range(8):
                        P.mm(ps[:], mix[:, kt, j * 128:(j + 1) * 128], wo[:, kt, half * 512:(half + 1) * 512], kt == 0, kt == 7,
                             [mix.b, wo.b], [ps.b])
                    P.tt(xn[:, j, half * 512:(half + 1) * 512], ps[:], xt[:, j, half * 512:(half + 1) * 512], ALU.add,
                         [ps.b, xt.b], [xn.b])
            if not last:
                P.dma(dst_d[cs, :].rearrange("(j p) d -> p j d", p=128), xn[:], [xn.b], [Buf()])
            else:
                for j in range(4):
                    P.act(junk[:], xn[:, j, :], AF.Square, [xn.b], [junk.b, ss.b], accum=ss[:, j:j + 1])
                P.act(sd[:], ss[:], AF.Sqrt, [ss.b, g.epsc.b], [sd.b], bias=g.epsc[:], scale=1.0 / D)
                P.op("dve", lambda e: e.reciprocal(rstd[:], sd[:]), [sd.b], [rstd.b])
                for j in range(4):
                    P.stt(yo[:, j, :], xn[:, j, :], rstd[:, j:j + 1], gB[:], ALU.mult, ALU.mult, [xn.b, rstd.b, gB.b], [yo.b])
                P.dma(dst_d[cs, :].rearrange("(j p) d -> p j d", p=128), yo[:], [yo.b], [Buf()])
def phase1(g):
    nc, P = g.nc, g.P
    NCH = g.NCH
    l = 0
    banks, ident = g.banks, g.ident
    xsrc = g.x_in
    with ExitStack() as es:
        def sb(name, shape, dt=F32):
            return Tile(es.enter_context(nc.sbuf_tensor("p1_%d_" % g.uid + name, list(shape), dt)))

        wA = sb("wA", [128, 8, DIN], BF)
        wR = sb("wR", [128, 8, 576], BF)
        wuq = sb("wuq", [128, 3, 384], BF)
        wuqR = sb("wuqR", [128, 3, 384], BF)
        wk = sb("wk", [128, 256], BF)
        wv = sb("wv", [128, 256], BF)
        gcol = sb("gcol", [128, 8])
        qng = sb("qng", [128, 3])
        kvng = sb("kvng", [128, 1])
        negfb = sb("negfb", [4, 1])
        P.dma(gcol[:], g.normg_d, (), [gcol.b])
        P.dma(qng[:], g.qng_d, (), [qng.b])
        P.dma(kvng[:], g.kvng_d, (), [kvng.b])
        P.dma(negfb[:], g.foxfb_d, (), [negfb.b])
        P.ts(negfb[:], negfb[:], -1.0, None, ALU.mult, None, [negfb.b], [negfb.b])
        with ExitStack() as es2:
            wst = [Tile(es2.enter_context(nc.sbuf_tensor("p1_%d_wst%d" % (g.uid, i), [128, DIN], F32))) for i in range(6)]
            for kt in range(6):
                P.dma(wst[kt][:], g.w_in_d[kt * 128:(kt + 1) * 128, :], (), [wst[kt].b], eng=("sp" if kt % 2 == 0 else "act"))
            for kt in range(8):
                st = wst[kt % 6]
                if kt >= 6:
                    P.dma(st[:], g.w_in_d[kt * 128:(kt + 1) * 128, :], (), [st.b], eng=("sp" if kt % 2 == 0 else "act"))
                if kt % 2 == 0:
                    P.ts(wA[:, kt, :], st[:], gcol[:, kt:kt + 1], None, ALU.mult, None, [st.b, gcol.b], [wA.b])
                else:
                    P.act(wA[:, kt, :], st[:], AF.Copy, [st.b, gcol.b], [wA.b], scale=gcol[:, kt:kt + 1])
            for i in range(3):
                st = wst[i % 2]
                P.dma(st[:, 0:384], g.wuq_d[i * 128:(i + 1) * 128, :], (), [st.b])
                P.ts(wuq[:, i, :], st[:, 0:384], qng[:, i:i + 1], 0.816496580927726, ALU.mult, ALU.mult, [st.b, qng.b], [wuq.b])
            st = wst[1]
            P.dma(st[:, 0:512], g.wukv_d, (), [st.b])
            sv = st[:, 0:512].rearrange("p (h e) -> p h e", h=4)
            P.ts(wk[:].rearrange("p (h e) -> p h e", h=4), sv[:, :, 0:64], kvng[:, 0:1], None, ALU.mult, None,
                 [st.b, kvng.b], [wk.b])
            P.ts(wv[:].rearrange("p (h e) -> p h e", h=4), sv[:, :, 64:128], kvng[:, 0:1], None, ALU.mult, None,
                 [st.b, kvng.b], [wv.b])
            P.barrier()
        P.ms(wR[:], 0.0, [wR.b])
        P.ms(wuqR[:], 0.0, [wuqR.b])
        for kt in range(8):
            for (src_o, dst_o) in ((O_MQ, 0), (O_MK, 256)):
                s4 = wA[:, kt, src_o:src_o + 256].rearrange("p (h t j) -> p h t j", h=4, t=2)
                d4 = wR[:, kt, dst_o:dst_o + 256].rearrange("p (h t j) -> p h t j", h=4, t=2)
                P.ts(d4[:, :, 0, :], s4[:, :, 1, :], -1.0, None, ALU.mult, None, [wA.b], [wR.b], eng="dve")
                P.cp(d4[:, :, 1, :], s4[:, :, 0, :], [wA.b], [wR.b], eng="act")
            P.ts(wR[:, kt, 512:528], wA[:, kt, O_KR + 16:O_KR + 32], -1.0, None, ALU.mult, None, [wA.b], [wR.b], eng="dve")
            P.cp(wR[:, kt, 528:544], wA[:, kt, O_KR:O_KR + 16], [wA.b], [wR.b], eng="act")
        for i in range(3):
            s3 = wuq[:, i, :].rearrange("p (h e) -> p h e", h=4)
            d3 = wuqR[:, i, :].rearrange("p (h e) -> p h e", h=4)
            P.ts(d3[:, :, 64:80], s3[:, :, 80:96], -1.0, None, ALU.mult, None, [wuq.b], [wuqR.b], eng="dve")
            P.cp(d3[:, :, 80:96], s3[:, :, 64:80], [wuq.b], [wuqR.b], eng="act")

        xt = [sb("xt%d" % i, [128, 4, 1024]) for i in range(2)]
        rt = [sb("rt%d" % i, [128, 4, 512]) for i in range(1)]
        xs = sb("xs", [128, 4, 1024], BF)
        junk = sb("junk", [128, 1024], BF)
        hT = [sb("hT%d" % i, [128, 8, 512], BF) for i in range(2)]
        ss, sd, rstd = sb("ss", [128, 4]), sb("sd", [128, 4]), sb("rstd", [128, 4])
        stA = Rot([sb("stA%d" % i, [128, 512], BF) for i in range(6)])
        tf = Rot([sb("tf%d" % i, [128, 512]) for i in range(4)])
        vst = Rot([sb("vst%d" % i, [128, 4, 65], BF) for i in range(3)])
        for t in vst.tiles:
            P.ms(t[:], 1.0, [t.b])
        cqf = sb("cqf", [128, 3, 512])
        cqn = sb("cqn", [128, 3, 512], BF)
        sqb = Rot([sb("sqb%d" % i, [128, 512], BF) for i in range(2)])
        ckn = sb("ckn", [128, 512], BF)
        rq = sb("rq", [128, 512])
        rq2 = sb("rq2", [128, 512])
        ckf = sb("ckf", [128, 512])
        nlf = sb("nlf", [4, 512])
        ncum = [sb("ncum%d" % i, [4, 512]) for i in range(2)]
        s8, r1, r2 = sb("s8", [4, 512]), sb("r1", [4, 512]), sb("r2", [4, 512])
        e1 = r1
        kst = Rot([sb("kst%d" % i, [4, 3, 512], BF) for i in range(1)])
        qst = Rot([sb("qst%d" % i, [4, 3, 512], BF) for i in range(1)])
        kmf = [sb("kmf%d" % i, [128, 32]) for i in range(2)]
        kmz = [sb("kmz%d" % i, [128, 32], BF) for i in range(4)]
        for t in kmf:
            P.ms(t[:], 0.0, [t.b])
        for i in range(4):
            P.ms(kmz[i][:], 0.0, [kmz[i].b])
        cmask = sb("cmask", [128, 512])
        notown = sb("notown", [128, 512])
        ncin = sb("ncin", [4, 1])
        P.ms(ncin[:], 0.0, [ncin.b])
        mqs = [sb("mqs%d" % i, [128, 512], BF) for i in range(2)]
        gm = sb("gm", [128, 512])
        t8 = sb("t8", [128, 128])
        thr = sb("thr", [128, 16])
        mbf = sb("mbf", [128, 512])
        mb = sb("mb", [128, 512], BF)
        mbT = sb("mbT", [32, 2048], BF)
        pr = Rot(banks[0:6])
        ptb = [banks[6], banks[7]]

        def load(c):
            xc = xt[c % 2]
            rd = []
            P.dma(xc[:], xsrc[c * 512:(c + 1) * 512, :].rearrange("(j p) d -> p j d", p=128), rd, [xc.b])

        def pre_norm(c):
            xc = xt[c % 2]
            for j in range(4):
                P.act(junk[:], xc[:, j, :], AF.Square, [xc.b], [junk.b, ss.b], accum=ss[:, j:j + 1])
            P.act(sd[:], ss[:], AF.Sqrt, [ss.b, g.epsc.b], [sd.b], bias=g.epsc[:], scale=1.0 / D)
            P.op("dve", lambda e: e.reciprocal(rstd[:], sd[:]), [sd.b], [rstd.b])
            for j in range(4):
                P.ts(xs[:, j, :], xc[:, j, :], rstd[:, j:j + 1], None, ALU.mult, None, [xc.b, rstd.b], [xs.b])

        def pre_tr(c):
            h = hT[c % 2]
            for kt in range(8):
                pt = ptb[kt % 2]
                ptv = pt[:].bitcast(BF)
                for j in range(4):
                    P.tr(ptv[:, j * 128:(j + 1) * 128], xs[:, j, kt * 128:(kt + 1) * 128], ident[:],
                         [xs.b, ident.b], [pt.b])
                P.cp(h[:, kt, :], ptv[:, 0:512], [pt.b], [h.b], eng=("dve" if kt % 2 == 0 else "act"))

        load(0)
        pre_norm(0)
        pre_tr(0)
        for c in range(NCH):
            if c + 1 < NCH:
                load(c + 1)
            cs = slice(c * 512, (c + 1) * 512)
            xc, h, rc = xt[c % 2], hT[c % 2], rt[0]
            P.dma(cmask[:], g.cmask_d[c], (), [cmask.b])
            P.dma(notown[:], g.notown_d[c], (), [notown.b])
            P.dma(rc[:], g.rope_d[:, :, :, c * 512:(c + 1) * 512].rearrange("a b p t -> p (a b) t"), (), [rc.b])
            def proj(src, lo, M):
                ps = pr.next()
                for kt in range(8):
                    P.mm(ps[0:M, :], src[:, kt, lo:lo + M], h[:, kt, :], kt == 0, kt == 7, [src.b, h.b], [ps.b])
                return ps

            def split_store(st, dst, bufs, h0):
                P.dma(dst[h0, 0:64, cs], st[0:64, :], [st.b], [bufs[h0][c]])
                P.dma(dst[h0 + 1, 0:64, cs], st[64:128, :], [st.b], [bufs[h0 + 1][c]])

            def values(j):
                for (vi, off) in ((0, O_FV), (1, O_MV), (2, None)):
                    ps = pr.next()
                    if off is not None:
                        for kt in range(8):
                            P.mm(ps[:, 0:256], h[:, kt, j * 128:(j + 1) * 128], wA[:, kt, off:off + 256], kt == 0, kt == 7,
                                 [wA.b, h.b], [ps.b])
                    else:
                        P.mm(ps[:, 0:256], ckn[:, j * 128:(j + 1) * 128], wv[:], True, True, [ckn.b, wv.b], [ps.b])
                    vt = vst.next()
                    P.cp(vt[:, :, 0:64], ps[:, 0:256].rearrange("p (h e) -> p h e", h=4), [ps.b], [vt.b], eng="act")
                    P.dma(g.VS[4 * vi:4 * vi + 4, :, c * 4 + j, :].rearrange("h p e -> p h e"), vt[:],
                          [vt.b], [g.bVS[4 * vi + k][c] for k in range(4)])

            for i in range(3):
                ps = proj(wA, O_CQ + 128 * i, 128)
                P.cp(cqf[:, i, :], ps[:], [ps.b], [cqf.b], eng="dve")
            ssb = pr.next()
            for i in range(3):
                sq = sqb.next()
                P.act(sq[:], cqf[:, i, :], AF.Square, [cqf.b], [sq.b])
                P.mm(ssb[:], g.ones_bf[:, 0:128], sq[:], i == 0, i == 2, [g.ones_bf.b, sq.b], [ssb.b])
            P.act(rq[:], ssb[:], AF.Ln, [ssb.b, g.epsc.b], [rq.b], bias=g.epsc[:], scale=1.0 / 384)
            P.act(rq[:], rq[:], AF.Exp, [rq.b], [rq.b], scale=-0.5)
            ps = proj(wA, O_CKV, 128)
            P.cp(ckf[:], ps[:], [ps.b], [ckf.b], eng="dve")
            sq = sqb.next()
            P.act(sq[:], ckf[:], AF.Square, [ckf.b], [sq.b])
            ssb = pr.next()
            P.mm(ssb[:], g.ones_bf[:, 0:128], sq[:], True, True, [g.ones_bf.b, sq.b], [ssb.b])
            P.act(rq2[:], ssb[:], AF.Ln, [ssb.b, g.epsc.b], [rq2.b], bias=g.epsc[:], scale=1.0 / 128)
            P.act(rq2[:], rq2[:], AF.Exp, [rq2.b], [rq2.b], scale=-0.5)
            for pair in range(2):
                for (off, dst, bufs) in ((O_FQ, g.QA, g.bQA), (O_FK, g.KA, g.bKA)):
                    ps = proj(wA, off + 128 * pair, 128)
                    st = stA.next()
                    P.cp(st[:], ps[:], [ps.b], [st.b], eng="act")
                    split_store(st, dst, bufs, 2 * pair)
            for gi, off in enumerate((O_FG, O_FG + 128, O_SG, O_SG + 128, O_MG, O_MG + 128, O_LG, O_LG + 128)):
                ps = proj(wA, off, 128)
                st = stA.next()
                P.act(st[:], ps[:], AF.Silu, [ps.b], [st.b])
                P.dma(g.GS[gi * 128:(gi + 1) * 128, cs], st[:], [st.b], [g.bGS[gi][c]])
            for i in range(3):
                P.tt(cqn[:, i, :], cqf[:, i, :], rq[:], ALU.mult, [cqf.b, rq.b], [cqn.b])
            P.tt(ckn[:], ckf[:], rq2[:], ALU.mult, [ckf.b, rq2.b], [ckn.b])
            if c + 1 < NCH:
                pre_norm(c + 1)
            ps = proj(wA, O_FF, 4)
            P.act(e1[:], ps[0:4, :], AF.Exp, [ps.b, negfb.b], [e1.b], bias=negfb[:], scale=-1.0)
            P.act(nlf[:], e1[:], AF.Ln, [e1.b, g.onec.b], [nlf.b], bias=g.onec[0:4, :], scale=1.0)
            nc_cur, nc_prev = ncum[c % 2], ncum[(c + 1) % 2]
            if c == 0:
                P.op("dve", lambda e, o=nc_cur: e.tensor_tensor_scan(o[:], g.ones_f[0:4, :], nlf[:], ncin[:, 0:1], ALU.mult, ALU.add),
                     [g.ones_f.b, nlf.b, ncin.b], [nc_cur.b])
            else:
                P.op("dve", lambda e, o=nc_cur, p_=nc_prev: e.tensor_tensor_scan(o[:], g.ones_f[0:4, :], nlf[:], p_[:, 511:512], ALU.mult, ALU.add),
                     [g.ones_f.b, nlf.b, nc_prev.b], [nc_cur.b])
            ks, qs = kst.next(), qst.next()
            P.ts(s8[:], nc_cur[:], 8.0, None, ALU.mult, None, [nc_cur.b], [s8.b])
            P.cp(ks[:, 0, :], s8[:], [s8.b], [ks.b])
            P.tt(r1[:], s8[:], ks[:, 0, :], ALU.subtract, [s8.b, ks.b], [r1.b])
            P.cp(ks[:, 1, :], r1[:], [r1.b], [ks.b])
            P.tt(r2[:], r1[:], ks[:, 1, :], ALU.subtract, [r1.b, ks.b], [r2.b])
            P.cp(ks[:, 2, :], r2[:], [r2.b], [ks.b])
            P.ts(qs[:], ks[:], -1.0, None, ALU.mult, None, [ks.b], [qs.b])
            for hh in range(4):
                P.dma(g.QA[hh:hh + 1, 64:67, cs], qs[hh:hh + 1, :, :], [qs.b], [g.bQA[hh][c]])
                P.dma(g.KA[hh:hh + 1, 67:70, cs], ks[hh:hh + 1, :, :], [ks.b], [g.bKA[hh][c]])
            for i in range(2):
                ps = proj(wA, O_SU + 128 * i, 128)
                st = stA.next()
                P.cp(st[:], ps[:], [ps.b], [st.b], eng="act")
                P.dma(g.UT[i * 128:(i + 1) * 128, cs], st[:], [st.b], [g.bUT[c]])
            for (off, roff, isq) in ((O_MK, 256, False), (O_MQ, 0, True)):
                for pair in range(2):
                    ps1 = proj(wA, off + 128 * pair, 128)
                    ps2 = proj(wR, roff + 128 * pair, 128)
                    t1, t2 = tf.next(), tf.next()
                    P.tt(t1[:], ps1[:], rc[:, 0, :], ALU.mult, [ps1.b, rc.b], [t1.b])
                    P.tt(t2[:], ps2[:], rc[:, 2, :], ALU.mult, [ps2.b, rc.b], [t2.b])
                    if isq:
                        st = mqs[pair]
                        P.tt(st[:], t1[:], t2[:], ALU.add, [t1.b, t2.b], [st.b])
                        split_store(st, g.QA, g.bQA, 4 + 2 * pair)
                    else:
                        st = stA.next()
                        P.tt(st[:], t1[:], t2[:], ALU.add, [t1.b, t2.b], [st.b])
                        split_store(st, g.KA, g.bKA, 4 + 2 * pair)
                        P.op("dve", lambda e, o=kmf[pair], s_=st: e.tensor_reduce(
                            o[:, 0:2], s_[:].rearrange("p (b t) -> p b t", b=2), AX.X, ALU.add),
                            [st.b], [kmf[pair].b])
                        for hx in range(2):
                            kz = kmz[2 * pair + hx]
                            P.cp(kz[64 * hx:64 * hx + 64, 2 * c:2 * c + 2], kmf[pair][64 * hx:64 * hx + 64, 0:2], [kmf[pair].b], [kz.b])
            values(0)
            gps = pr.next()
            for hh in range(4):
                pair, base = hh // 2, 64 * (hh % 2)
                for j in range(4):
                    i16 = hh * 4 + j
                    P.mm(gps[:, i16 * 32:(i16 + 1) * 32], mqs[pair][:, j * 128:(j + 1) * 128],
                         kmz[hh][:, 0:32], True, True, [mqs[pair].b, kmz[hh].b], [gps.b])
            P.tt(gm[:], gps[:], cmask[:], ALU.add, [gps.b, cmask.b], [gm.b])
            for i16 in range(16):
                P.op("dve", lambda e, i=i16: e.max(t8[:, i * 8:(i + 1) * 8], gm[:, i * 32:(i + 1) * 32]), [gm.b], [t8.b])
            P.ts(thr[:], t8[:].rearrange("p (i e) -> p i e", e=8)[:, :, 2], -1e29, None, ALU.max, None, [t8.b], [thr.b])
            for i16 in range(16):
                P.ts(mbf[:, i16 * 32:(i16 + 1) * 32], gm[:, i16 * 32:(i16 + 1) * 32], thr[:, i16:i16 + 1], -NEGB,
                     ALU.is_ge, ALU.mult, [gm.b, thr.b], [mbf.b])
            P.ts(mb[:], mbf[:], NEGB, None, ALU.add, None, [mbf.b], [mb.b])
            P.tt(mb[:], mb[:], notown[:], ALU.mult, [mb.b, notown.b], [mb.b])
            values(1)
            for hh in range(4):
                psq, psr = pr.next(), pr.next()
                for i in range(3):
                    P.mm(psq[0:96, :], wuq[:, i, 96 * hh:96 * hh + 96], cqn[:, i, :], i == 0, i == 2, [wuq.b, cqn.b], [psq.b])
                for i in range(3):
                    P.mm(psr[0:96, :], wuqR[:, i, 96 * hh:96 * hh + 96], cqn[:, i, :], i == 0, i == 2, [wuqR.b, cqn.b], [psr.b])
                st = stA.next()
                t1, t2 = tf.next(), tf.next()
                P.cp(st[0:64, :], psq[0:64, :], [psq.b], [st.b], eng="act")
                P.tt(t1[64:96, :], psq[64:96, :], rc[64:96, 1, :], ALU.mult, [psq.b, rc.b], [t1.b])
                P.tt(t2[64:96, :], psr[64:96, :], rc[64:96, 3, :], ALU.mult, [psr.b, rc.b], [t2.b])
                P.tt(st[64:96, :], t1[64:96, :], t2[64:96, :], ALU.add, [t1.b, t2.b], [st.b])
                P.dma(g.QA[8 + hh, 0:96, cs], st[0:96, :], [st.b], [g.bQA[8 + hh][c]])
            for pair in range(2):
                ps = pr.next()
                P.mm(ps[:], wk[:, 128 * pair:128 * pair + 128], ckn[:], True, True, [wk.b, ckn.b], [ps.b])
                st = stA.next()
                P.cp(st[:], ps[:], [ps.b], [st.b], eng="act")
                split_store(st, g.KA, g.bKA, 8 + 2 * pair)
            values(2)
            ps1 = proj(wA, O_KR, 32)
            ps2 = proj(wR, 512, 32)
            t1, t2 = tf.next(), tf.next()
            st = stA.next()
            P.tt(t1[0:32, :], ps1[0:32, :], rc[0:32, 1, :], ALU.mult, [ps1.b, rc.b], [t1.b])
            P.tt(t2[0:32, :], ps2[0:32, :], rc[0:32, 3, :], ALU.mult, [ps2.b, rc.b], [t2.b])
            P.tt(st[0:32, :], t1[0:32, :], t2[0:32, :], ALU.add, [t1.b, t2.b], [st.b])
            for hh in range(4):
                P.dma(g.KA[8 + hh, 64:96, cs], st[0:32, :], [st.b], [g.bKA[8 + hh][c]])
            for half2 in range(2):
                pt = ptb[half2]
                ptv = pt[:].bitcast(BF)
                for k8 in range(8):
                    i16 = half2 * 8 + k8
                    P.tr(ptv[0:32, k8 * 128:(k8 + 1) * 128], mb[:, i16 * 32:(i16 + 1) * 32], ident[:], [mb.b, ident.b], [pt.b])
                P.cp(mbT[:, half2 * 1024:(half2 + 1) * 1024], ptv[0:32, :], [pt.b], [mbT.b], eng="act")
            for hh in range(4):
                P.dma(g.QA[4 + hh, 64:96, cs], mbT[:, hh * 512:(hh + 1) * 512], [mbT.b], [g.bQA[4 + hh][c]])
            if c + 1 < NCH:
                pre_tr(c + 1)
            values(3)


def phase3f(g, a_d, a16_d, b_d, c_d, d_d, gw_d, gb_d):
    nc, P, banks, NCH = g.nc, g.P, g.banks, g.NCH
    PI = math.pi
    esp = ExitStack()
    sb = lambda name, shape, dt=F32: Tile(esp.enter_context(nc.sbuf_tensor("p3_%d_" % g.uid + name, list(shape), dt)))
    g_sb_saved = sb

    def ld(name, shape, src, dt=F32):
        t = sb(name, shape, dt)
        P.dma(t[:], src, (), [t.b])
        return t

    A = ld("A", [128, 3, 16], a_d.rearrange("a p g -> p a g"))
    dcol = ld("dcol", [128, 2], d_d)
    gbc = ld("gbc", [128, 2], gb_d)
    gwf = ld("gwf", [128, 2, 256], gw_d.rearrange("(i p) o -> p i o", p=128))
    idf = ld("idf", [128, 128], g.idf_d)
    jsw = ld("jsw", [128, 128], g.jsw_d)
    gwb = sb("gwb", [128, 2, 256], BF)
    P.cp(gwb[:], gwf[:], [gwf.b], [gwb.b])

    def lamparts(pre, src, np_, nf):
        mk = lambda n: sb(pre + n, [np_, nf])
        dt, th, r, k, x1, sn, cs = mk("dt"), mk("th"), mk("r"), mk("k"), mk("x1"), mk("sn"), mk("cs")
        are, aim, ldt = src[:, 0, :], src[:, 1, :], src[:, 2, :]
        P.act(dt[:], ldt, AF.Exp, [src.b], [dt.b])
        P.tt(th[:], aim, dt[:], ALU.mult, [src.b, dt.b], [th.b])
        P.tt(r[:], are, dt[:], ALU.mult, [src.b, dt.b], [r.b])
        P.act(r[:], r[:], AF.Exp, [r.b], [r.b])
        for (dst, shift) in ((sn, 0.0), (cs, PI / 2)):
            P.ts(x1[:], th[:], shift, None, ALU.add, None, [th.b], [x1.b])
            P.ts(k[:], x1[:], 1.0 / (2 * PI), 12582912.0, ALU.mult, ALU.add, [x1.b], [k.b])
            P.ts(k[:], k[:], -12582912.0, None, ALU.add, None, [k.b], [k.b])
            P.stt(x1[:], k[:], -6.28125, x1[:], ALU.mult, ALU.add, [k.b, x1.b], [x1.b])
            P.stt(x1[:], k[:], -0.0019353071795864769, x1[:], ALU.mult, ALU.add, [k.b, x1.b], [x1.b])
            P.ts(x1[:], x1[:], 3.1415925, -3.1415925, ALU.min, ALU.max, [x1.b], [x1.b])
            P.act(dst[:], x1[:], AF.Sin, [x1.b], [dst.b])
        return r, cs, sn

    Ct, St = sb("Ct", [128, 16, 512], BF), sb("St", [128, 16, 512], BF)
    Mrot = sb("Mrot", [128, 16, 128])
    LB, LS = sb("LB", [16, 16, 128], BF), sb("LS", [16, 16, 128], BF)
    c1p, c2p = sb("c1p", [128, 16, 128], BF), sb("c2p", [128, 16, 128], BF)
    rkeep = sb("rkeep", [128, 16])
    Rdec = sb("Rdec", [128, 16, 512])
    xin = sb("xin", [128, 16])
    couts = [sb("cout%d" % i, [128, 16]) for i in range(2)]
    gsb = g_sb_saved
    es1 = ExitStack()
    sb = lambda name, shape, dt=F32: Tile(es1.enter_context(nc.sbuf_tensor("s1_%d_" % g.uid + name, list(shape), dt)))
    r, cm, sm = lamparts("a_", A, 128, 16)
    P.cp(rkeep[:], r[:], [r.b], [rkeep.b])
    for gi in range(16):
        P.act(Rdec[:, gi, :], g.ones_f[:], AF.Copy, [g.ones_f.b, r.b], [Rdec.b], scale=r[:, gi:gi + 1])
    tA, tB = sb("tA", [128, 16, 256]), sb("tB", [128, 16, 256])
    Ct32, St32 = sb("Ct32", [128, 16, 512]), sb("St32", [128, 16, 512])
    P.ms(Ct32[:, :, 0:1], 1.0, [Ct32.b])
    P.ms(St32[:, :, 0:1], 0.0, [St32.b])
    c3 = lambda t: t[:].rearrange("p (g o) -> p g o", o=1)
    P.cp(Ct32[:, :, 1:2], c3(cm), [cm.b], [Ct32.b])
    P.cp(St32[:, :, 1:2], c3(sm), [sm.b], [St32.b])
    m = 1
    q1, q2 = sb("q1", [128, 16]), sb("q2", [128, 16])
    cur = (cm, sm)
    nxt = (sb("cm2", [128, 16]), sb("sm2", [128, 16]))
    while m <= 256:
        c_, s_ = cur
        if m > 1 or True:
            pass
        if m >= 2 or m == 1:
            pass
        if m > 1:
            pass
        cb = c3(c_).to_broadcast([128, 16, m])
        sbb = c3(s_).to_broadcast([128, 16, m])
        if m >= 2:
            P.tt(tA[:, :, 0:m], St32[:, :, 0:m], sbb, ALU.mult, [St32.b, s_.b], [tA.b])
            P.tt(tB[:, :, 0:m], Ct32[:, :, 0:m], cb, ALU.mult, [Ct32.b, c_.b], [tB.b])
            P.tt(Ct32[:, :, m:2 * m], tB[:, :, 0:m], tA[:, :, 0:m], ALU.subtract, [tA.b, tB.b], [Ct32.b])
            P.tt(tA[:, :, 0:m], St32[:, :, 0:m], cb, ALU.mult, [St32.b, c_.b], [tA.b])
            P.tt(tB[:, :, 0:m], Ct32[:, :, 0:m], sbb, ALU.mult, [Ct32.b, s_.b], [tB.b])
            P.tt(St32[:, :, m:2 * m], tB[:, :, 0:m], tA[:, :, 0:m], ALU.add, [tA.b, tB.b], [St32.b])
        c2, s2 = nxt
        P.tt(q1[:], c_[:], c_[:], ALU.mult, [c_.b], [q1.b])
        P.tt(q2[:], s_[:], s_[:], ALU.mult, [s_.b], [q2.b])
        P.tt(c2[:], q1[:], q2[:], ALU.subtract, [q1.b, q2.b], [c2.b])
        P.tt(q1[:], c_[:], s_[:], ALU.mult, [c_.b, s_.b], [q1.b])
        P.ts(s2[:], q1[:], 2.0, None, ALU.mult, None, [q1.b], [s2.b])
        cur, nxt = (c2, s2), (c_, s_)
        m *= 2
    c512, s512 = cur
    P.cp(Ct[:], Ct32[:], [Ct32.b], [Ct.b])
    P.cp(St[:], St32[:], [St32.b], [St.b], eng="act")
    ssg = sb("ssg", [128, 16])
    P.cp(ssg[0:64, :], s512[0:64, :], [s512.b], [ssg.b])
    P.ts(ssg[64:128, :], s512[64:128, :], -1.0, None, ALU.mult, None, [s512.b], [ssg.b])
    for gi in range(16):
        P.ts(Mrot[:, gi, :], idf[:], c512[:, gi:gi + 1], None, ALU.mult, None, [idf.b, c512.b], [Mrot.b])
        P.stt(Mrot[:, gi, :], jsw[:], ssg[:, gi:gi + 1], Mrot[:, gi, :], ALU.mult, ALU.add, [jsw.b, ssg.b, Mrot.b], [Mrot.b])
    P.barrier()
    es1.close()
    es2 = ExitStack()
    sb = lambda name, shape, dt=F32: Tile(es2.enter_context(nc.sbuf_tensor("s2_%d_" % g.uid + name, list(shape), dt)))
    A16 = ld("A16", [16, 3, 1024], a16_d.rearrange("a p g -> p a g"))
    Bw = ld("Bw", [16, 2, 1024], b_d.rearrange("a p g -> p a g"))
    Cw = ld("Cw", [128, 2, 256], c_d.rearrange("a p g -> p a g"))
    r16, c16, s16 = lamparts("b_", A16, 16, 1024)
    mk16 = lambda n: sb("b_" + n, [16, 1024])
    lre, lim, den, cr, ci, w1, w2 = mk16("lre"), mk16("lim"), mk16("den"), mk16("cr"), mk16("ci"), mk16("w1"), mk16("w2")
    are, aim = A16[:, 0, :], A16[:, 1, :]
    P.tt(lre[:], r16[:], c16[:], ALU.mult, [r16.b, c16.b], [lre.b])
    P.tt(lim[:], r16[:], s16[:], ALU.mult, [r16.b, s16.b], [lim.b])
    P.ts(lre[:], lre[:], -1.0, None, ALU.add, None, [lre.b], [lre.b])
    P.tt(w1[:], are, are, ALU.mult, [A16.b], [w1.b])
    P.tt(w2[:], aim, aim, ALU.mult, [A16.b], [w2.b])
    P.tt(den[:], w1[:], w2[:], ALU.add, [w1.b, w2.b], [den.b])
    P.op("dve", lambda e: e.reciprocal(den[:], den[:]), [den.b], [den.b])
    P.tt(w1[:], lre[:], are, ALU.mult, [lre.b, A16.b], [w1.b])
    P.tt(w2[:], lim[:], aim, ALU.mult, [lim.b, A16.b], [w2.b])
    P.tt(cr[:], w1[:], w2[:], ALU.add, [w1.b, w2.b], [cr.b])
    P.tt(cr[:], cr[:], den[:], ALU.mult, [cr.b, den.b], [cr.b])
    P.tt(w1[:], lim[:], are, ALU.mult, [lim.b, A16.b], [w1.b])
    P.tt(w2[:], lre[:], aim, ALU.mult, [lre.b, A16.b], [w2.b])
    P.tt(ci[:], w1[:], w2[:], ALU.subtract, [w1.b, w2.b], [ci.b])
    P.tt(ci[:], ci[:], den[:], ALU.mult, [ci.b, den.b], [ci.b])
    bre, bim = Bw[:, 0, :], Bw[:, 1, :]
    Bre, Bim = mk16("Bre"), mk16("Bim")
    P.tt(w1[:], cr[:], bre, ALU.mult, [cr.b, Bw.b], [w1.b])
    P.tt(w2[:], ci[:], bim, ALU.mult, [ci.b, Bw.b], [w2.b])
    P.tt(Bre[:], w1[:], w2[:], ALU.subtract, [w1.b, w2.b], [Bre.b])
    P.tt(w1[:], cr[:], bim, ALU.mult, [cr.b, Bw.b], [w1.b])
    P.tt(w2[:], ci[:], bre, ALU.mult, [ci.b, Bw.b], [w2.b])
    P.tt(Bim[:], w1[:], w2[:], ALU.add, [w1.b, w2.b], [Bim.b])
    v3 = lambda t: t[:].rearrange("c (g p) -> c g p", g=16)
    P.cp(LB[:, :, 0:64], v3(Bre), [Bre.b], [LB.b])
    P.cp(LB[:, :, 64:128], v3(Bim), [Bim.b], [LB.b])
    P.cp(LS[:, :, 0:64], v3(Bim), [Bim.b], [LS.b])
    P.ts(LS[:, :, 64:128], v3(Bre), -1.0, None, ALU.mult, None, [Bre.b], [LS.b])
    c1f, c2f = sb("c1f", [128, 256]), sb("c2f", [128, 256])
    P.cp(c1f[0:64, :], Cw[0:64, 0, :], [Cw.b], [c1f.b])
    P.ts(c1f[64:128, :], Cw[64:128, 0, :], -1.0, None, ALU.mult, None, [Cw.b], [c1f.b])
    P.ts(c2f[:], Cw[:, 1, :], -1.0, None, ALU.mult, None, [Cw.b], [c2f.b])
    P.ms(c1p[:], 0.0, [c1p.b])
    P.ms(c2p[:], 0.0, [c2p.b])
    for gi in range(16):
        o = 16 * (gi % 8)
        P.cp(c1p[:, gi, o:o + 16], c1f[:, 16 * gi:16 * gi + 16], [c1f.b], [c1p.b])
        P.cp(c2p[:, gi, o:o + 16], c2f[:, 16 * gi:16 * gi + 16], [c2f.b], [c2p.b])
    P.barrier()
    es2.close()
    sb = gsb
    pbr = Rot(banks[0:4])
    yps = [banks[4], banks[5]]
    m1 = Rot([sb("m1_%d" % i, [128, 512], BF) for i in range(4)])
    m2 = Rot([sb("m2_%d" % i, [128, 512], BF) for i in range(4)])
    bt = Rot([sb("bt_%d" % i, [128, 512], BF) for i in range(4)])
    xs = Rot([sb("xs_%d" % i, [128, 512], BF) for i in range(4)])
    d1 = Rot([sb("d1_%d" % i, [128, 512], BF) for i in range(4)])
    d2 = Rot([sb("d2_%d" % i, [128, 512], BF) for i in range(4)])
    rdr = Rot([None])
    pbs = Rot([sb("pbs_%d" % i, [128, 512], BF) for i in range(4)])
    pws = Rot([sb("pws_%d" % i, [128, 512], BF) for i in range(4)])
    ep = {}
    for hf in range(2):
        ep["yv%d" % hf] = sb("yv%d" % hf, [128, 512])
        w_ = sb("wk%d" % hf, [128, 512])
        for n in ("sq", "in", "th", "sg", "o1"):
            ep["%s%d" % (n, hf)] = w_
        ep["o2%d" % hf] = sb("o2%d" % hf, [128, 512], BF)
    gf = [sb("gf%d" % i, [128, 512]) for i in range(2)]
    gb16 = [sb("gb%d" % i, [128, 512], BF) for i in range(2)]
    uTs = [sb("uT%d" % i, [16, 16, 512], BF) for i in range(2)]
    ufs = [sb("uf%d" % i, [128, 2, 512], BF) for i in range(2)]
    gss = [sb("gsT%d" % i, [128, 2, 512], BF) for i in range(2)]
    pc = banks[6]

    def load(c):
        cs_ = slice(c * 512, (c + 1) * 512)
        P.dma(uTs[c % 2][:], g.UT[:, cs_].rearrange("(g c) t -> c g t", c=16), (), [uTs[c % 2].b])
        P.dma(ufs[c % 2][:], g.UT[:, cs_].rearrange("(j p) t -> p j t", p=128), (), [ufs[c % 2].b])
        P.dma(gss[c % 2][:], g.GS[256:512, cs_].rearrange("(j p) t -> p j t", p=128), (), [gss[c % 2].b])

    def epi_a(c):
        uf = ufs[c % 2]
        for hf in range(2):
            yv = ep["yv%d" % hf]
            P.stt(yv[:], uf[:, hf, :], dcol[:, hf:hf + 1], yps[hf][:], ALU.mult, ALU.add, [uf.b, dcol.b, yps[hf].b], [yv.b])

    def epi_b(c):
        cs = slice(c * 512, (c + 1) * 512)
        gsT = gss[c % 2]
        for hf in range(2):
            yv, sq, inn, th = ep["yv%d" % hf], ep["sq%d" % hf], ep["in%d" % hf], ep["th%d" % hf]
            P.tt(sq[:], yv[:], yv[:], ALU.mult, [yv.b], [sq.b])
            P.ts(sq[:], sq[:], 0.044715, 1.0, ALU.mult, ALU.add, [sq.b], [sq.b])
            P.tt(inn[:], sq[:], yv[:], ALU.mult, [sq.b, yv.b], [inn.b])
            P.act(th[:], inn[:], AF.Tanh, [inn.b], [th.b], scale=0.7978845608028654)
            P.stt(gf[hf][:], th[:], 1.0, yv[:], ALU.add, ALU.mult, [th.b, yv.b], [gf[hf].b])
            P.ts(gf[hf][:], gf[hf][:], 0.5, None, ALU.mult, None, [gf[hf].b], [gf[hf].b])
            P.cp(gb16[hf][:], gf[hf][:], [gf[hf].b], [gb16[hf].b])
        for oh in range(2):
            ps = banks[7]
            for ih in range(2):
                P.mm(ps[:], gwb[:, ih, oh * 128:(oh + 1) * 128], gb16[ih][:], ih == 0, ih == 1, [gwb.b, gb16[ih].b], [ps.b])
            sg, o1, o2 = ep["sg%d" % oh], ep["o1%d" % oh], ep["o2%d" % oh]
            P.act(sg[:], ps[:], AF.Sigmoid, [ps.b, gbc.b], [sg.b], bias=gbc[:, oh:oh + 1], scale=1.0)
            P.tt(o1[:], gf[oh][:], sg[:], ALU.mult, [gf[oh].b, sg.b], [o1.b])
            P.tt(o2[:], o1[:], gsT[:, oh, :], ALU.mult, [o1.b, gsT.b], [o2.b])
            P.dma(g.MIX[256 + oh * 128:256 + (oh + 1) * 128, cs], o2[:], [o2.b], [Buf()])


    load(0)
    if NCH > 1:
        load(1)
    xins = [xin, sb("xin1", [128, 16])]
    P.ms(xins[0][:], 0.0, [xins[0].b])

    def in_mm(c, gp):
        uT = uTs[c % 2]
        T = {}
        for gi in (gp, gp + 1):
            pb, psw = pbr.next(), pbr.next()
            P.mm(pb[:], LB[:, gi, :], uT[:, gi, :], True, True, [LB.b, uT.b], [pb.b])
            P.mm(psw[:], LS[:, gi, :], uT[:, gi, :], True, True, [LS.b, uT.b], [psw.b])
            pb_s, pw_s = pbs.next(), pws.next()
            P.cp(pb_s[:], pb[:], [pb.b], [pb_s.b], eng="act")
            P.cp(pw_s[:], psw[:], [psw.b], [pw_s.b], eng="act")
            T[gi] = (pb_s, pw_s, m1.next(), m2.next(), bt.next(), xs.next(), d1.next(), d2.next(), rdr.next())
        return T

    pairs = [(c, gp) for c in range(NCH) for gp in range(0, 16, 2)]
    Tnext = in_mm(0, 0)
    for pi, (c, gp) in enumerate(pairs):
        cs = slice(c * 512, (c + 1) * 512)
        cout = couts[c % 2]
        xcur, xnxt = xins[c % 2], xins[(c + 1) % 2]
        gl = (gp, gp + 1)
        if gp == 4 and c > 0:
            epi_b(c - 1)
            if c + 1 < NCH:
                load(c + 1)
        T = Tnext
        for gi in gl:
            pb, psw, a1, a2, b_, x_, e1, e2, rd_ = T[gi]
            P.tt(a1[:], pb[:], Ct[:, gi, :], ALU.mult, [pb.b, Ct.b], [a1.b])
        for gi in gl:
            pb, psw, a1, a2, b_, x_, e1, e2, rd_ = T[gi]
            P.tt(a2[:], psw[:], St[:, gi, :], ALU.mult, [psw.b, St.b], [a2.b])
        if pi + 1 < len(pairs):
            Tnext = in_mm(*pairs[pi + 1])
        for gi in gl:
            pb, psw, a1, a2, b_, x_, e1, e2, rd_ = T[gi]
            P.tt(b_[:], a1[:], a2[:], ALU.add, [a1.b, a2.b], [b_.b])
        for gi in gl:
            pb, psw, a1, a2, b_, x_, e1, e2, rd_ = T[gi]
            P.op("dve", lambda e, x_=x_, b_=b_, gi=gi, xc=xcur: e.tensor_tensor_scan(x_[:], Rdec[:, gi, :], b_[:], xc[:, gi:gi + 1], ALU.mult, ALU.add),
                 [Rdec.b, b_.b, xcur.b], [x_.b])
        for gi in gl:
            x_ = T[gi][5]
            P.cp(cout[:, gi:gi + 1], x_[:, 511:512], [x_.b], [cout.b], eng="act")
        if c + 1 < NCH:
            for gi in gl:
                P.mm(pc[:, gi:gi + 1], Mrot[:, gi, :], cout[:, gi:gi + 1], True, True, [Mrot.b, cout.b], [pc.b])
        for gi in gl:
            pb, psw, a1, a2, b_, x_, e1, e2, rd_ = T[gi]
            P.tt(e1[:], x_[:], Ct[:, gi, :], ALU.mult, [x_.b, Ct.b], [e1.b])
        for gi in gl:
            pb, psw, a1, a2, b_, x_, e1, e2, rd_ = T[gi]
            P.tt(e2[:], x_[:], St[:, gi, :], ALU.mult, [x_.b, St.b], [e2.b])
        if c + 1 < NCH:
            P.cp(xnxt[:, gp:gp + 2], pc[:, gp:gp + 2], [pc.b], [xnxt.b])
        for gi in gl:
            pb, psw, a1, a2, b_, x_, e1, e2, rd_ = T[gi]
            yp = yps[gi // 8]
            P.mm(yp[:], c1p[:, gi, :], e1[:], gi % 8 == 0, False, [c1p.b, e1.b], [yp.b])
            P.mm(yp[:], c2p[:, gi, :], e2[:], False, gi % 8 == 7, [c2p.b, e2.b], [yp.b])
        if gp == 14:
            epi_a(c)
    epi_b(NCH - 1)

    esp.close()
def _bf(a):
    return np.asarray(a, dtype=np.float32).astype(ml_dtypes.bfloat16)


def host_consts(S):
    pos = np.arange(S, dtype=np.float32)
    rope = np.zeros((2, 2, 128, S), np.float32)
    for bi, half in enumerate((32, 16)):
        inv = np.power(np.float32(10000.0), -np.arange(half, dtype=np.float32) / np.float32(half)).astype(np.float32)
        ang = (pos[None, :] * inv[:, None]).astype(np.float32)
        rows = np.arange(128) % half
        rope[0, bi] = np.cos(ang)[rows]
        rope[1, bi] = np.sin(ang)[rows]
    ident = _bf(np.eye(128))
    r = np.arange(128)
    negtri = _bf(np.where(r[:, None] <= r[None, :], 0.0, NEGB))
    blkind = _bf((np.arange(S)[None, :] // 256) == np.arange(32)[:, None])
    return {"rope": rope, "ident": ident, "negtri": negtri, "blkind": blkind}


def host_layout(inp, S, depth):
    f = lambda a: np.ascontiguousarray(np.asarray(a, dtype=np.float32))
    L = depth
    m = {}
    m["w_in"] = f(np.asarray(inp["w_in"])[:L])
    m["w_out"] = f(np.asarray(inp["w_out"])[:L])
    m["norm_g"] = f(np.asarray(inp["norm_g"])[:L].reshape(L, 8, 128).transpose(0, 2, 1))
    m["final_g"] = f(np.broadcast_to(np.asarray(inp["final_g"]).reshape(1, D), (128, D)))
    m["fox_fb"] = f(np.asarray(inp["fox_fb"])[:L].reshape(L, 4, 1))
    m["mla_w_uq"] = f(np.asarray(inp["mla_w_uq"])[:L])
    m["mla_q_norm"] = f(np.asarray(inp["mla_q_norm"])[:L].reshape(L, 3, 128).transpose(0, 2, 1))
    m["mla_w_ukv"] = f(np.asarray(inp["mla_w_ukv"])[:L])
    m["mla_kv_norm"] = f(np.asarray(inp["mla_kv_norm"])[:L].reshape(L, 128, 1))
    are, aim, ldt = (np.asarray(inp[k])[:L] for k in ("s5_a_re", "s5_a_im", "s5_log_dt"))
    ldtb = np.broadcast_to(ldt[:, :, None], (L, 16, 64))
    tT = lambda a: np.concatenate([a.transpose(0, 2, 1)] * 2, axis=1)
    m["s5_a"] = f(np.stack([tT(are), tT(aim), tT(ldtb)], axis=1))
    rep = lambda a: np.broadcast_to(a.reshape(L, 1, 1024), (L, 16, 1024))
    m["s5_a16"] = f(np.stack([rep(are), rep(aim), rep(ldtb)], axis=1))
    bre, bim = (np.asarray(inp[k])[:L] for k in ("s5_b_re", "s5_b_im"))
    tb = lambda a: a.transpose(0, 3, 1, 2).reshape(L, 16, 1024)
    m["s5_b"] = f(np.stack([tb(bre), tb(bim)], axis=1))
    cre, cim = (np.asarray(inp[k])[:L] for k in ("s5_c_re", "s5_c_im"))
    tc_ = lambda a: a.transpose(0, 3, 1, 2).reshape(L, 64, 256)
    m["s5_c"] = f(np.stack([np.concatenate([tc_(cre), tc_(cim)], axis=1),
                            np.concatenate([tc_(cim), tc_(cre)], axis=1)], axis=1))
    m["s5_d"] = f(np.asarray(inp["s5_d"])[:L].reshape(L, 2, 128).transpose(0, 2, 1))
    m["s5_glu_w"] = f(np.asarray(inp["s5_glu_w"])[:L])
    m["s5_glu_b"] = f(np.asarray(inp["s5_glu_b"])[:L].reshape(L, 2, 128).transpose(0, 2, 1))
    return m


def fused_consts(S):
    NCH = S // 512
    cst = host_consts(S)
    r = np.arange(128)
    negm = np.stack([np.where((r[:, None] + 128 * j) <= np.arange(512)[None, :], 0.0, NEGB) for j in range(4)], axis=1)
    cst["negm"] = _bf(negm)
    cst.pop("negtri", None)
    cst["identf"] = np.eye(128, dtype=np.float32)
    cst["swapj"] = np.roll(np.eye(128, dtype=np.float32), 64, axis=1).copy()
    cm = np.full((NCH, 128, 4, 4, 32), -1e30, np.float32)
    no = np.ones((NCH, 128, 4, 4, 32), np.float32)
    for c in range(NCH):
        cm[c, :, :, 0:2, 0:2 * c] = 0.0
        cm[c, :, :, 2:4, 0:2 * c + 1] = 0.0
        no[c, :, :, 0:2, 2 * c] = 0.0
        no[c, :, :, 2:4, 2 * c + 1] = 0.0
    cst["cmask"] = cm.reshape(NCH, 128, 512)
    cst["notown"] = no.reshape(NCH, 128, 512)
    return cst


def fused_maps(inputs, S, depth):
    m = host_layout(inputs, S, depth)
    m.update(fused_consts(S))
    return m


def kernel(**inputs):
    x = np.ascontiguousarray(np.asarray(inputs["x"], dtype=np.float32))
    B, S, _ = x.shape
    nc = build_fused(S, DEPTH)
    shared = fused_maps(inputs, S, DEPTH)
    in_maps = []
    for b in range(B):
        mp = dict(shared)
        mp["x"] = np.ascontiguousarray(x[b])
        in_maps.append(mp)
    res = run_bass_kernel_spmd(nc, in_maps, core_ids=list(range(B)))
    return np.stack([np.asarray(r["out"], dtype=np.float32) for r in res.results], axis=0)
```

```python
import math
from contextlib import ExitStack
import numpy as np
import ml_dtypes
import concourse.bass as bass
import concourse.mybir as mybir
from concourse.bass_utils import run_bass_kernel_spmd

F32 = mybir.dt.float32
BF = mybir.dt.bfloat16
AF = mybir.ActivationFunctionType
ALU = mybir.AluOpType
AX = mybir.AxisListType

D = 1024
DIN = 3364
DEPTH = 2
EPS = 1e-6
O_FQ, O_FK, O_FV, O_FG, O_FF = 0, 256, 512, 768, 1024
O_SU, O_SG = 1028, 1284
O_MQ, O_MK, O_MV, O_MG = 1540, 1796, 2052, 2308
O_CQ, O_CKV, O_KR, O_LG = 2564, 2948, 3076, 3108
NEGB = -30000.0
KDIM = [70] * 4 + [96] * 4 + [96] * 4
SEM_LIMIT = 20000
DMA_RING = 12
import os
CUT = int(os.environ.get('P1CUT', '99'))
SUB = int(os.environ.get('P1SUB', '99'))


class Buf:
    __slots__ = ("w", "r", "rd")

    def __init__(self):
        self.w = None
        self.r = {}
        self.rd = []


class Op:
    __slots__ = ("eng", "fn", "deps", "marked", "sem", "val", "dma")


class Prog:
    ENG = ("pe", "act", "dve", "pool", "sp")

    def __init__(self, nc):
        self.nc = nc
        self.streams = {e: [] for e in self.ENG}
        self.dmas = []
        self.nsem = 0

    def op(self, eng, fn, r=(), w=(), dma=False):
        o = Op()
        o.eng, o.fn, o.dma, o.marked, o.sem, o.val = eng, fn, dma, dma, None, 0
        deps = set()
        for b in r:
            if b.w is not None:
                deps.add(b.w)
        for b in w:
            if b.w is not None:
                deps.add(b.w)
            deps.update(b.r.values())
            deps.update(b.rd)
        if eng == "pe" and not dma:
            deps = {d for d in deps if not (d.eng == "pe" and not d.dma)}
        o.deps = list(deps)
        for d in o.deps:
            d.marked = True
        for b in r:
            if dma:
                b.rd.append(o)
            else:
                b.r[eng] = o
        for b in w:
            b.w = o
            b.r = {}
            b.rd = []
        self.streams[eng].append(o)
        if dma:
            self.dmas.append(o)
        return o

    def barrier(self):
        lasts = []
        for s in self.streams.values():
            for o in reversed(s):
                if o.fn is not None:
                    lasts.append(o)
                    break
        pend = list(self.dmas)
        self.dmas = []
        for e in self.ENG:
            o = Op()
            o.eng, o.fn, o.dma, o.marked, o.sem, o.val = e, None, False, False, None, 0
            o.deps = list(set(lasts + pend))
            for d in o.deps:
                d.marked = True
            self.streams[e].append(o)

    def mm(self, out, lhsT, rhs, start, stop, r, w):
        return self.op("pe", lambda e: e.matmul(out, lhsT, rhs, start=start, stop=stop), r, w)

    def tr(self, out, in_, ident, r, w):
        return self.op("pe", lambda e: e.transpose(out, in_, ident), r, w)

    def act(self, out, in_, func, r, w, bias=None, scale=None, accum=None):
        kw = {}
        if bias is not None:
            kw["bias"] = bias
        if scale is not None:
            kw["scale"] = scale
        if accum is not None:
            kw["accum_out"] = accum
        return self.op("act", lambda e: e.activation(out, in_, func, **kw), r, w)

    def ts(self, out, in0, s1, s2, op0, op1, r, w, eng="dve"):
        if op1 is None:
            return self.op(eng, lambda e: e.tensor_scalar(out, in0, s1, None, op0), r, w)
        return self.op(eng, lambda e: e.tensor_scalar(out, in0, s1, s2, op0, op1), r, w)

    def tt(self, out, in0, in1, op, r, w, eng="dve"):
        return self.op(eng, lambda e: e.tensor_tensor(out, in0, in1, op), r, w)

    def stt(self, out, in0, sc, in1, op0, op1, r, w):
        return self.op("dve", lambda e: e.scalar_tensor_tensor(out, in0, sc, in1, op0, op1), r, w)

    def cp(self, out, in_, r, w, eng="dve"):
        if eng == "act":
            return self.op("act", lambda e: e.copy(out, in_), r, w)
        return self.op(eng, lambda e: e.tensor_copy(out, in_), r, w)

    def ms(self, ap, val, w, eng="dve"):
        return self.op(eng, lambda e: e.memset(ap, val), (), w)

    def dma(self, out, in_, r, w, eng="sp"):
        return self.op(eng, lambda e: e.dma_start(out=out, in_=in_), r, w, dma=True)

    def finalize(self, final_deps):
        nc = self.nc
        sems = []

        def newsem():
            s = nc.alloc_semaphore("s%d" % len(sems))
            sems.append(s)
            return len(sems) - 1

        for e in self.ENG:
            fo = Op()
            fo.eng, fo.fn, fo.dma, fo.marked, fo.sem, fo.val = e, None, False, False, None, 0
            fo.deps = list(final_deps) if e == "sp" else []
            self.streams[e].append(fo)
        for e in self.ENG:
            cnt, sem = 0, None
            ring, rval, rlast, nd = [], [], [], 0
            for o in self.streams[e]:
                if o.fn is None:
                    continue
                if o.dma:
                    if len(ring) < DMA_RING:
                        ring.append(newsem())
                        rval.append(0)
                        rlast.append(None)
                    slot = nd % DMA_RING
                    nd += 1
                    if rlast[slot] is not None:
                        o.deps.append(rlast[slot])
                    rval[slot] += 16
                    o.sem, o.val = ring[slot], rval[slot]
                    rlast[slot] = o
                elif o.marked:
                    if sem is None or cnt >= SEM_LIMIT:
                        sem, cnt = newsem(), 0
                    cnt += 1
                    o.sem, o.val = sem, cnt
        self.nsem = len(sems)
        streams = self.streams

        def mk(eng):
            def body(e):
                waited = {}
                for o in streams[eng]:
                    for d in o.deps:
                        if d.sem is None:
                            continue
                        if waited.get(d.sem, 0) < d.val:
                            e.wait_ge(sems[d.sem], d.val)
                            waited[d.sem] = d.val
                    if o.fn is None:
                        continue
                    ins = o.fn(e)
                    if o.sem is not None:
                        ins.then_inc(sems[o.sem], 16 if o.dma else 1)
            return body

        with nc.Block() as block:
            block.tensor(mk("pe"))
            block.scalar(mk("act"))
            block.vector(mk("dve"))
            block.gpsimd(mk("pool"))
            block.sync(mk("sp"))


class Tile:
    __slots__ = ("t", "b")

    def __init__(self, t):
        self.t = t
        self.b = Buf()

    def __getitem__(self, k):
        return self.t[k]


class View:
    __slots__ = ("t", "b")

    def __init__(self, ap):
        self.t = ap
        self.b = Buf()

    def __getitem__(self, k):
        return self.t[k]


class Rot:
    def __init__(self, tiles):
        self.tiles = tiles
        self.i = 0

    def next(self):
        t = self.tiles[self.i % len(self.tiles)]
        self.i += 1
        return t


class NS:
    pass


def build_fused(S, depth=DEPTH, dbg=None, stop=None):
    g = NS()
    nc = bass.Bass("TRN2", target_bir_lowering=False)
    g.nc, g.P = nc, Prog(nc)
    P = g.P
    g.S, g.NT, g.NCH = S, S // 128, S // 512
    NCH = g.NCH
    g.es = ExitStack()
    din = lambda name, shape, dt=F32: nc.dram_tensor(name, list(shape), dt, kind="ExternalInput").ap()
    dsc = lambda name, shape, dt: nc.dram_tensor(name, list(shape), dt, kind=("ExternalOutput" if dbg else "Internal")).ap()
    g.sb = lambda name, shape, dt=F32: Tile(g.es.enter_context(nc.sbuf_tensor("sb_" + name, list(shape), dt)))
    g.pbig = g.es.enter_context(nc.psum_tensor("pbig", [128, 3072], F32))
    g.banks = [View(g.pbig[:, i * 512:(i + 1) * 512]) for i in range(6)]
    g.banks += [Tile(g.es.enter_context(nc.psum_tensor("bank%d" % i, [128, 512], F32))) for i in range(6, 8)]
    x_in = din("x", [S, D])
    out_d = nc.dram_tensor("out", [S, D], F32, kind="ExternalOutput").ap()
    W = NS()
    W.w_in = din("w_in", [depth, D, DIN])
    W.w_out = din("w_out", [depth, D, D])
    W.norm_g = din("norm_g", [depth, 128, 8])
    W.final_g = din("final_g", [128, D])
    W.fox_fb = din("fox_fb", [depth, 4, 1])
    W.wuq = din("mla_w_uq", [depth, 384, 384])
    W.qng = din("mla_q_norm", [depth, 128, 3])
    W.wukv = din("mla_w_ukv", [depth, 128, 512])
    W.kvng = din("mla_kv_norm", [depth, 128, 1])
    W.s5a = din("s5_a", [depth, 3, 128, 16])
    W.s5a16 = din("s5_a16", [depth, 3, 16, 1024])
    W.s5b = din("s5_b", [depth, 2, 16, 1024])
    W.s5c = din("s5_c", [depth, 2, 128, 256])
    W.s5d = din("s5_d", [depth, 128, 2])
    W.gluw = din("s5_glu_w", [depth, 256, 256])
    W.glub = din("s5_glu_b", [depth, 128, 2])
    g.rope_d = din("rope", [2, 2, 128, S])
    g.ident_d = din("ident", [128, 128], BF)
    g.negm_d = din("negm", [128, 4, 512], BF)
    g.blkind_d = din("blkind", [32, S], BF)
    g.cmask_d = din("cmask", [NCH, 128, 512])
    g.notown_d = din("notown", [NCH, 128, 512])
    g.idf_d = din("identf", [128, 128])
    g.jsw_d = din("swapj", [128, 128])
    g.QA = dsc("QA", [12, 96, S], BF)
    g.KA = dsc("KA", [12, 96, S], BF)
    g.VS = dsc("VS", [12, 128, g.NT, 65], BF)
    g.GS = dsc("GS", [1024, S], BF)
    g.UT = dsc("UT", [256, S], BF)
    g.MIX = dsc("MIX", [1024, S], BF)
    g.XRES = dsc("XRES", [S, D], F32)
    g.RL = dsc("RL", [12 * NCH, 512], F32)
    g.bQA = [[Buf() for _ in range(NCH)] for _ in range(12)]
    g.bKA = [[Buf() for _ in range(NCH)] for _ in range(12)]
    g.bVS = [[Buf() for _ in range(NCH)] for _ in range(12)]
    g.bGS = [[Buf() for _ in range(NCH)] for _ in range(8)]
    g.bUT = [Buf() for _ in range(NCH)]
    g.ones_bf = g.sb("ones_bf", [128, 512], BF)
    g.ones_f = g.sb("ones_f", [128, 512], F32)
    g.epsc = g.sb("epsc", [128, 1], F32)
    g.onec = g.sb("onec", [128, 1], F32)
    g.ident = g.sb("ident", [128, 128], BF)
    zt = g.sb("zt", [32, 512], BF)
    P.ms(g.ones_bf[:], 1.0, [g.ones_bf.b])
    P.ms(g.ones_f[:], 1.0, [g.ones_f.b])
    P.ms(g.epsc[:], EPS, [g.epsc.b])
    P.ms(g.onec[:], 1.0, [g.onec.b])
    P.ms(zt[:], 0.0, [zt.b])
    P.dma(g.ident[:], g.ident_d, (), [g.ident.b])
    for h in range(4):
        for c in range(NCH):
            cs = slice(c * 512, (c + 1) * 512)
            P.dma(g.QA[h, 67:70, cs], g.ones_bf[0:3, :], [g.ones_bf.b], [g.bQA[h][c]])
            P.dma(g.KA[h, 64:67, cs], g.ones_bf[0:3, :], [g.ones_bf.b], [g.bKA[h][c]])
            P.dma(g.QA[h, 70:96, cs], zt[0:26, :], [zt.b], [g.bQA[h][c]])
            P.dma(g.KA[h, 70:96, cs], zt[0:26, :], [zt.b], [g.bKA[h][c]])
    P.barrier()
    for l in range(depth):
        last = l == depth - 1
        g.uid = l
        g.x_in = x_in if l == 0 else g.XRES
        g.w_in_d, g.normg_d, g.foxfb_d = W.w_in[l], W.norm_g[l], W.fox_fb[l]
        g.wuq_d, g.qng_d, g.wukv_d, g.kvng_d = W.wuq[l], W.qng[l], W.wukv[l], W.kvng[l]
        phase1(g)
        P.barrier()
        if dbg == (l, 1) or stop == (l, 1):
            break
        phase2f(g)
        P.barrier()
        if dbg == (l, 2) or stop == (l, 2):
            break
        phase3f(g, W.s5a[l], W.s5a16[l], W.s5b[l], W.s5c[l], W.s5d[l], W.gluw[l], W.glub[l])
        P.barrier()
        if dbg == (l, 3) or stop == (l, 3):
            break
        phase4f(g, W.w_out[l], W.final_g, g.x_in, (out_d if last else g.XRES), last)
        P.barrier()
    finals = [o for s in P.streams.values() for o in s if o.dma]
    P.finalize(finals)
    g.es.close()
    return nc


def phase2f(g):
    nc, P, banks, S, NT, NCH = g.nc, g.P, g.banks, g.S, g.NT, g.NCH
    UT_ = 3
    with ExitStack() as es:
        sb = lambda name, shape, dt=F32: Tile(es.enter_context(nc.sbuf_tensor("p2_%d_" % g.uid + name, list(shape), dt)))
        negm = sb("negm", [128, 4, 512], BF)
        P.dma(negm[:], g.negm_d, (), [negm.b])
        Qs = [sb("Qt%d" % i, [96, S], BF) for i in range(2)]
        Ks = [sb("Kt%d" % i, [96, S], BF) for i in range(2)]
        Vs = [sb("Vt%d" % i, [128, NT, 65], BF) for i in range(2)]
        Gs = [sb("Gt%d" % i, [64, S], BF) for i in range(2)]
        scb = [View(g.pbig[:, 0:1536]), View(g.pbig[:, 1536:3072])]
        oacc = Rot(banks[6:8])
        pts = Rot([sb("pt%d" % i, [128, 512 * UT_], BF) for i in range(4)])
        rl = Rot([sb("rl%d" % i, [128, 512]) for i in range(5)])
        rlb = Rot([sb("rlb%d" % i, [64, 512]) for i in range(5)])
        osb = Rot([sb("osb%d" % i, [65, 512]) for i in range(5)])
        ot = Rot([sb("ot%d" % i, [64, 512]) for i in range(2)])
        om = Rot([sb("om%d" % i, [64, 512], BF) for i in range(2)])
        grow = lambda h: (64 * h if h < 4 else 256 + 64 * h)

        def load(h):
            Qt, Kt, Vt, Gt = Qs[h % 2], Ks[h % 2], Vs[h % 2], Gs[h % 2]
            moba = 4 <= h < 8
            for c in range(NCH):
                cs = slice(c * 512, (c + 1) * 512)
                P.dma(Qt[:, cs], g.QA[h, :, cs], (), [Qt.b], eng="pool")
                if moba:
                    P.dma(Kt[0:64, cs], g.KA[h, 0:64, cs], (), [Kt.b], eng="pool")
                    P.dma(Kt[64:96, cs], g.blkind_d[:, cs], (), [Kt.b], eng="pool")
                else:
                    P.dma(Kt[:, cs], g.KA[h, :, cs], (), [Kt.b], eng="pool")
            P.dma(Vt[:], g.VS[h], (), [Vt.b], eng="pool")
            P.dma(Gt[:], g.GS[grow(h):grow(h) + 64, :], (), [Gt.b], eng="pool")

        units = []
        for h in range(12):
            for qc in range(NCH):
                nk = 4 * qc + 4
                k0 = 0
                while k0 < nk:
                    n_ = min(UT_, nk - k0)
                    units.append((h, qc, k0, n_, k0 + n_ == nk))
                    k0 += n_
        state = {"oa": None}
        sbuf_of = {}
        pt_of = {}

        def emit_qk(n):
            h, qc, k0, n_, _ = units[n]
            Qt, Kt = Qs[h % 2], Ks[h % 2]
            qs = slice(qc * 512, (qc + 1) * 512)
            sp = scb[n % 2]
            sbuf_of[n] = sp
            for t in range(n_):
                kt = k0 + t
                j = kt - 4 * qc
                ks = slice(kt * 128, (kt + 1) * 128)
                dst = sp[:, t * 512:(t + 1) * 512]
                if j >= 0:
                    lo = 128 * max(j, 0)
                    dstj = sp[:, t * 512 + lo:(t + 1) * 512]
                    qsj = slice(qc * 512 + lo, (qc + 1) * 512)
                    P.mm(dstj, g.ident[:], negm[:, j, lo:512], True, False, [g.ident.b, negm.b], [sp.b])
                    P.mm(dstj, Kt[:, ks], Qt[:, qsj], False, True, [Kt.b, Qt.b], [sp.b])
                else:
                    P.mm(dst, Kt[:, ks], Qt[:, qs], True, True, [Kt.b, Qt.b], [sp.b])

        def emit_exp(n):
            n_ = units[n][3]
            sp = sbuf_of.pop(n)
            pt = pts.next()
            pt_of[n] = pt
            P.act(pt[:, 0:512 * n_], sp[:, 0:512 * n_], AF.Exp, [sp.b], [pt.b], scale=0.125)

        pend = []

        def epi_head(h, qc, oa):
            a, rb, o1 = rl.next(), rlb.next(), osb.next()
            P.cp(o1[:], oa[0:65, :], [oa.b], [o1.b])
            P.op("dve", lambda e, a=a, o1=o1: e.reciprocal(a[64:65, :], o1[64:65, :]), [o1.b], [a.b])
            rw = g.RL[h * NCH + qc:h * NCH + qc + 1, :]
            bw = Buf()
            P.dma(rw, a[64:65, :], [a.b], [bw])
            P.dma(rb[:], rw.to_broadcast([64, 512]), [bw], [rb.b])
            pend.append([0, h, qc, rb, o1])

        def epi_tail(h, qc, rb, o1):
            Gt = Gs[h % 2]
            qs = slice(qc * 512, (qc + 1) * 512)
            g0 = grow(h)
            o2, o3 = ot.next(), om.next()
            P.tt(o2[:], o1[0:64, :], rb[:], ALU.mult, [o1.b, rb.b], [o2.b])
            P.tt(o3[:], o2[:], Gt[:, qs], ALU.mult, [o2.b, Gt.b], [o3.b])
            P.dma(g.MIX[g0:g0 + 64, qs], o3[:], [o3.b], [Buf()])

        def flush(min_age):
            while pend and pend[0][0] >= min_age:
                _, h2, qc2, rb, o1 = pend.pop(0)
                epi_tail(h2, qc2, rb, o1)

        def emit_pv(n):
            h, qc, k0, n_, lastu = units[n]
            Vt = Vs[h % 2]
            nk = 4 * qc + 4
            if k0 == 0:
                state["oa"] = oacc.next()
            oa = state["oa"]
            pt = pt_of.pop(n)
            for t in range(n_):
                kt = k0 + t
                lo = 128 * max(kt - 4 * qc, 0)
                P.mm(oa[0:65, lo:512], Vt[:, kt, :], pt[:, t * 512 + lo:(t + 1) * 512], kt == 0, kt == nk - 1, [Vt.b, pt.b], [oa.b])
            if lastu:
                epi_head(h, qc, oa)
            for p_ in pend:
                p_[0] += 1
            flush(6)

        load(0)
        N = len(units)
        emit_qk(0)
        for n in range(N):
            h, qc, k0 = units[n][0], units[n][1], units[n][2]
            if h + 1 < 12 and ((NCH > 3 and qc == 3 and k0 == 0) or (NCH <= 3 and qc == 0 and k0 == 0)):
                if NCH <= 3:
                    flush(0)
                load(h + 1)
            emit_exp(n)
            if n + 1 < N:
                emit_qk(n + 1)
            emit_pv(n)
        flush(0)


def phase4f(g, wo_d, fg_d, x_d, dst_d, last):
    nc, P, banks, NCH = g.nc, g.P, g.banks, g.NCH
    with ExitStack() as es:
        sb = lambda name, shape, dt=F32: Tile(es.enter_context(nc.sbuf_tensor("p4_%d_" % g.uid + name, list(shape), dt)))
        wo = sb("wo", [128, 8, 1024], BF)
        wst = [sb("wst%d" % i, [128, 1024]) for i in range(2)]
        mixs = [sb("mix%d" % i, [128, 8, 512], BF) for i in range(2)]
        xts = [sb("xt%d" % i, [128, 4, 1024]) for i in range(2)]
        xns = [sb("xn%d" % i, [128, 4, 1024]) for i in range(2)]
        yo = sb("yo", [128, 4, 1024])
        gB = sb("gB", [128, 1024])
        junk = sb("junk", [128, 1024], BF)
        ss, sd, rstd = sb("ss", [128, 4]), sb("sd", [128, 4]), sb("rstd", [128, 4])
        P.dma(gB[:], fg_d, (), [gB.b])
        for kt in range(8):
            st = wst[kt % 2]
            P.dma(st[:], wo_d[kt * 128:(kt + 1) * 128, :], (), [st.b])
            P.cp(wo[:, kt, :], st[:], [st.b], [wo.b], eng=("dve" if kt % 2 == 0 else "act"))
        pr = Rot(banks[0:4])

        def load(c):
            cs = slice(c * 512, (c + 1) * 512)
            P.dma(mixs[c % 2][:], g.MIX[:, cs].rearrange("(kt p) t -> p kt t", p=128), (), [mixs[c % 2].b])
            P.dma(xts[c % 2][:], x_d[cs, :].rearrange("(j p) d -> p j d", p=128), (), [xts[c % 2].b])

        load(0)
        for c in range(NCH):
            if c + 1 < NCH:
                load(c + 1)
            cs = slice(c * 512, (c + 1) * 512)
            mix, xt, xn = mixs[c % 2], xts[c % 2], xns[c % 2]
            for j in range(4):
                for half in range(2):
                    ps = pr.next()
                    for kt in range(8):
                        P.mm(ps[:], mix[:, kt, j * 128:(j + 1) * 128], wo[:, kt, half * 512:(half + 1) * 512], kt == 0, kt == 7,
                             [mix.b, wo.b], [ps.b])
                    P.tt(xn[:, j, half * 512:(half + 1) * 512], ps[:], xt[:, j, half * 512:(half + 1) * 512], ALU.add,
                         [ps.b, xt.b], [xn.b])
            if not last:
                P.dma(dst_d[cs, :].rearrange("(j p) d -> p j d", p=128), xn[:], [xn.b], [Buf()])
            else:
                for j in range(4):
                    P.act(junk[:], xn[:, j, :], AF.Square, [xn.b], [junk.b, ss.b], accum=ss[:, j:j + 1])
                P.act(sd[:], ss[:], AF.Sqrt, [ss.b, g.epsc.b], [sd.b], bias=g.epsc[:], scale=1.0 / D)
                P.op("dve", lambda e: e.reciprocal(rstd[:], sd[:]), [sd.b], [rstd.b])
                for j in range(4):
                    P.stt(yo[:, j, :], xn[:, j, :], rstd[:, j:j + 1], gB[:], ALU.mult, ALU.mult, [xn.b, rstd.b, gB.b], [yo.b])
                P.dma(dst_d[cs, :].rearrange("(j p) d -> p j d", p=128), yo[:], [yo.b], [Buf()])
def phase1(g):
    nc, P = g.nc, g.P
    NCH = g.NCH
    l = 0
    banks, ident = g.banks, g.ident
    xsrc = g.x_in
    with ExitStack() as es:
        def sb(name, shape, dt=F32):
            return Tile(es.enter_context(nc.sbuf_tensor("p1_%d_" % g.uid + name, list(shape), dt)))

        wA = sb("wA", [128, 8, DIN], BF)
        wR = sb("wR", [128, 8, 576], BF)
        wuq = sb("wuq", [128, 3, 384], BF)
        wuqR = sb("wuqR", [128, 3, 384], BF)
        wk = sb("wk", [128, 256], BF)
        wv = sb("wv", [128, 256], BF)
        gcol = sb("gcol", [128, 8])
        qng = sb("qng", [128, 3])
        kvng = sb("kvng", [128, 1])
        negfb = sb("negfb", [4, 1])
        P.dma(gcol[:], g.normg_d, (), [gcol.b])
        P.dma(qng[:], g.qng_d, (), [qng.b])
        P.dma(kvng[:], g.kvng_d, (), [kvng.b])
        P.dma(negfb[:], g.foxfb_d, (), [negfb.b])
        P.ts(negfb[:], negfb[:], -1.0, None, ALU.mult, None, [negfb.b], [negfb.b])
        with ExitStack() as es2:
            wst = [Tile(es2.enter_context(nc.sbuf_tensor("p1_%d_wst%d" % (g.uid, i), [128, DIN], F32))) for i in range(6)]
            for kt in range(6):
                P.dma(wst[kt][:], g.w_in_d[kt * 128:(kt + 1) * 128, :], (), [wst[kt].b], eng=("sp" if kt % 2 == 0 else "act"))
            for kt in range(8):
                st = wst[kt % 6]
                if kt >= 6:
                    P.dma(st[:], g.w_in_d[kt * 128:(kt + 1) * 128, :], (), [st.b], eng=("sp" if kt % 2 == 0 else "act"))
                if kt % 2 == 0:
                    P.ts(wA[:, kt, :], st[:], gcol[:, kt:kt + 1], None, ALU.mult, None, [st.b, gcol.b], [wA.b])
                else:
                    P.act(wA[:, kt, :], st[:], AF.Copy, [st.b, gcol.b], [wA.b], scale=gcol[:, kt:kt + 1])
            for i in range(3):
                st = wst[i % 2]
                P.dma(st[:, 0:384], g.wuq_d[i * 128:(i + 1) * 128, :], (), [st.b])
                P.ts(wuq[:, i, :], st[:, 0:384], qng[:, i:i + 1], 0.816496580927726, ALU.mult, ALU.mult, [st.b, qng.b], [wuq.b])
            st = wst[1]
            P.dma(st[:, 0:512], g.wukv_d, (), [st.b])
            sv = st[:, 0:512].rearrange("p (h e) -> p h e", h=4)
            P.ts(wk[:].rearrange("p (h e) -> p h e", h=4), sv[:, :, 0:64], kvng[:, 0:1], None, ALU.mult, None,
                 [st.b, kvng.b], [wk.b])
            P.ts(wv[:].rearrange("p (h e) -> p h e", h=4), sv[:, :, 64:128], kvng[:, 0:1], None, ALU.mult, None,
                 [st.b, kvng.b], [wv.b])
            P.barrier()
        P.ms(wR[:], 0.0, [wR.b])
        P.ms(wuqR[:], 0.0, [wuqR.b])
        for kt in range(8):
            for (src_o, dst_o) in ((O_MQ, 0), (O_MK, 256)):
                s4 = wA[:, kt, src_o:src_o + 256].rearrange("p (h t j) -> p h t j", h=4, t=2)
                d4 = wR[:, kt, dst_o:dst_o + 256].rearrange("p (h t j) -> p h t j", h=4, t=2)
                P.ts(d4[:, :, 0, :], s4[:, :, 1, :], -1.0, None, ALU.mult, None, [wA.b], [wR.b], eng="dve")
                P.cp(d4[:, :, 1, :], s4[:, :, 0, :], [wA.b], [wR.b], eng="act")
            P.ts(wR[:, kt, 512:528], wA[:, kt, O_KR + 16:O_KR + 32], -1.0, None, ALU.mult, None, [wA.b], [wR.b], eng="dve")
            P.cp(wR[:, kt, 528:544], wA[:, kt, O_KR:O_KR + 16], [wA.b], [wR.b], eng="act")
        for i in range(3):
            s3 = wuq[:, i, :].rearrange("p (h e) -> p h e", h=4)
            d3 = wuqR[:, i, :].rearrange("p (h e) -> p h e", h=4)
            P.ts(d3[:, :, 64:80], s3[:, :, 80:96], -1.0, None, ALU.mult, None, [wuq.b], [wuqR.b], eng="dve")
            P.cp(d3[:, :, 80:96], s3[:, :, 64:80], [wuq.b], [wuqR.b], eng="act")

        xt = [sb("xt%d" % i, [128, 4, 1024]) for i in range(2)]
        rt = [sb("rt%d" % i, [128, 4, 512]) for i in range(1)]
        xs = sb("xs", [128, 4, 1024], BF)
        junk = sb("junk", [128, 1024], BF)
        hT = [sb("hT%d" % i, [128, 8, 512], BF) for i in range(2)]
        ss, sd, rstd = sb("ss", [128, 4]), sb("sd", [128, 4]), sb("rstd", [128, 4])
        stA = Rot([sb("stA%d" % i, [128, 512], BF) for i in range(6)])
        tf = Rot([sb("tf%d" % i, [128, 512]) for i in range(4)])
        vst = Rot([sb("vst%d" % i, [128, 4, 65], BF) for i in range(3)])
        for t in vst.tiles:
            P.ms(t[:], 1.0, [t.b])
        cqf = sb("cqf", [128, 3, 512])
        cqn = sb("cqn", [128, 3, 512], BF)
        sqb = Rot([sb("sqb%d" % i, [128, 512], BF) for i in range(2)])
        ckn = sb("ckn", [128, 512], BF)
        rq = sb("rq", [128, 512])
        rq2 = sb("rq2", [128, 512])
        ckf = sb("ckf", [128, 512])
        nlf = sb("nlf", [4, 512])
        ncum = [sb("ncum%d" % i, [4, 512]) for i in range(2)]
        s8, r1, r2 = sb("s8", [4, 512]), sb("r1", [4, 512]), sb("r2", [4, 512])
        e1 = r1
        kst = Rot([sb("kst%d" % i, [4, 3, 512], BF) for i in range(1)])
        qst = Rot([sb("qst%d" % i, [4, 3, 512], BF) for i in range(1)])
        kmf = [sb("kmf%d" % i, [128, 32]) for i in range(2)]
        kmz = [sb("kmz%d" % i, [128, 32], BF) for i in range(4)]
        for t in kmf:
            P.ms(t[:], 0.0, [t.b])
        for i in range(4):
            P.ms(kmz[i][:], 0.0, [kmz[i].b])
        cmask = sb("cmask", [128, 512])
        notown = sb("notown", [128, 512])
        ncin = sb("ncin", [4, 1])
        P.ms(ncin[:], 0.0, [ncin.b])
        mqs = [sb("mqs%d" % i, [128, 512], BF) for i in range(2)]
        gm = sb("gm", [128, 512])
        t8 = sb("t8", [128, 128])
        thr = sb("thr", [128, 16])
        mbf = sb("mbf", [128, 512])
        mb = sb("mb", [128, 512], BF)
        mbT = sb("mbT", [32, 2048], BF)
        pr = Rot(banks[0:6])
        ptb = [banks[6], banks[7]]

        def load(c):
            xc = xt[c % 2]
            rd = []
            P.dma(xc[:], xsrc[c * 512:(c + 1) * 512, :].rearrange("(j p) d -> p j d", p=128), rd, [xc.b])

        def pre_norm(c):
            xc = xt[c % 2]
            for j in range(4):
                P.act(junk[:], xc[:, j, :], AF.Square, [xc.b], [junk.b, ss.b], accum=ss[:, j:j + 1])
            P.act(sd[:], ss[:], AF.Sqrt, [ss.b, g.epsc.b], [sd.b], bias=g.epsc[:], scale=1.0 / D)
            P.op("dve", lambda e: e.reciprocal(rstd[:], sd[:]), [sd.b], [rstd.b])
            for j in range(4):
                P.ts(xs[:, j, :], xc[:, j, :], rstd[:, j:j + 1], None, ALU.mult, None, [xc.b, rstd.b], [xs.b])

        def pre_tr(c):
            h = hT[c % 2]
            for kt in range(8):
                pt = ptb[kt % 2]
                ptv = pt[:].bitcast(BF)
                for j in range(4):
                    P.tr(ptv[:, j * 128:(j + 1) * 128], xs[:, j, kt * 128:(kt + 1) * 128], ident[:],
                         [xs.b, ident.b], [pt.b])
                P.cp(h[:, kt, :], ptv[:, 0:512], [pt.b], [h.b], eng=("dve" if kt % 2 == 0 else "act"))

        load(0)
        pre_norm(0)
        pre_tr(0)
        for c in range(NCH):
            if c + 1 < NCH:
                load(c + 1)
            cs = slice(c * 512, (c + 1) * 512)
            xc, h, rc = xt[c % 2], hT[c % 2], rt[0]
            P.dma(cmask[:], g.cmask_d[c], (), [cmask.b])
            P.dma(notown[:], g.notown_d[c], (), [notown.b])
            P.dma(rc[:], g.rope_d[:, :, :, c * 512:(c + 1) * 512].rearrange("a b p t -> p (a b) t"), (), [rc.b])
            def proj(src, lo, M):
                ps = pr.next()
                for kt in range(8):
                    P.mm(ps[0:M, :], src[:, kt, lo:lo + M], h[:, kt, :], kt == 0, kt == 7, [src.b, h.b], [ps.b])
                return ps

            def split_store(st, dst, bufs, h0):
                P.dma(dst[h0, 0:64, cs], st[0:64, :], [st.b], [bufs[h0][c]])
                P.dma(dst[h0 + 1, 0:64, cs], st[64:128, :], [st.b], [bufs[h0 + 1][c]])

            def values(j):
                for (vi, off) in ((0, O_FV), (1, O_MV), (2, None)):
                    ps = pr.next()
                    if off is not None:
                        for kt in range(8):
                            P.mm(ps[:, 0:256], h[:, kt, j * 128:(j + 1) * 128], wA[:, kt, off:off + 256], kt == 0, kt == 7,
                                 [wA.b, h.b], [ps.b])
                    else:
                        P.mm(ps[:, 0:256], ckn[:, j * 128:(j + 1) * 128], wv[:], True, True, [ckn.b, wv.b], [ps.b])
                    vt = vst.next()
                    P.cp(vt[:, :, 0:64], ps[:, 0:256].rearrange("p (h e) -> p h e", h=4), [ps.b], [vt.b], eng="act")
                    P.dma(g.VS[4 * vi:4 * vi + 4, :, c * 4 + j, :].rearrange("h p e -> p h e"), vt[:],
                          [vt.b], [g.bVS[4 * vi + k][c] for k in range(4)])

            for i in range(3):
                ps = proj(wA, O_CQ + 128 * i, 128)
                P.cp(cqf[:, i, :], ps[:], [ps.b], [cqf.b], eng="dve")
            ssb = pr.next()
            for i in range(3):
                sq = sqb.next()
                P.act(sq[:], cqf[:, i, :], AF.Square, [cqf.b], [sq.b])
                P.mm(ssb[:], g.ones_bf[:, 0:128], sq[:], i == 0, i == 2, [g.ones_bf.b, sq.b], [ssb.b])
            P.act(rq[:], ssb[:], AF.Ln, [ssb.b, g.epsc.b], [rq.b], bias=g.epsc[:], scale=1.0 / 384)
            P.act(rq[:], rq[:], AF.Exp, [rq.b], [rq.b], scale=-0.5)
            ps = proj(wA, O_CKV, 128)
            P.cp(ckf[:], ps[:], [ps.b], [ckf.b], eng="dve")
            sq = sqb.next()
            P.act(sq[:], ckf[:], AF.Square, [ckf.b], [sq.b])
            ssb = pr.next()
            P.mm(ssb[:], g.ones_bf[:, 0:128], sq[:], True, True, [g.ones_bf.b, sq.b], [ssb.b])
            P.act(rq2[:], ssb[:], AF.Ln, [ssb.b, g.epsc.b], [rq2.b], bias=g.epsc[:], scale=1.0 / 128)
            P.act(rq2[:], rq2[:], AF.Exp, [rq2.b], [rq2.b], scale=-0.5)
            for pair in range(2):
                for (off, dst, bufs) in ((O_FQ, g.QA, g.bQA), (O_FK, g.KA, g.bKA)):
                    ps = proj(wA, off + 128 * pair, 128)
                    st = stA.next()
                    P.cp(st[:], ps[:], [ps.b], [st.b], eng="act")
                    split_store(st, dst, bufs, 2 * pair)
            for gi, off in enumerate((O_FG, O_FG + 128, O_SG, O_SG + 128, O_MG, O_MG + 128, O_LG, O_LG + 128)):
                ps = proj(wA, off, 128)
                st = stA.next()
                P.act(st[:], ps[:], AF.Silu, [ps.b], [st.b])
                P.dma(g.GS[gi * 128:(gi + 1) * 128, cs], st[:], [st.b], [g.bGS[gi][c]])
            for i in range(3):
                P.tt(cqn[:, i, :], cqf[:, i, :], rq[:], ALU.mult, [cqf.b, rq.b], [cqn.b])
            P.tt(ckn[:], ckf[:], rq2[:], ALU.mult, [ckf.b, rq2.b], [ckn.b])
            if c + 1 < NCH:
                pre_norm(c + 1)
            ps = proj(wA, O_FF, 4)
            P.act(e1[:], ps[0:4, :], AF.Exp, [ps.b, negfb.b], [e1.b], bias=negfb[:], scale=-1.0)
            P.act(nlf[:], e1[:], AF.Ln, [e1.b, g.onec.b], [nlf.b], bias=g.onec[0:4, :], scale=1.0)
            nc_cur, nc_prev = ncum[c % 2], ncum[(c + 1) % 2]
            if c == 0:
                P.op("dve", lambda e, o=nc_cur: e.tensor_tensor_scan(o[:], g.ones_f[0:4, :], nlf[:], ncin[:, 0:1], ALU.mult, ALU.add),
                     [g.ones_f.b, nlf.b, ncin.b], [nc_cur.b])
            else:
                P.op("dve", lambda e, o=nc_cur, p_=nc_prev: e.tensor_tensor_scan(o[:], g.ones_f[0:4, :], nlf[:], p_[:, 511:512], ALU.mult, ALU.add),
                     [g.ones_f.b, nlf.b, nc_prev.b], [nc_cur.b])
            ks, qs = kst.next(), qst.next()
            P.ts(s8[:], nc_cur[:], 8.0, None, ALU.mult, None, [nc_cur.b], [s8.b])
            P.cp(ks[:, 0, :], s8[:], [s8.b], [ks.b])
            P.tt(r1[:], s8[:], ks[:, 0, :], ALU.subtract, [s8.b, ks.b], [r1.b])
            P.cp(ks[:, 1, :], r1[:], [r1.b], [ks.b])
            P.tt(r2[:], r1[:], ks[:, 1, :], ALU.subtract, [r1.b, ks.b], [r2.b])
            P.cp(ks[:, 2, :], r2[:], [r2.b], [ks.b])
            P.ts(qs[:], ks[:], -1.0, None, ALU.mult, None, [ks.b], [qs.b])
            for hh in range(4):
                P.dma(g.QA[hh:hh + 1, 64:67, cs], qs[hh:hh + 1, :, :], [qs.b], [g.bQA[hh][c]])
                P.dma(g.KA[hh:hh + 1, 67:70, cs], ks[hh:hh + 1, :, :], [ks.b], [g.bKA[hh][c]])
            for i in range(2):
                ps = proj(wA, O_SU + 128 * i, 128)
                st = stA.next()
                P.cp(st[:], ps[:], [ps.b], [st.b], eng="act")
                P.dma(g.UT[i * 128:(i + 1) * 128, cs], st[:], [st.b], [g.bUT[c]])
            for (off, roff, isq) in ((O_MK, 256, False), (O_MQ, 0, True)):
                for pair in range(2):
                    ps1 = proj(wA, off + 128 * pair, 128)
                    ps2 = proj(wR, roff + 128 * pair, 128)
                    t1, t2 = tf.next(), tf.next()
                    P.tt(t1[:], ps1[:], rc[:, 0, :], ALU.mult, [ps1.b, rc.b], [t1.b])
                    P.tt(t2[:], ps2[:], rc[:, 2, :], ALU.mult, [ps2.b, rc.b], [t2.b])
                    if isq:
                        st = mqs[pair]
                        P.tt(st[:], t1[:], t2[:], ALU.add, [t1.b, t2.b], [st.b])
                        split_store(st, g.QA, g.bQA, 4 + 2 * pair)
                    else:
                        st = stA.next()
                        P.tt(st[:], t1[:], t2[:], ALU.add, [t1.b, t2.b], [st.b])
                        split_store(st, g.KA, g.bKA, 4 + 2 * pair)
                        P.op("dve", lambda e, o=kmf[pair], s_=st: e.tensor_reduce(
                            o[:, 0:2], s_[:].rearrange("p (b t) -> p b t", b=2), AX.X, ALU.add),
                            [st.b], [kmf[pair].b])
                        for hx in range(2):
                            kz = kmz[2 * pair + hx]
                            P.cp(kz[64 * hx:64 * hx + 64, 2 * c:2 * c + 2], kmf[pair][64 * hx:64 * hx + 64, 0:2], [kmf[pair].b], [kz.b])
            values(0)
            gps = pr.next()
            for hh in range(4):
                pair, base = hh // 2, 64 * (hh % 2)
                for j in range(4):
                    i16 = hh * 4 + j
                    P.mm(gps[:, i16 * 32:(i16 + 1) * 32], mqs[pair][:, j * 128:(j + 1) * 128],
                         kmz[hh][:, 0:32], True, True, [mqs[pair].b, kmz[hh].b], [gps.b])
            P.tt(gm[:], gps[:], cmask[:], ALU.add, [gps.b, cmask.b], [gm.b])
            for i16 in range(16):
                P.op("dve", lambda e, i=i16: e.max(t8[:, i * 8:(i + 1) * 8], gm[:, i * 32:(i + 1) * 32]), [gm.b], [t8.b])
            P.ts(thr[:], t8[:].rearrange("p (i e) -> p i e", e=8)[:, :, 2], -1e29, None, ALU.max, None, [t8.b], [thr.b])
            for i16 in range(16):
                P.ts(mbf[:, i16 * 32:(i16 + 1) * 32], gm[:, i16 * 32:(i16 + 1) * 32], thr[:, i16:i16 + 1], -NEGB,
                     ALU.is_ge, ALU.mult, [gm.b, thr.b], [mbf.b])
            P.ts(mb[:], mbf[:], NEGB, None, ALU.add, None, [mbf.b], [mb.b])
            P.tt(mb[:], mb[:], notown[:], ALU.mult, [mb.b, notown.b], [mb.b])
            values(1)
            for hh in range(4):
                psq, psr = pr.next(), pr.next()
                for i in range(3):
                    P.mm(psq[0:96, :], wuq[:, i, 96 * hh:96 * hh + 96], cqn[:, i, :], i == 0, i == 2, [wuq.b, cqn.b], [psq.b])
                for i in range(3):
                    P.mm(psr[0:96, :], wuqR[:, i, 96 * hh:96 * hh + 96], cqn[:, i, :], i == 0, i == 2, [wuqR.b, cqn.b], [psr.b])
                st = stA.next()
                t1, t2 = tf.next(), tf.next()
                P.cp(st[0:64, :], psq[0:64, :], [psq.b], [st.b], eng="act")
                P.tt(t1[64:96, :], psq[64:96, :], rc[64:96, 1, :], ALU.mult, [psq.b, rc.b], [t1.b])
                P.tt(t2[64:96, :], psr[64:96, :], rc[64:96, 3, :], ALU.mult, [psr.b, rc.b], [t2.b])
                P.tt(st[64:96, :], t1[64:96, :], t2[64:96, :], ALU.add, [t1.b, t2.b], [st.b])
                P.dma(g.QA[8 + hh, 0:96, cs], st[0:96, :], [st.b], [g.bQA[8 + hh][c]])
            for pair in range(2):
                ps = pr.next()
                P.mm(ps[:], wk[:, 128 * pair:128 * pair + 128], ckn[:], True, True, [wk.b, ckn.b], [ps.b])
                st = stA.next()
                P.cp(st[:], ps[:], [ps.b], [st.b], eng="act")
                split_store(st, g.KA, g.bKA, 8 + 2 * pair)
            values(2)
            ps1 = proj(wA, O_KR, 32)
            ps2 = proj(wR, 512, 32)
            t1, t2 = tf.next(), tf.next()
            st = stA.next()
            P.tt(t1[0:32, :], ps1[0:32, :], rc[0:32, 1, :], ALU.mult, [ps1.b, rc.b], [t1.b])
            P.tt(t2[0:32, :], ps2[0:32, :], rc[0:32, 3, :], ALU.mult, [ps2.b, rc.b], [t2.b])
            P.tt(st[0:32, :], t1[0:32, :], t2[0:32, :], ALU.add, [t1.b, t2.b], [st.b])
            for hh in range(4):
                P.dma(g.KA[8 + hh, 64:96, cs], st[0:32, :], [st.b], [g.bKA[8 + hh][c]])
            for half2 in range(2):
                pt = ptb[half2]
                ptv = pt[:].bitcast(BF)
                for k8 in range(8):
                    i16 = half2 * 8 + k8
                    P.tr(ptv[0:32, k8 * 128:(k8 + 1) * 128], mb[:, i16 * 32:(i16 + 1) * 32], ident[:], [mb.b, ident.b], [pt.b])
                P.cp(mbT[:, half2 * 1024:(half2 + 1) * 1024], ptv[0:32, :], [pt.b], [mbT.b], eng="act")
            for hh in range(4):
                P.dma(g.QA[4 + hh, 64:96, cs], mbT[:, hh * 512:(hh + 1) * 512], [mbT.b], [g.bQA[4 + hh][c]])
            if c + 1 < NCH:
                pre_tr(c + 1)
            values(3)


def phase3f(g, a_d, a16_d, b_d, c_d, d_d, gw_d, gb_d):
    nc, P, banks, NCH = g.nc, g.P, g.banks, g.NCH
    PI = math.pi
    esp = ExitStack()
    sb = lambda name, shape, dt=F32: Tile(esp.enter_context(nc.sbuf_tensor("p3_%d_" % g.uid + name, list(shape), dt)))
    g_sb_saved = sb

    def ld(name, shape, src, dt=F32):
        t = sb(name, shape, dt)
        P.dma(t[:], src, (), [t.b])
        return t

    A = ld("A", [128, 3, 16], a_d.rearrange("a p g -> p a g"))
    dcol = ld("dcol", [128, 2], d_d)
    gbc = ld("gbc", [128, 2], gb_d)
    gwf = ld("gwf", [128, 2, 256], gw_d.rearrange("(i p) o -> p i o", p=128))
    idf = ld("idf", [128, 128], g.idf_d)
    jsw = ld("jsw", [128, 128], g.jsw_d)
    gwb = sb("gwb", [128, 2, 256], BF)
    P.cp(gwb[:], gwf[:], [gwf.b], [gwb.b])

    def lamparts(pre, src, np_, nf):
        mk = lambda n: sb(pre + n, [np_, nf])
        dt, th, r, k, x1, sn, cs = mk("dt"), mk("th"), mk("r"), mk("k"), mk("x1"), mk("sn"), mk("cs")
        are, aim, ldt = src[:, 0, :], src[:, 1, :], src[:, 2, :]
        P.act(dt[:], ldt, AF.Exp, [src.b], [dt.b])
        P.tt(th[:], aim, dt[:], ALU.mult, [src.b, dt.b], [th.b])
        P.tt(r[:], are, dt[:], ALU.mult, [src.b, dt.b], [r.b])
        P.act(r[:], r[:], AF.Exp, [r.b], [r.b])
        for (dst, shift) in ((sn, 0.0), (cs, PI / 2)):
            P.ts(x1[:], th[:], shift, None, ALU.add, None, [th.b], [x1.b])
            P.ts(k[:], x1[:], 1.0 / (2 * PI), 12582912.0, ALU.mult, ALU.add, [x1.b], [k.b])
            P.ts(k[:], k[:], -12582912.0, None, ALU.add, None, [k.b], [k.b])
            P.stt(x1[:], k[:], -6.28125, x1[:], ALU.mult, ALU.add, [k.b, x1.b], [x1.b])
            P.stt(x1[:], k[:], -0.0019353071795864769, x1[:], ALU.mult, ALU.add, [k.b, x1.b], [x1.b])
            P.ts(x1[:], x1[:], 3.1415925, -3.1415925, ALU.min, ALU.max, [x1.b], [x1.b])
            P.act(dst[:], x1[:], AF.Sin, [x1.b], [dst.b])
        return r, cs, sn

    Ct, St = sb("Ct", [128, 16, 512], BF), sb("St", [128, 16, 512], BF)
    Mrot = sb("Mrot", [128, 16, 128])
    LB, LS = sb("LB", [16, 16, 128], BF), sb("LS", [16, 16, 128], BF)
    c1p, c2p = sb("c1p", [128, 16, 128], BF), sb("c2p", [128, 16, 128], BF)
    rkeep = sb("rkeep", [128, 16])
    Rdec = sb("Rdec", [128, 16, 512])
    xin = sb("xin", [128, 16])
    couts = [sb("cout%d" % i, [128, 16]) for i in range(2)]
    gsb = g_sb_saved
    es1 = ExitStack()
    sb = lambda name, shape, dt=F32: Tile(es1.enter_context(nc.sbuf_tensor("s1_%d_" % g.uid + name, list(shape), dt)))
    r, cm, sm = lamparts("a_", A, 128, 16)
    P.cp(rkeep[:], r[:], [r.b], [rkeep.b])
    for gi in range(16):
        P.act(Rdec[:, gi, :], g.ones_f[:], AF.Copy, [g.ones_f.b, r.b], [Rdec.b], scale=r[:, gi:gi + 1])
    tA, tB = sb("tA", [128, 16, 256]), sb("tB", [128, 16, 256])
    Ct32, St32 = sb("Ct32", [128, 16, 512]), sb("St32", [128, 16, 512])
    P.ms(Ct32[:, :, 0:1], 1.0, [Ct32.b])
    P.ms(St32[:, :, 0:1], 0.0, [St32.b])
    c3 = lambda t: t[:].rearrange("p (g o) -> p g o", o=1)
    P.cp(Ct32[:, :, 1:2], c3(cm), [cm.b], [Ct32.b])
    P.cp(St32[:, :, 1:2], c3(sm), [sm.b], [St32.b])
    m = 1
    q1, q2 = sb("q1", [128, 16]), sb("q2", [128, 16])
    cur = (cm, sm)
    nxt = (sb("cm2", [128, 16]), sb("sm2", [128, 16]))
    while m <= 256:
        c_, s_ = cur
        if m > 1 or True:
            pass
        if m >= 2 or m == 1:
            pass
        if m > 1:
            pass
        cb = c3(c_).to_broadcast([128, 16, m])
        sbb = c3(s_).to_broadcast([128, 16, m])
        if m >= 2:
            P.tt(tA[:, :, 0:m], St32[:, :, 0:m], sbb, ALU.mult, [St32.b, s_.b], [tA.b])
            P.tt(tB[:, :, 0:m], Ct32[:, :, 0:m], cb, ALU.mult, [Ct32.b, c_.b], [tB.b])
            P.tt(Ct32[:, :, m:2 * m], tB[:, :, 0:m], tA[:, :, 0:m], ALU.subtract, [tA.b, tB.b], [Ct32.b])
            P.tt(tA[:, :, 0:m], St32[:, :, 0:m], cb, ALU.mult, [St32.b, c_.b], [tA.b])
            P.tt(tB[:, :, 0:m], Ct32[:, :, 0:m], sbb, ALU.mult, [Ct32.b, s_.b], [tB.b])
            P.tt(St32[:, :, m:2 * m], tB[:, :, 0:m], tA[:, :, 0:m], ALU.add, [tA.b, tB.b], [St32.b])
        c2, s2 = nxt
        P.tt(q1[:], c_[:], c_[:], ALU.mult, [c_.b], [q1.b])
        P.tt(q2[:], s_[:], s_[:], ALU.mult, [s_.b], [q2.b])
        P.tt(c2[:], q1[:], q2[:], ALU.subtract, [q1.b, q2.b], [c2.b])
        P.tt(q1[:], c_[:], s_[:], ALU.mult, [c_.b, s_.b], [q1.b])
        P.ts(s2[:], q1[:], 2.0, None, ALU.mult, None, [q1.b], [s2.b])
        cur, nxt = (c2, s2), (c_, s_)
        m *= 2
    c512, s512 = cur
    P.cp(Ct[:], Ct32[:], [Ct32.b], [Ct.b])
    P.cp(St[:], St32[:], [St32.b], [St.b], eng="act")
    ssg = sb("ssg", [128, 16])
    P.cp(ssg[0:64, :], s512[0:64, :], [s512.b], [ssg.b])
    P.ts(ssg[64:128, :], s512[64:128, :], -1.0, None, ALU.mult, None, [s512.b], [ssg.b])
    for gi in range(16):
        P.ts(Mrot[:, gi, :], idf[:], c512[:, gi:gi + 1], None, ALU.mult, None, [idf.b, c512.b], [Mrot.b])
        P.stt(Mrot[:, gi, :], jsw[:], ssg[:, gi:gi + 1], Mrot[:, gi, :], ALU.mult, ALU.add, [jsw.b, ssg.b, Mrot.b], [Mrot.b])
    P.barrier()
    es1.close()
    es2 = ExitStack()
    sb = lambda name, shape, dt=F32: Tile(es2.enter_context(nc.sbuf_tensor("s2_%d_" % g.uid + name, list(shape), dt)))
    A16 = ld("A16", [16, 3, 1024], a16_d.rearrange("a p g -> p a g"))
    Bw = ld("Bw", [16, 2, 1024], b_d.rearrange("a p g -> p a g"))
    Cw = ld("Cw", [128, 2, 256], c_d.rearrange("a p g -> p a g"))
    r16, c16, s16 = lamparts("b_", A16, 16, 1024)
    mk16 = lambda n: sb("b_" + n, [16, 1024])
    lre, lim, den, cr, ci, w1, w2 = mk16("lre"), mk16("lim"), mk16("den"), mk16("cr"), mk16("ci"), mk16("w1"), mk16("w2")
    are, aim = A16[:, 0, :], A16[:, 1, :]
    P.tt(lre[:], r16[:], c16[:], ALU.mult, [r16.b, c16.b], [lre.b])
    P.tt(lim[:], r16[:], s16[:], ALU.mult, [r16.b, s16.b], [lim.b])
    P.ts(lre[:], lre[:], -1.0, None, ALU.add, None, [lre.b], [lre.b])
    P.tt(w1[:], are, are, ALU.mult, [A16.b], [w1.b])
    P.tt(w2[:], aim, aim, ALU.mult, [A16.b], [w2.b])
    P.tt(den[:], w1[:], w2[:], ALU.add, [w1.b, w2.b], [den.b])
    P.op("dve", lambda e: e.reciprocal(den[:], den[:]), [den.b], [den.b])
    P.tt(w1[:], lre[:], are, ALU.mult, [lre.b, A16.b], [w1.b])
    P.tt(w2[:], lim[:], aim, ALU.mult, [lim.b, A16.b], [w2.b])
    P.tt(cr[:], w1[:], w2[:], ALU.add, [w1.b, w2.b], [cr.b])
    P.tt(cr[:], cr[:], den[:], ALU.mult, [cr.b, den.b], [cr.b])
    P.tt(w1[:], lim[:], are, ALU.mult, [lim.b, A16.b], [w1.b])
    P.tt(w2[:], lre[:], aim, ALU.mult, [lre.b, A16.b], [w2.b])
    P.tt(ci[:], w1[:], w2[:], ALU.subtract, [w1.b, w2.b], [ci.b])
    P.tt(ci[:], ci[:], den[:], ALU.mult, [ci.b, den.b], [ci.b])
    bre, bim = Bw[:, 0, :], Bw[:, 1, :]
    Bre, Bim = mk16("Bre"), mk16("Bim")
    P.tt(w1[:], cr[:], bre, ALU.mult, [cr.b, Bw.b], [w1.b])
    P.tt(w2[:], ci[:], bim, ALU.mult, [ci.b, Bw.b], [w2.b])
    P.tt(Bre[:], w1[:], w2[:], ALU.subtract, [w1.b, w2.b], [Bre.b])
    P.tt(w1[:], cr[:], bim, ALU.mult, [cr.b, Bw.b], [w1.b])
    P.tt(w2[:], ci[:], bre, ALU.mult, [ci.b, Bw.b], [w2.b])
    P.tt(Bim[:], w1[:], w2[:], ALU.add, [w1.b, w2.b], [Bim.b])
    v3 = lambda t: t[:].rearrange("c (g p) -> c g p", g=16)
    P.cp(LB[:, :, 0:64], v3(Bre), [Bre.b], [LB.b])
    P.cp(LB[:, :, 64:128], v3(Bim), [Bim.b], [LB.b])
    P.cp(LS[:, :, 0:64], v3(Bim), [Bim.b], [LS.b])
    P.ts(LS[:, :, 64:128], v3(Bre), -1.0, None, ALU.mult, None, [Bre.b], [LS.b])
    c1f, c2f = sb("c1f", [128, 256]), sb("c2f", [128, 256])
    P.cp(c1f[0:64, :], Cw[0:64, 0, :], [Cw.b], [c1f.b])
    P.ts(c1f[64:128, :], Cw[64:128, 0, :], -1.0, None, ALU.mult, None, [Cw.b], [c1f.b])
    P.ts(c2f[:], Cw[:, 1, :], -1.0, None, ALU.mult, None, [Cw.b], [c2f.b])
    P.ms(c1p[:], 0.0, [c1p.b])
    P.ms(c2p[:], 0.0, [c2p.b])
    for gi in range(16):
        o = 16 * (gi % 8)
        P.cp(c1p[:, gi, o:o + 16], c1f[:, 16 * gi:16 * gi + 16], [c1f.b], [c1p.b])
        P.cp(c2p[:, gi, o:o + 16], c2f[:, 16 * gi:16 * gi + 16], [c2f.b], [c2p.b])
    P.barrier()
    es2.close()
    sb = gsb
    pbr = Rot(banks[0:4])
    yps = [banks[4], banks[5]]
    m1 = Rot([sb("m1_%d" % i, [128, 512], BF) for i in range(4)])
    m2 = Rot([sb("m2_%d" % i, [128, 512], BF) for i in range(4)])
    bt = Rot([sb("bt_%d" % i, [128, 512], BF) for i in range(4)])
    xs = Rot([sb("xs_%d" % i, [128, 512], BF) for i in range(4)])
    d1 = Rot([sb("d1_%d" % i, [128, 512], BF) for i in range(4)])
    d2 = Rot([sb("d2_%d" % i, [128, 512], BF) for i in range(4)])
    rdr = Rot([None])
    pbs = Rot([sb("pbs_%d" % i, [128, 512], BF) for i in range(4)])
    pws = Rot([sb("pws_%d" % i, [128, 512], BF) for i in range(4)])
    ep = {}
    for hf in range(2):
        ep["yv%d" % hf] = sb("yv%d" % hf, [128, 512])
        w_ = sb("wk%d" % hf, [128, 512])
        for n in ("sq", "in", "th", "sg", "o1"):
            ep["%s%d" % (n, hf)] = w_
        ep["o2%d" % hf] = sb("o2%d" % hf, [128, 512], BF)
    gf = [sb("gf%d" % i, [128, 512]) for i in range(2)]
    gb16 = [sb("gb%d" % i, [128, 512], BF) for i in range(2)]
    uTs = [sb("uT%d" % i, [16, 16, 512], BF) for i in range(2)]
    ufs = [sb("uf%d" % i, [128, 2, 512], BF) for i in range(2)]
    gss = [sb("gsT%d" % i, [128, 2, 512], BF) for i in range(2)]
    pc = banks[6]

    def load(c):
        cs_ = slice(c * 512, (c + 1) * 512)
        P.dma(uTs[c % 2][:], g.UT[:, cs_].rearrange("(g c) t -> c g t", c=16), (), [uTs[c % 2].b])
        P.dma(ufs[c % 2][:], g.UT[:, cs_].rearrange("(j p) t -> p j t", p=128), (), [ufs[c % 2].b])
        P.dma(gss[c % 2][:], g.GS[256:512, cs_].rearrange("(j p) t -> p j t", p=128), (), [gss[c % 2].b])

    def epi_a(c):
        uf = ufs[c % 2]
        for hf in range(2):
            yv = ep["yv%d" % hf]
            P.stt(yv[:], uf[:, hf, :], dcol[:, hf:hf + 1], yps[hf][:], ALU.mult, ALU.add, [uf.b, dcol.b, yps[hf].b], [yv.b])

    def epi_b(c):
        cs = slice(c * 512, (c + 1) * 512)
        gsT = gss[c % 2]
        for hf in range(2):
            yv, sq, inn, th = ep["yv%d" % hf], ep["sq%d" % hf], ep["in%d" % hf], ep["th%d" % hf]
            P.tt(sq[:], yv[:], yv[:], ALU.mult, [yv.b], [sq.b])
            P.ts(sq[:], sq[:], 0.044715, 1.0, ALU.mult, ALU.add, [sq.b], [sq.b])
            P.tt(inn[:], sq[:], yv[:], ALU.mult, [sq.b, yv.b], [inn.b])
            P.act(th[:], inn[:], AF.Tanh, [inn.b], [th.b], scale=0.7978845608028654)
            P.stt(gf[hf][:], th[:], 1.0, yv[:], ALU.add, ALU.mult, [th.b, yv.b], [gf[hf].b])
            P.ts(gf[hf][:], gf[hf][:], 0.5, None, ALU.mult, None, [gf[hf].b], [gf[hf].b])
            P.cp(gb16[hf][:], gf[hf][:], [gf[hf].b], [gb16[hf].b])
        for oh in range(2):
            ps = banks[7]
            for ih in range(2):
                P.mm(ps[:], gwb[:, ih, oh * 128:(oh + 1) * 128], gb16[ih][:], ih == 0, ih == 1, [gwb.b, gb16[ih].b], [ps.b])
            sg, o1, o2 = ep["sg%d" % oh], ep["o1%d" % oh], ep["o2%d" % oh]
            P.act(sg[:], ps[:], AF.Sigmoid, [ps.b, gbc.b], [sg.b], bias=gbc[:, oh:oh + 1], scale=1.0)
            P.tt(o1[:], gf[oh][:], sg[:], ALU.mult, [gf[oh].b, sg.b], [o1.b])
            P.tt(o2[:], o1[:], gsT[:, oh, :], ALU.mult, [o1.b, gsT.b], [o2.b])
            P.dma(g.MIX[256 + oh * 128:256 + (oh + 1) * 128, cs], o2[:], [o2.b], [Buf()])


    load(0)
    if NCH > 1:
        load(1)
    xins = [xin, sb("xin1", [128, 16])]
    P.ms(xins[0][:], 0.0, [xins[0].b])

    def in_mm(c, gp):
        uT = uTs[c % 2]
        T = {}
        for gi in (gp, gp + 1):
            pb, psw = pbr.next(), pbr.next()
            P.mm(pb[:], LB[:, gi, :], uT[:, gi, :], True, True, [LB.b, uT.b], [pb.b])
            P.mm(psw[:], LS[:, gi, :], uT[:, gi, :], True, True, [LS.b, uT.b], [psw.b])
            pb_s, pw_s = pbs.next(), pws.next()
            P.cp(pb_s[:], pb[:], [pb.b], [pb_s.b], eng="act")
            P.cp(pw_s[:], psw[:], [psw.b], [pw_s.b], eng="act")
            T[gi] = (pb_s, pw_s, m1.next(), m2.next(), bt.next(), xs.next(), d1.next(), d2.next(), rdr.next())
        return T

    pairs = [(c, gp) for c in range(NCH) for gp in range(0, 16, 2)]
    Tnext = in_mm(0, 0)
    for pi, (c, gp) in enumerate(pairs):
        cs = slice(c * 512, (c + 1) * 512)
        cout = couts[c % 2]
        xcur, xnxt = xins[c % 2], xins[(c + 1) % 2]
        gl = (gp, gp + 1)
        if gp == 4 and c > 0:
            epi_b(c - 1)
            if c + 1 < NCH:
                load(c + 1)
        T = Tnext
        for gi in gl:
            pb, psw, a1, a2, b_, x_, e1, e2, rd_ = T[gi]
            P.tt(a1[:], pb[:], Ct[:, gi, :], ALU.mult, [pb.b, Ct.b], [a1.b])
        for gi in gl:
            pb, psw, a1, a2, b_, x_, e1, e2, rd_ = T[gi]
            P.tt(a2[:], psw[:], St[:, gi, :], ALU.mult, [psw.b, St.b], [a2.b])
        if pi + 1 < len(pairs):
            Tnext = in_mm(*pairs[pi + 1])
        for gi in gl:
            pb, psw, a1, a2, b_, x_, e1, e2, rd_ = T[gi]
            P.tt(b_[:], a1[:], a2[:], ALU.add, [a1.b, a2.b], [b_.b])
        for gi in gl:
            pb, psw, a1, a2, b_, x_, e1, e2, rd_ = T[gi]
            P.op("dve", lambda e, x_=x_, b_=b_, gi=gi, xc=xcur: e.tensor_tensor_scan(x_[:], Rdec[:, gi, :], b_[:], xc[:, gi:gi + 1], ALU.mult, ALU.add),
                 [Rdec.b, b_.b, xcur.b], [x_.b])
        for gi in gl:
            x_ = T[gi][5]
            P.cp(cout[:, gi:gi + 1], x_[:, 511:512], [x_.b], [cout.b], eng="act")
        if c + 1 < NCH:
            for gi in gl:
                P.mm(pc[:, gi:gi + 1], Mrot[:, gi, :], cout[:, gi:gi + 1], True, True, [Mrot.b, cout.b], [pc.b])
        for gi in gl:
            pb, psw, a1, a2, b_, x_, e1, e2, rd_ = T[gi]
            P.tt(e1[:], x_[:], Ct[:, gi, :], ALU.mult, [x_.b, Ct.b], [e1.b])
        for gi in gl:
            pb, psw, a1, a2, b_, x_, e1, e2, rd_ = T[gi]
            P.tt(e2[:], x_[:], St[:, gi, :], ALU.mult, [x_.b, St.b], [e2.b])
        if c + 1 < NCH:
            P.cp(xnxt[:, gp:gp + 2], pc[:, gp:gp + 2], [pc.b], [xnxt.b])
        for gi in gl:
            pb, psw, a1, a2, b_, x_, e1, e2, rd_ = T[gi]
            yp = yps[gi // 8]
            P.mm(yp[:], c1p[:, gi, :], e1[:], gi % 8 == 0, False, [c1p.b, e1.b], [yp.b])
            P.mm(yp[:], c2p[:, gi, :], e2[:], False, gi % 8 == 7, [c2p.b, e2.b], [yp.b])
        if gp == 14:
            epi_a(c)
    epi_b(NCH - 1)

    esp.close()
def _bf(a):
    return np.asarray(a, dtype=np.float32).astype(ml_dtypes.bfloat16)


def host_consts(S):
    pos = np.arange(S, dtype=np.float32)
    rope = np.zeros((2, 2, 128, S), np.float32)
    for bi, half in enumerate((32, 16)):
        inv = np.power(np.float32(10000.0), -np.arange(half, dtype=np.float32) / np.float32(half)).astype(np.float32)
        ang = (pos[None, :] * inv[:, None]).astype(np.float32)
        rows = np.arange(128) % half
        rope[0, bi] = np.cos(ang)[rows]
        rope[1, bi] = np.sin(ang)[rows]
    ident = _bf(np.eye(128))
    r = np.arange(128)
    negtri = _bf(np.where(r[:, None] <= r[None, :], 0.0, NEGB))
    blkind = _bf((np.arange(S)[None, :] // 256) == np.arange(32)[:, None])
    return {"rope": rope, "ident": ident, "negtri": negtri, "blkind": blkind}


def host_layout(inp, S, depth):
    f = lambda a: np.ascontiguousarray(np.asarray(a, dtype=np.float32))
    L = depth
    m = {}
    m["w_in"] = f(np.asarray(inp["w_in"])[:L])
    m["w_out"] = f(np.asarray(inp["w_out"])[:L])
    m["norm_g"] = f(np.asarray(inp["norm_g"])[:L].reshape(L, 8, 128).transpose(0, 2, 1))
    m["final_g"] = f(np.broadcast_to(np.asarray(inp["final_g"]).reshape(1, D), (128, D)))
    m["fox_fb"] = f(np.asarray(inp["fox_fb"])[:L].reshape(L, 4, 1))
    m["mla_w_uq"] = f(np.asarray(inp["mla_w_uq"])[:L])
    m["mla_q_norm"] = f(np.asarray(inp["mla_q_norm"])[:L].reshape(L, 3, 128).transpose(0, 2, 1))
    m["mla_w_ukv"] = f(np.asarray(inp["mla_w_ukv"])[:L])
    m["mla_kv_norm"] = f(np.asarray(inp["mla_kv_norm"])[:L].reshape(L, 128, 1))
    are, aim, ldt = (np.asarray(inp[k])[:L] for k in ("s5_a_re", "s5_a_im", "s5_log_dt"))
    ldtb = np.broadcast_to(ldt[:, :, None], (L, 16, 64))
    tT = lambda a: np.concatenate([a.transpose(0, 2, 1)] * 2, axis=1)
    m["s5_a"] = f(np.stack([tT(are), tT(aim), tT(ldtb)], axis=1))
    rep = lambda a: np.broadcast_to(a.reshape(L, 1, 1024), (L, 16, 1024))
    m["s5_a16"] = f(np.stack([rep(are), rep(aim), rep(ldtb)], axis=1))
    bre, bim = (np.asarray(inp[k])[:L] for k in ("s5_b_re", "s5_b_im"))
    tb = lambda a: a.transpose(0, 3, 1, 2).reshape(L, 16, 1024)
    m["s5_b"] = f(np.stack([tb(bre), tb(bim)], axis=1))
    cre, cim = (np.asarray(inp[k])[:L] for k in ("s5_c_re", "s5_c_im"))
    tc_ = lambda a: a.transpose(0, 3, 1, 2).reshape(L, 64, 256)
    m["s5_c"] = f(np.stack([np.concatenate([tc_(cre), tc_(cim)], axis=1),
                            np.concatenate([tc_(cim), tc_(cre)], axis=1)], axis=1))
    m["s5_d"] = f(np.asarray(inp["s5_d"])[:L].reshape(L, 2, 128).transpose(0, 2, 1))
    m["s5_glu_w"] = f(np.asarray(inp["s5_glu_w"])[:L])
    m["s5_glu_b"] = f(np.asarray(inp["s5_glu_b"])[:L].reshape(L, 2, 128).transpose(0, 2, 1))
    return m


def fused_consts(S):
    NCH = S // 512
    cst = host_consts(S)
    r = np.arange(128)
    negm = np.stack([np.where((r[:, None] + 128 * j) <= np.arange(512)[None, :], 0.0, NEGB) for j in range(4)], axis=1)
    cst["negm"] = _bf(negm)
    cst.pop("negtri", None)
    cst["identf"] = np.eye(128, dtype=np.float32)
    cst["swapj"] = np.roll(np.eye(128, dtype=np.float32), 64, axis=1).copy()
    cm = np.full((NCH, 128, 4, 4, 32), -1e30, np.float32)
    no = np.ones((NCH, 128, 4, 4, 32), np.float32)
    for c in range(NCH):
        cm[c, :, :, 0:2, 0:2 * c] = 0.0
        cm[c, :, :, 2:4, 0:2 * c + 1] = 0.0
        no[c, :, :, 0:2, 2 * c] = 0.0
        no[c, :, :, 2:4, 2 * c + 1] = 0.0
    cst["cmask"] = cm.reshape(NCH, 128, 512)
    cst["notown"] = no.reshape(NCH, 128, 512)
    return cst


def fused_maps(inputs, S, depth):
    m = host_layout(inputs, S, depth)
    m.update(fused_consts(S))
    return m


def kernel(**inputs):
    x = np.ascontiguousarray(np.asarray(inputs["x"], dtype=np.float32))
    B, S, _ = x.shape
    nc = build_fused(S, DEPTH)
    shared = fused_maps(inputs, S, DEPTH)
    in_maps = []
    for b in range(B):
        mp = dict(shared)
        mp["x"] = np.ascontiguousarray(x[b])
        in_maps.append(mp)
    res = run_bass_kernel_spmd(nc, in_maps, core_ids=list(range(B)))
    return np.stack([np.asarray(r["out"], dtype=np.float32) for r in res.results], axis=0)
```

```python
import math
from contextlib import ExitStack
import numpy as np
import ml_dtypes
import concourse.bass as bass
import concourse.mybir as mybir
from concourse.bass_utils import run_bass_kernel_spmd

F32 = mybir.dt.float32
BF = mybir.dt.bfloat16
AF = mybir.ActivationFunctionType
ALU = mybir.AluOpType
AX = mybir.AxisListType

D = 1024
DIN = 3364
DEPTH = 2
EPS = 1e-6
O_FQ, O_FK, O_FV, O_FG, O_FF = 0, 256, 512, 768, 1024
O_SU, O_SG = 1028, 1284
O_MQ, O_MK, O_MV, O_MG = 1540, 1796, 2052, 2308
O_CQ, O_CKV, O_KR, O_LG = 2564, 2948, 3076, 3108
NEGB = -30000.0
KDIM = [70] * 4 + [96] * 4 + [96] * 4
SEM_LIMIT = 20000
DMA_RING = 12
import os
CUT = int(os.environ.get('P1CUT', '99'))
SUB = int(os.environ.get('P1SUB', '99'))


class Buf:
    __slots__ = ("w", "r", "rd")

    def __init__(self):
        self.w = None
        self.r = {}
        self.rd = []


class Op:
    __slots__ = ("eng", "fn", "deps", "marked", "sem", "val", "dma")


class Prog:
    ENG = ("pe", "act", "dve", "pool", "sp")

    def __init__(self, nc):
        self.nc = nc
        self.streams = {e: [] for e in self.ENG}
        self.dmas = []
        self.nsem = 0

    def op(self, eng, fn, r=(), w=(), dma=False):
        o = Op()
        o.eng, o.fn, o.dma, o.marked, o.sem, o.val = eng, fn, dma, dma, None, 0
        deps = set()
        for b in r:
            if b.w is not None:
                deps.add(b.w)
        for b in w:
            if b.w is not None:
                deps.add(b.w)
            deps.update(b.r.values())
            deps.update(b.rd)
        if eng == "pe" and not dma:
            deps = {d for d in deps if not (d.eng == "pe" and not d.dma)}
        o.deps = list(deps)
        for d in o.deps:
            d.marked = True
        for b in r:
            if dma:
                b.rd.append(o)
            else:
                b.r[eng] = o
        for b in w:
            b.w = o
            b.r = {}
            b.rd = []
        self.streams[eng].append(o)
        if dma:
            self.dmas.append(o)
        return o

    def barrier(self):
        lasts = []
        for s in self.streams.values():
            for o in reversed(s):
                if o.fn is not None:
                    lasts.append(o)
                    break
        pend = list(self.dmas)
        self.dmas = []
        for e in self.ENG:
            o = Op()
            o.eng, o.fn, o.dma, o.marked, o.sem, o.val = e, None, False, False, None, 0
            o.deps = list(set(lasts + pend))
            for d in o.deps:
                d.marked = True
            self.streams[e].append(o)

    def mm(self, out, lhsT, rhs, start, stop, r, w):
        return self.op("pe", lambda e: e.matmul(out, lhsT, rhs, start=start, stop=stop), r, w)

    def tr(self, out, in_, ident, r, w):
        return self.op("pe", lambda e: e.transpose(out, in_, ident), r, w)

    def act(self, out, in_, func, r, w, bias=None, scale=None, accum=None):
        kw = {}
        if bias is not None:
            kw["bias"] = bias
        if scale is not None:
            kw["scale"] = scale
        if accum is not None:
            kw["accum_out"] = accum
        return self.op("act", lambda e: e.activation(out, in_, func, **kw), r, w)

    def ts(self, out, in0, s1, s2, op0, op1, r, w, eng="dve"):
        if op1 is None:
            return self.op(eng, lambda e: e.tensor_scalar(out, in0, s1, None, op0), r, w)
        return self.op(eng, lambda e: e.tensor_scalar(out, in0, s1, s2, op0, op1), r, w)

    def tt(self, out, in0, in1, op, r, w, eng="dve"):
        return self.op(eng, lambda e: e.tensor_tensor(out, in0, in1, op), r, w)

    def stt(self, out, in0, sc, in1, op0, op1, r, w):
        return self.op("dve", lambda e: e.scalar_tensor_tensor(out, in0, sc, in1, op0, op1), r, w)

    def cp(self, out, in_, r, w, eng="dve"):
        if eng == "act":
            return self.op("act", lambda e: e.copy(out, in_), r, w)
        return self.op(eng, lambda e: e.tensor_copy(out, in_), r, w)

    def ms(self, ap, val, w, eng="dve"):
        return self.op(eng, lambda e: e.memset(ap, val), (), w)

    def dma(self, out, in_, r, w, eng="sp"):
        return self.op(eng, lambda e: e.dma_start(out=out, in_=in_), r, w, dma=True)

    def finalize(self, final_deps):
        nc = self.nc
        sems = []

        def newsem():
            s = nc.alloc_semaphore("s%d" % len(sems))
            sems.append(s)
            return len(sems) - 1

        for e in self.ENG:
            fo = Op()
            fo.eng, fo.fn, fo.dma, fo.marked, fo.sem, fo.val = e, None, False, False, None, 0
            fo.deps = list(final_deps) if e == "sp" else []
            self.streams[e].append(fo)
        for e in self.ENG:
            cnt, sem = 0, None
            ring, rval, rlast, nd = [], [], [], 0
            for o in self.streams[e]:
                if o.fn is None:
                    continue
                if o.dma:
                    if len(ring) < DMA_RING:
                        ring.append(newsem())
                        rval.append(0)
                        rlast.append(None)
                    slot = nd % DMA_RING
                    nd += 1
                    if rlast[slot] is not None:
                        o.deps.append(rlast[slot])
                    rval[slot] += 16
                    o.sem, o.val = ring[slot], rval[slot]
                    rlast[slot] = o
                elif o.marked:
                    if sem is None or cnt >= SEM_LIMIT:
                        sem, cnt = newsem(), 0
                    cnt += 1
                    o.sem, o.val = sem, cnt
        self.nsem = len(sems)
        streams = self.streams

        def mk(eng):
            def body(e):
                waited = {}
                for o in streams[eng]:
                    for d in o.deps:
                        if d.sem is None:
                            continue
                        if waited.get(d.sem, 0) < d.val:
                            e.wait_ge(sems[d.sem], d.val)
                            waited[d.sem] = d.val
                    if o.fn is None:
                        continue
                    ins = o.fn(e)
                    if o.sem is not None:
                        ins.then_inc(sems[o.sem], 16 if o.dma else 1)
            return body

        with nc.Block() as block:
            block.tensor(mk("pe"))
            block.scalar(mk("act"))
            block.vector(mk("dve"))
            block.gpsimd(mk("pool"))
            block.sync(mk("sp"))


class Tile:
    __slots__ = ("t", "b")

    def __init__(self, t):
        self.t = t
        self.b = Buf()

    def __getitem__(self, k):
        return self.t[k]


class View:
    __slots__ = ("t", "b")

    def __init__(self, ap):
        self.t = ap
        self.b = Buf()

    def __getitem__(self, k):
        return self.t[k]


class Rot:
    def __init__(self, tiles):
        self.tiles = tiles
        self.i = 0

    def next(self):
        t = self.tiles[self.i % len(self.tiles)]
        self.i += 1
        return t


class NS:
    pass


def build_fused(S, depth=DEPTH, dbg=None, stop=None):
    g = NS()
    nc = bass.Bass("TRN2", target_bir_lowering=False)
    g.nc, g.P = nc, Prog(nc)
    P = g.P
    g.S, g.NT, g.NCH = S, S // 128, S // 512
    NCH = g.NCH
    g.es = ExitStack()
    din = lambda name, shape, dt=F32: nc.dram_tensor(name, list(shape), dt, kind="ExternalInput").ap()
    dsc = lambda name, shape, dt: nc.dram_tensor(name, list(shape), dt, kind=("ExternalOutput" if dbg else "Internal")).ap()
    g.sb = lambda name, shape, dt=F32: Tile(g.es.enter_context(nc.sbuf_tensor("sb_" + name, list(shape), dt)))
    g.pbig = g.es.enter_context(nc.psum_tensor("pbig", [128, 3072], F32))
    g.banks = [View(g.pbig[:, i * 512:(i + 1) * 512]) for i in range(6)]
    g.banks += [Tile(g.es.enter_context(nc.psum_tensor("bank%d" % i, [128, 512], F32))) for i in range(6, 8)]
    x_in = din("x", [S, D])
    out_d = nc.dram_tensor("out", [S, D], F32, kind="ExternalOutput").ap()
    W = NS()
    W.w_in = din("w_in", [depth, D, DIN])
    W.w_out = din("w_out", [depth, D, D])
    W.norm_g = din("norm_g", [depth, 128, 8])
    W.final_g = din("final_g", [128, D])
    W.fox_fb = din("fox_fb", [depth, 4, 1])
    W.wuq = din("mla_w_uq", [depth, 384, 384])
    W.qng = din("mla_q_norm", [depth, 128, 3])
    W.wukv = din("mla_w_ukv", [depth, 128, 512])
    W.kvng = din("mla_kv_norm", [depth, 128, 1])
    W.s5a = din("s5_a", [depth, 3, 128, 16])
    W.s5a16 = din("s5_a16", [depth, 3, 16, 1024])
    W.s5b = din("s5_b", [depth, 2, 16, 1024])
    W.s5c = din("s5_c", [depth, 2, 128, 256])
    W.s5d = din("s5_d", [depth, 128, 2])
    W.gluw = din("s5_glu_w", [depth, 256, 256])
    W.glub = din("s5_glu_b", [depth, 128, 2])
    g.rope_d = din("rope", [2, 2, 128, S])
    g.ident_d = din("ident", [128, 128], BF)
    g.negm_d = din("negm", [128, 4, 512], BF)
    g.blkind_d = din("blkind", [32, S], BF)
    g.cmask_d = din("cmask", [NCH, 128, 512])
    g.notown_d = din("notown", [NCH, 128, 512])
    g.idf_d = din("identf", [128, 128])
    g.jsw_d = din("swapj", [128, 128])
    g.QA = dsc("QA", [12, 96, S], BF)
    g.KA = dsc("KA", [12, 96, S], BF)
    g.VS = dsc("VS", [12, 128, g.NT, 65], BF)
    g.GS = dsc("GS", [1024, S], BF)
    g.UT = dsc("UT", [256, S], BF)
    g.MIX = dsc("MIX", [1024, S], BF)
    g.XRES = dsc("XRES", [S, D], F32)
    g.RL = dsc("RL", [12 * NCH, 512], F32)
    g.bQA = [[Buf() for _ in range(NCH)] for _ in range(12)]
    g.bKA = [[Buf() for _ in range(NCH)] for _ in range(12)]
    g.bVS = [[Buf() for _ in range(NCH)] for _ in range(12)]
    g.bGS = [[Buf() for _ in range(NCH)] for _ in range(8)]
    g.bUT = [Buf() for _ in range(NCH)]
    g.ones_bf = g.sb("ones_bf", [128, 512], BF)
    g.ones_f = g.sb("ones_f", [128, 512], F32)
    g.epsc = g.sb("epsc", [128, 1], F32)
    g.onec = g.sb("onec", [128, 1], F32)
    g.ident = g.sb("ident", [128, 128], BF)
    zt = g.sb("zt", [32, 512], BF)
    P.ms(g.ones_bf[:], 1.0, [g.ones_bf.b])
    P.ms(g.ones_f[:], 1.0, [g.ones_f.b])
    P.ms(g.epsc[:], EPS, [g.epsc.b])
    P.ms(g.onec[:], 1.0, [g.onec.b])
    P.ms(zt[:], 0.0, [zt.b])
    P.dma(g.ident[:], g.ident_d, (), [g.ident.b])
    for h in range(4):
        for c in range(NCH):
            cs = slice(c * 512, (c + 1) * 512)
            P.dma(g.QA[h, 67:70, cs], g.ones_bf[0:3, :], [g.ones_bf.b], [g.bQA[h][c]])
            P.dma(g.KA[h, 64:67, cs], g.ones_bf[0:3, :], [g.ones_bf.b], [g.bKA[h][c]])
            P.dma(g.QA[h, 70:96, cs], zt[0:26, :], [zt.b], [g.bQA[h][c]])
            P.dma(g.KA[h, 70:96, cs], zt[0:26, :], [zt.b], [g.bKA[h][c]])
    P.barrier()
    for l in range(depth):
        last = l == depth - 1
        g.uid = l
        g.x_in = x_in if l == 0 else g.XRES
        g.w_in_d, g.normg_d, g.foxfb_d = W.w_in[l], W.norm_g[l], W.fox_fb[l]
        g.wuq_d, g.qng_d, g.wukv_d, g.kvng_d = W.wuq[l], W.qng[l], W.wukv[l], W.kvng[l]
        phase1(g)
        P.barrier()
        if dbg == (l, 1) or stop == (l, 1):
            break
        phase2f(g)
        P.barrier()
        if dbg == (l, 2) or stop == (l, 2):
            break
        phase3f(g, W.s5a[l], W.s5a16[l], W.s5b[l], W.s5c[l], W.s5d[l], W.gluw[l], W.glub[l])
        P.barrier()
        if dbg == (l, 3) or stop == (l, 3):
            break
        phase4f(g, W.w_out[l], W.final_g, g.x_in, (out_d if last else g.XRES), last)
        P.barrier()
    finals = [o for s in P.streams.values() for o in s if o.dma]
    P.finalize(finals)
    g.es.close()
    return nc


def phase2f(g):
    nc, P, banks, S, NT, NCH = g.nc, g.P, g.banks, g.S, g.NT, g.NCH
    UT_ = 3
    with ExitStack() as es:
        sb = lambda name, shape, dt=F32: Tile(es.enter_context(nc.sbuf_tensor("p2_%d_" % g.uid + name, list(shape), dt)))
        negm = sb("negm", [128, 4, 512], BF)
        P.dma(negm[:], g.negm_d, (), [negm.b])
        Qs = [sb("Qt%d" % i, [96, S], BF) for i in range(2)]
        Ks = [sb("Kt%d" % i, [96, S], BF) for i in range(2)]
        Vs = [sb("Vt%d" % i, [128, NT, 65], BF) for i in range(2)]
        Gs = [sb("Gt%d" % i, [64, S], BF) for i in range(2)]
        scb = [View(g.pbig[:, 0:1536]), View(g.pbig[:, 1536:3072])]
        oacc = Rot(banks[6:8])
        pts = Rot([sb("pt%d" % i, [128, 512 * UT_], BF) for i in range(4)])
        rl = Rot([sb("rl%d" % i, [128, 512]) for i in range(5)])
        rlb = Rot([sb("rlb%d" % i, [64, 512]) for i in range(5)])
        osb = Rot([sb("osb%d" % i, [65, 512]) for i in range(5)])
        ot = Rot([sb("ot%d" % i, [64, 512]) for i in range(2)])
        om = Rot([sb("om%d" % i, [64, 512], BF) for i in range(2)])
        grow = lambda h: (64 * h if h < 4 else 256 + 64 * h)

        def load(h):
            Qt, Kt, Vt, Gt = Qs[h % 2], Ks[h % 2], Vs[h % 2], Gs[h % 2]
            moba = 4 <= h < 8
            for c in range(NCH):
                cs = slice(c * 512, (c + 1) * 512)
                P.dma(Qt[:, cs], g.QA[h, :, cs], (), [Qt.b], eng="pool")
                if moba:
                    P.dma(Kt[0:64, cs], g.KA[h, 0:64, cs], (), [Kt.b], eng="pool")
                    P.dma(Kt[64:96, cs], g.blkind_d[:, cs], (), [Kt.b], eng="pool")
                else:
                    P.dma(Kt[:, cs], g.KA[h, :, cs], (), [Kt.b], eng="pool")
            P.dma(Vt[:], g.VS[h], (), [Vt.b], eng="pool")
            P.dma(Gt[:], g.GS[grow(h):grow(h) + 64, :], (), [Gt.b], eng="pool")

        units = []
        for h in range(12):
            for qc in range(NCH):
                nk = 4 * qc + 4
                k0 = 0
                while k0 < nk:
                    n_ = min(UT_, nk - k0)
                    units.append((h, qc, k0, n_, k0 + n_ == nk))
                    k0 += n_
        state = {"oa": None}
        sbuf_of = {}
        pt_of = {}

        def emit_qk(n):
            h, qc, k0, n_, _ = units[n]
            Qt, Kt = Qs[h % 2], Ks[h % 2]
            qs = slice(qc * 512, (qc + 1) * 512)
            sp = scb[n % 2]
            sbuf_of[n] = sp
            for t in range(n_):
                kt = k0 + t
                j = kt - 4 * qc
                ks = slice(kt * 128, (kt + 1) * 128)
                dst = sp[:, t * 512:(t + 1) * 512]
                if j >= 0:
                    P.mm(dst, g.ident[:], negm[:, j, :], True, False, [g.ident.b, negm.b], [sp.b])
                    P.mm(dst, Kt[:, ks], Qt[:, qs], False, True, [Kt.b, Qt.b], [sp.b])
                else:
                    P.mm(dst, Kt[:, ks], Qt[:, qs], True, True, [Kt.b, Qt.b], [sp.b])

        def emit_exp(n):
            n_ = units[n][3]
            sp = sbuf_of.pop(n)
            pt = pts.next()
            pt_of[n] = pt
            P.act(pt[:, 0:512 * n_], sp[:, 0:512 * n_], AF.Exp, [sp.b], [pt.b], scale=0.125)

        pend = []

        def epi_head(h, qc, oa):
            a, rb, o1 = rl.next(), rlb.next(), osb.next()
            P.cp(o1[:], oa[0:65, :], [oa.b], [o1.b])
            P.op("dve", lambda e, a=a, o1=o1: e.reciprocal(a[64:65, :], o1[64:65, :]), [o1.b], [a.b])
            rw = g.RL[h * NCH + qc:h * NCH + qc + 1, :]
            bw = Buf()
            P.dma(rw, a[64:65, :], [a.b], [bw])
            P.dma(rb[:], rw.to_broadcast([64, 512]), [bw], [rb.b])
            pend.append([0, h, qc, rb, o1])

        def epi_tail(h, qc, rb, o1):
            Gt = Gs[h % 2]
            qs = slice(qc * 512, (qc + 1) * 512)
            g0 = grow(h)
            o2, o3 = ot.next(), om.next()
            P.tt(o2[:], o1[0:64, :], rb[:], ALU.mult, [o1.b, rb.b], [o2.b])
            P.tt(o3[:], o2[:], Gt[:, qs], ALU.mult, [o2.b, Gt.b], [o3.b])
            P.dma(g.MIX[g0:g0 + 64, qs], o3[:], [o3.b], [Buf()])

        def flush(min_age):
            while pend and pend[0][0] >= min_age:
                _, h2, qc2, rb, o1 = pend.pop(0)
                epi_tail(h2, qc2, rb, o1)

        def emit_pv(n):
            h, qc, k0, n_, lastu = units[n]
            Vt = Vs[h % 2]
            nk = 4 * qc + 4
            if k0 == 0:
                state["oa"] = oacc.next()
            oa = state["oa"]
            pt = pt_of.pop(n)
            for t in range(n_):
                kt = k0 + t
                P.mm(oa[0:65, :], Vt[:, kt, :], pt[:, t * 512:(t + 1) * 512], kt == 0, kt == nk - 1, [Vt.b, pt.b], [oa.b])
            if lastu:
                epi_head(h, qc, oa)
            for p_ in pend:
                p_[0] += 1
            flush(6)

        load(0)
        N = len(units)
        emit_qk(0)
        for n in range(N):
            h, qc, k0 = units[n][0], units[n][1], units[n][2]
            if h + 1 < 12 and ((NCH > 3 and qc == 3 and k0 == 0) or (NCH <= 3 and qc == 0 and k0 == 0)):
                if NCH <= 3:
                    flush(0)
                load(h + 1)
            emit_exp(n)
            if n + 1 < N:
                emit_qk(n + 1)
            emit_pv(n)
        flush(0)


def phase4f(g, wo_d, fg_d, x_d, dst_d, last):
    nc, P, banks, NCH = g.nc, g.P, g.banks, g.NCH
    with ExitStack() as es:
        sb = lambda name, shape, dt=F32: Tile(es.enter_context(nc.sbuf_tensor("p4_%d_" % g.uid + name, list(shape), dt)))
        wo = sb("wo", [128, 8, 1024], BF)
        wst = [sb("wst%d" % i, [128, 1024]) for i in range(2)]
        mixs = [sb("mix%d" % i, [128, 8, 512], BF) for i in range(2)]
        xts = [sb("xt%d" % i, [128, 4, 1024]) for i in range(2)]
        xns = [sb("xn%d" % i, [128, 4, 1024]) for i in range(2)]
        yo = sb("yo", [128, 4, 1024])
        gB = sb("gB", [128, 1024])
        junk = sb("junk", [128, 1024], BF)
        ss, sd, rstd = sb("ss", [128, 4]), sb("sd", [128, 4]), sb("rstd", [128, 4])
        P.dma(gB[:], fg_d, (), [gB.b])
        for kt in range(8):
            st = wst[kt % 2]
            P.dma(st[:], wo_d[kt * 128:(kt + 1) * 128, :], (), [st.b])
            P.cp(wo[:, kt, :], st[:], [st.b], [wo.b], eng=("dve" if kt % 2 == 0 else "act"))
        pr = Rot(banks[0:4])

        def load(c):
            cs = slice(c * 512, (c + 1) * 512)
            P.dma(mixs[c % 2][:], g.MIX[:, cs].rearrange("(kt p) t -> p kt t", p=128), (), [mixs[c % 2].b])
            P.dma(xts[c % 2][:], x_d[cs, :].rearrange("(j p) d -> p j d", p=128), (), [xts[c % 2].b])

        load(0)
        for c in range(NCH):
            if c + 1 < NCH:
                load(c + 1)
            cs = slice(c * 512, (c + 1) * 512)
            mix, xt, xn = mixs[c % 2], xts[c % 2], xns[c % 2]
            for j in range(4):
                for half in range(2):
                    ps = pr.next()
                    for kt in range(8):
                        P.mm(ps[:], mix[:, kt, j * 128:(j + 1) * 128], wo[:, kt, half * 512:(half + 1) * 512], kt == 0, kt == 7,
                             [mix.b, wo.b], [ps.b])
                    P.tt(xn[:, j, half * 512:(half + 1) * 512], ps[:], xt[:, j, half * 512:(half + 1) * 512], ALU.add,
                         [ps.b, xt.b], [xn.b])
            if not last:
                P.dma(dst_d[cs, :].rearrange("(j p) d -> p j d", p=128), xn[:], [xn.b], [Buf()])
            else:
                for j in range(4):
                    P.act(junk[:], xn[:, j, :], AF.Square, [xn.b], [junk.b, ss.b], accum=ss[:, j:j + 1])
                P.act(sd[:], ss[:], AF.Sqrt, [ss.b, g.epsc.b], [sd.b], bias=g.epsc[:], scale=1.0 / D)
                P.op("dve", lambda e: e.reciprocal(rstd[:], sd[:]), [sd.b], [rstd.b])
                for j in range(4):
                    P.stt(yo[:, j, :], xn[:, j, :], rstd[:, j:j + 1], gB[:], ALU.mult, ALU.mult, [xn.b, rstd.b, gB.b], [yo.b])
                P.dma(dst_d[cs, :].rearrange("(j p) d -> p j d", p=128), yo[:], [yo.b], [Buf()])
def phase1(g):
    nc, P = g.nc, g.P
    NCH = g.NCH
    l = 0
    banks, ident = g.banks, g.ident
    xsrc = g.x_in
    with ExitStack() as es:
        def sb(name, shape, dt=F32):
            return Tile(es.enter_context(nc.sbuf_tensor("p1_%d_" % g.uid + name, list(shape), dt)))

        wA = sb("wA", [128, 8, DIN], BF)
        wR = sb("wR", [128, 8, 576], BF)
        wuq = sb("wuq", [128, 3, 384], BF)
        wuqR = sb("wuqR", [128, 3, 384], BF)
        wk = sb("wk", [128, 256], BF)
        wv = sb("wv", [128, 256], BF)
        gcol = sb("gcol", [128, 8])
        qng = sb("qng", [128, 3])
        kvng = sb("kvng", [128, 1])
        negfb = sb("negfb", [4, 1])
        P.dma(gcol[:], g.normg_d, (), [gcol.b])
        P.dma(qng[:], g.qng_d, (), [qng.b])
        P.dma(kvng[:], g.kvng_d, (), [kvng.b])
        P.dma(negfb[:], g.foxfb_d, (), [negfb.b])
        P.ts(negfb[:], negfb[:], -1.0, None, ALU.mult, None, [negfb.b], [negfb.b])
        with ExitStack() as es2:
            wst = [Tile(es2.enter_context(nc.sbuf_tensor("p1_%d_wst%d" % (g.uid, i), [128, DIN], F32))) for i in range(6)]
            for kt in range(6):
                P.dma(wst[kt][:], g.w_in_d[kt * 128:(kt + 1) * 128, :], (), [wst[kt].b], eng=("sp" if kt % 2 == 0 else "act"))
            for kt in range(8):
                st = wst[kt % 6]
                if kt >= 6:
                    P.dma(st[:], g.w_in_d[kt * 128:(kt + 1) * 128, :], (), [st.b], eng=("sp" if kt % 2 == 0 else "act"))
                if kt % 2 == 0:
                    P.ts(wA[:, kt, :], st[:], gcol[:, kt:kt + 1], None, ALU.mult, None, [st.b, gcol.b], [wA.b])
                else:
                    P.act(wA[:, kt, :], st[:], AF.Copy, [st.b, gcol.b], [wA.b], scale=gcol[:, kt:kt + 1])
            for i in range(3):
                st = wst[i % 2]
                P.dma(st[:, 0:384], g.wuq_d[i * 128:(i + 1) * 128, :], (), [st.b])
                P.ts(wuq[:, i, :], st[:, 0:384], qng[:, i:i + 1], 0.816496580927726, ALU.mult, ALU.mult, [st.b, qng.b], [wuq.b])
            st = wst[1]
            P.dma(st[:, 0:512], g.wukv_d, (), [st.b])
            sv = st[:, 0:512].rearrange("p (h e) -> p h e", h=4)
            P.ts(wk[:].rearrange("p (h e) -> p h e", h=4), sv[:, :, 0:64], kvng[:, 0:1], None, ALU.mult, None,
                 [st.b, kvng.b], [wk.b])
            P.ts(wv[:].rearrange("p (h e) -> p h e", h=4), sv[:, :, 64:128], kvng[:, 0:1], None, ALU.mult, None,
                 [st.b, kvng.b], [wv.b])
            P.barrier()
        P.ms(wR[:], 0.0, [wR.b])
        P.ms(wuqR[:], 0.0, [wuqR.b])
        for kt in range(8):
            for (src_o, dst_o) in ((O_MQ, 0), (O_MK, 256)):
                s4 = wA[:, kt, src_o:src_o + 256].rearrange("p (h t j) -> p h t j", h=4, t=2)
                d4 = wR[:, kt, dst_o:dst_o + 256].rearrange("p (h t j) -> p h t j", h=4, t=2)
                P.ts(d4[:, :, 0, :], s4[:, :, 1, :], -1.0, None, ALU.mult, None, [wA.b], [wR.b], eng="dve")
                P.cp(d4[:, :, 1, :], s4[:, :, 0, :], [wA.b], [wR.b], eng="act")
            P.ts(wR[:, kt, 512:528], wA[:, kt, O_KR + 16:O_KR + 32], -1.0, None, ALU.mult, None, [wA.b], [wR.b], eng="dve")
            P.cp(wR[:, kt, 528:544], wA[:, kt, O_KR:O_KR + 16], [wA.b], [wR.b], eng="act")
        for i in range(3):
            s3 = wuq[:, i, :].rearrange("p (h e) -> p h e", h=4)
            d3 = wuqR[:, i, :].rearrange("p (h e) -> p h e", h=4)
            P.ts(d3[:, :, 64:80], s3[:, :, 80:96], -1.0, None, ALU.mult, None, [wuq.b], [wuqR.b], eng="dve")
            P.cp(d3[:, :, 80:96], s3[:, :, 64:80], [wuq.b], [wuqR.b], eng="act")

        xt = [sb("xt%d" % i, [128, 4, 1024]) for i in range(2)]
        rt = [sb("rt%d" % i, [128, 4, 512]) for i in range(1)]
        xs = sb("xs", [128, 4, 1024], BF)
        junk = sb("junk", [128, 1024], BF)
        hT = [sb("hT%d" % i, [128, 8, 512], BF) for i in range(2)]
        ss, sd, rstd = sb("ss", [128, 4]), sb("sd", [128, 4]), sb("rstd", [128, 4])
        stA = Rot([sb("stA%d" % i, [128, 512], BF) for i in range(6)])
        tf = Rot([sb("tf%d" % i, [128, 512]) for i in range(4)])
        vst = Rot([sb("vst%d" % i, [128, 4, 65], BF) for i in range(3)])
        for t in vst.tiles:
            P.ms(t[:], 1.0, [t.b])
        cqf = sb("cqf", [128, 3, 512])
        cqn = sb("cqn", [128, 3, 512], BF)
        sqb = Rot([sb("sqb%d" % i, [128, 512], BF) for i in range(2)])
        ckn = sb("ckn", [128, 512], BF)
        rq = sb("rq", [128, 512])
        rq2 = sb("rq2", [128, 512])
        ckf = sb("ckf", [128, 512])
        nlf = sb("nlf", [4, 512])
        ncum = [sb("ncum%d" % i, [4, 512]) for i in range(2)]
        s8, r1, r2 = sb("s8", [4, 512]), sb("r1", [4, 512]), sb("r2", [4, 512])
        e1 = r1
        kst = Rot([sb("kst%d" % i, [4, 3, 512], BF) for i in range(1)])
        qst = Rot([sb("qst%d" % i, [4, 3, 512], BF) for i in range(1)])
        kmf = [sb("kmf%d" % i, [128, 32]) for i in range(2)]
        kmz = [sb("kmz%d" % i, [128, 32], BF) for i in range(4)]
        for t in kmf:
            P.ms(t[:], 0.0, [t.b])
        for i in range(4):
            P.ms(kmz[i][:], 0.0, [kmz[i].b])
        cmask = sb("cmask", [128, 512])
        notown = sb("notown", [128, 512])
        ncin = sb("ncin", [4, 1])
        P.ms(ncin[:], 0.0, [ncin.b])
        mqs = [sb("mqs%d" % i, [128, 512], BF) for i in range(2)]
        gm = sb("gm", [128, 512])
        t8 = sb("t8", [128, 128])
        thr = sb("thr", [128, 16])
        mbf = sb("mbf", [128, 512])
        mb = sb("mb", [128, 512], BF)
        mbT = sb("mbT", [32, 2048], BF)
        pr = Rot(banks[0:6])
        ptb = [banks[6], banks[7]]

        def load(c):
            xc = xt[c % 2]
            rd = []
            P.dma(xc[:], xsrc[c * 512:(c + 1) * 512, :].rearrange("(j p) d -> p j d", p=128), rd, [xc.b])

        def pre_norm(c):
            xc = xt[c % 2]
            for j in range(4):
                P.act(junk[:], xc[:, j, :], AF.Square, [xc.b], [junk.b, ss.b], accum=ss[:, j:j + 1])
            P.act(sd[:], ss[:], AF.Sqrt, [ss.b, g.epsc.b], [sd.b], bias=g.epsc[:], scale=1.0 / D)
            P.op("dve", lambda e: e.reciprocal(rstd[:], sd[:]), [sd.b], [rstd.b])
            for j in range(4):
                P.ts(xs[:, j, :], xc[:, j, :], rstd[:, j:j + 1], None, ALU.mult, None, [xc.b, rstd.b], [xs.b])

        def pre_tr(c):
            h = hT[c % 2]
            for kt in range(8):
                pt = ptb[kt % 2]
                ptv = pt[:].bitcast(BF)
                for j in range(4):
                    P.tr(ptv[:, j * 128:(j + 1) * 128], xs[:, j, kt * 128:(kt + 1) * 128], ident[:],
                         [xs.b, ident.b], [pt.b])
                P.cp(h[:, kt, :], ptv[:, 0:512], [pt.b], [h.b], eng=("dve" if kt % 2 == 0 else "act"))

        load(0)
        pre_norm(0)
        pre_tr(0)
        for c in range(NCH):
            if c + 1 < NCH:
                load(c + 1)
            cs = slice(c * 512, (c + 1) * 512)
            xc, h, rc = xt[c % 2], hT[c % 2], rt[0]
            P.dma(cmask[:], g.cmask_d[c], (), [cmask.b])
            P.dma(notown[:], g.notown_d[c], (), [notown.b])
            P.dma(rc[:], g.rope_d[:, :, :, c * 512:(c + 1) * 512].rearrange("a b p t -> p (a b) t"), (), [rc.b])
            def proj(src, lo, M):
                ps = pr.next()
                for kt in range(8):
                    P.mm(ps[0:M, :], src[:, kt, lo:lo + M], h[:, kt, :], kt == 0, kt == 7, [src.b, h.b], [ps.b])
                return ps

            def split_store(st, dst, bufs, h0):
                P.dma(dst[h0, 0:64, cs], st[0:64, :], [st.b], [bufs[h0][c]])
                P.dma(dst[h0 + 1, 0:64, cs], st[64:128, :], [st.b], [bufs[h0 + 1][c]])

            def gates(lo_, hi_):
                for gi, off in list(enumerate((O_FG, O_FG + 128, O_SG, O_SG + 128, O_MG, O_MG + 128, O_LG, O_LG + 128)))[lo_:hi_]:
                    ps = proj(wA, off, 128)
                    st = stA.next()
                    P.act(st[:], ps[:], AF.Silu, [ps.b], [st.b])
                    P.dma(g.GS[gi * 128:(gi + 1) * 128, cs], st[:], [st.b], [g.bGS[gi][c]])

            def values(j):
                for (vi, off) in ((0, O_FV), (1, O_MV), (2, None)):
                    ps = pr.next()
                    if off is not None:
                        for kt in range(8):
                            P.mm(ps[:, 0:256], h[:, kt, j * 128:(j + 1) * 128], wA[:, kt, off:off + 256], kt == 0, kt == 7,
                                 [wA.b, h.b], [ps.b])
                    else:
                        P.mm(ps[:, 0:256], ckn[:, j * 128:(j + 1) * 128], wv[:], True, True, [ckn.b, wv.b], [ps.b])
                    vt = vst.next()
                    P.cp(vt[:, :, 0:64], ps[:, 0:256].rearrange("p (h e) -> p h e", h=4), [ps.b], [vt.b], eng="act")
                    P.dma(g.VS[4 * vi:4 * vi + 4, :, c * 4 + j, :].rearrange("h p e -> p h e"), vt[:],
                          [vt.b], [g.bVS[4 * vi + k][c] for k in range(4)])

            for i in range(3):
                ps = proj(wA, O_CQ + 128 * i, 128)
                P.cp(cqf[:, i, :], ps[:], [ps.b], [cqf.b], eng="act")
            ssb = pr.next()
            for i in range(3):
                sq = sqb.next()
                P.act(sq[:], cqf[:, i, :], AF.Square, [cqf.b], [sq.b])
                P.mm(ssb[:], g.ones_bf[:, 0:128], sq[:], i == 0, i == 2, [g.ones_bf.b, sq.b], [ssb.b])
            P.act(rq[:], ssb[:], AF.Ln, [ssb.b, g.epsc.b], [rq.b], bias=g.epsc[:], scale=1.0 / 384)
            P.act(rq[:], rq[:], AF.Exp, [rq.b], [rq.b], scale=-0.5)
            ps = proj(wA, O_CKV, 128)
            P.cp(ckf[:], ps[:], [ps.b], [ckf.b], eng="act")
            sq = sqb.next()
            P.act(sq[:], ckf[:], AF.Square, [ckf.b], [sq.b])
            ssb = pr.next()
            P.mm(ssb[:], g.ones_bf[:, 0:128], sq[:], True, True, [g.ones_bf.b, sq.b], [ssb.b])
            P.act(rq2[:], ssb[:], AF.Ln, [ssb.b, g.epsc.b], [rq2.b], bias=g.epsc[:], scale=1.0 / 128)
            P.act(rq2[:], rq2[:], AF.Exp, [rq2.b], [rq2.b], scale=-0.5)
            for pair in range(2):
                for (off, dst, bufs) in ((O_FQ, g.QA, g.bQA), (O_FK, g.KA, g.bKA)):
                    ps = proj(wA, off + 128 * pair, 128)
                    st = stA.next()
                    P.cp(st[:], ps[:], [ps.b], [st.b], eng="act")
                    split_store(st, dst, bufs, 2 * pair)
            gates(0, 4)
            for i in range(3):
                P.tt(cqn[:, i, :], cqf[:, i, :], rq[:], ALU.mult, [cqf.b, rq.b], [cqn.b])
            P.tt(ckn[:], ckf[:], rq2[:], ALU.mult, [ckf.b, rq2.b], [ckn.b])
            if c + 1 < NCH:
                pre_norm(c + 1)
            ps = proj(wA, O_FF, 4)
            P.act(e1[:], ps[0:4, :], AF.Exp, [ps.b, negfb.b], [e1.b], bias=negfb[:], scale=-1.0)
            P.act(nlf[:], e1[:], AF.Ln, [e1.b, g.onec.b], [nlf.b], bias=g.onec[0:4, :], scale=1.0)
            nc_cur, nc_prev = ncum[c % 2], ncum[(c + 1) % 2]
            if c == 0:
                P.op("dve", lambda e, o=nc_cur: e.tensor_tensor_scan(o[:], g.ones_f[0:4, :], nlf[:], ncin[:, 0:1], ALU.mult, ALU.add),
                     [g.ones_f.b, nlf.b, ncin.b], [nc_cur.b])
            else:
                P.op("dve", lambda e, o=nc_cur, p_=nc_prev: e.tensor_tensor_scan(o[:], g.ones_f[0:4, :], nlf[:], p_[:, 511:512], ALU.mult, ALU.add),
                     [g.ones_f.b, nlf.b, nc_prev.b], [nc_cur.b])
            ks, qs = kst.next(), qst.next()
            P.ts(s8[:], nc_cur[:], 8.0, None, ALU.mult, None, [nc_cur.b], [s8.b])
            P.cp(ks[:, 0, :], s8[:], [s8.b], [ks.b])
            P.tt(r1[:], s8[:], ks[:, 0, :], ALU.subtract, [s8.b, ks.b], [r1.b])
            P.cp(ks[:, 1, :], r1[:], [r1.b], [ks.b])
            P.tt(r2[:], r1[:], ks[:, 1, :], ALU.subtract, [r1.b, ks.b], [r2.b])
            P.cp(ks[:, 2, :], r2[:], [r2.b], [ks.b])
            P.ts(qs[:], ks[:], -1.0, None, ALU.mult, None, [ks.b], [qs.b])
            for hh in range(4):
                P.dma(g.QA[hh:hh + 1, 64:67, cs], qs[hh:hh + 1, :, :], [qs.b], [g.bQA[hh][c]])
                P.dma(g.KA[hh:hh + 1, 67:70, cs], ks[hh:hh + 1, :, :], [ks.b], [g.bKA[hh][c]])
            for (off, roff, isq) in ((O_MK, 256, False), (O_MQ, 0, True)):
                for pair in range(2):
                    ps1 = proj(wA, off + 128 * pair, 128)
                    ps2 = proj(wR, roff + 128 * pair, 128)
                    t1, t2 = tf.next(), tf.next()
                    P.tt(t1[:], ps1[:], rc[:, 0, :], ALU.mult, [ps1.b, rc.b], [t1.b])
                    P.tt(t2[:], ps2[:], rc[:, 2, :], ALU.mult, [ps2.b, rc.b], [t2.b])
                    if isq:
                        st = mqs[pair]
                        P.tt(st[:], t1[:], t2[:], ALU.add, [t1.b, t2.b], [st.b])
                        split_store(st, g.QA, g.bQA, 4 + 2 * pair)
                    else:
                        st = stA.next()
                        P.tt(st[:], t1[:], t2[:], ALU.add, [t1.b, t2.b], [st.b])
                        split_store(st, g.KA, g.bKA, 4 + 2 * pair)
                        P.op("dve", lambda e, o=kmf[pair], s_=st: e.tensor_reduce(
                            o[:, 0:2], s_[:].rearrange("p (b t) -> p b t", b=2), AX.X, ALU.add),
                            [st.b], [kmf[pair].b])
                        for hx in range(2):
                            kz = kmz[2 * pair + hx]
                            P.cp(kz[64 * hx:64 * hx + 64, 2 * c:2 * c + 2], kmf[pair][64 * hx:64 * hx + 64, 0:2], [kmf[pair].b], [kz.b])
            values(0)
            for i in range(2):
                ps = proj(wA, O_SU + 128 * i, 128)
                st = stA.next()
                P.cp(st[:], ps[:], [ps.b], [st.b], eng="act")
                P.dma(g.UT[i * 128:(i + 1) * 128, cs], st[:], [st.b], [g.bUT[c]])
            gps = pr.next()
            for hh in range(4):
                pair, base = hh // 2, 64 * (hh % 2)
                for j in range(4):
                    i16 = hh * 4 + j
                    P.mm(gps[:, i16 * 32:(i16 + 1) * 32], mqs[pair][:, j * 128:(j + 1) * 128],
                         kmz[hh][:, 0:32], True, True, [mqs[pair].b, kmz[hh].b], [gps.b])
            P.tt(gm[:], gps[:], cmask[:], ALU.add, [gps.b, cmask.b], [gm.b])
            for i16 in range(16):
                P.op("dve", lambda e, i=i16: e.max(t8[:, i * 8:(i + 1) * 8], gm[:, i * 32:(i + 1) * 32]), [gm.b], [t8.b])
            P.ts(thr[:], t8[:].rearrange("p (i e) -> p i e", e=8)[:, :, 2], -1e29, None, ALU.max, None, [t8.b], [thr.b])
            for i16 in range(16):
                P.ts(mbf[:, i16 * 32:(i16 + 1) * 32], gm[:, i16 * 32:(i16 + 1) * 32], thr[:, i16:i16 + 1], -NEGB,
                     ALU.is_ge, ALU.mult, [gm.b, thr.b], [mbf.b])
            P.ts(mb[:], mbf[:], NEGB, None, ALU.add, None, [mbf.b], [mb.b])
            P.tt(mb[:], mb[:], notown[:], ALU.mult, [mb.b, notown.b], [mb.b])
            values(1)
            gates(4, 8)
            for hh in range(4):
                psq, psr = pr.next(), pr.next()
                for i in range(3):
                    P.mm(psq[0:96, :], wuq[:, i, 96 * hh:96 * hh + 96], cqn[:, i, :], i == 0, i == 2, [wuq.b, cqn.b], [psq.b])
                for i in range(3):
                    P.mm(psr[0:96, :], wuqR[:, i, 96 * hh:96 * hh + 96], cqn[:, i, :], i == 0, i == 2, [wuqR.b, cqn.b], [psr.b])
                st = stA.next()
                t1, t2 = tf.next(), tf.next()
                P.cp(st[0:64, :], psq[0:64, :], [psq.b], [st.b], eng="act")
                P.tt(t1[64:96, :], psq[64:96, :], rc[64:96, 1, :], ALU.mult, [psq.b, rc.b], [t1.b])
                P.tt(t2[64:96, :], psr[64:96, :], rc[64:96, 3, :], ALU.mult, [psr.b, rc.b], [t2.b])
                P.tt(st[64:96, :], t1[64:96, :], t2[64:96, :], ALU.add, [t1.b, t2.b], [st.b])
                P.dma(g.QA[8 + hh, 0:96, cs], st[0:96, :], [st.b], [g.bQA[8 + hh][c]])
            for pair in range(2):
                ps = pr.next()
                P.mm(ps[:], wk[:, 128 * pair:128 * pair + 128], ckn[:], True, True, [wk.b, ckn.b], [ps.b])
                st = stA.next()
                P.cp(st[:], ps[:], [ps.b], [st.b], eng="act")
                split_store(st, g.KA, g.bKA, 8 + 2 * pair)
            values(2)
            ps1 = proj(wA, O_KR, 32)
            ps2 = proj(wR, 512, 32)
            t1, t2 = tf.next(), tf.next()
            st = stA.next()
            P.tt(t1[0:32, :], ps1[0:32, :], rc[0:32, 1, :], ALU.mult, [ps1.b, rc.b], [t1.b])
            P.tt(t2[0:32, :], ps2[0:32, :], rc[0:32, 3, :], ALU.mult, [ps2.b, rc.b], [t2.b])
            P.tt(st[0:32, :], t1[0:32, :], t2[0:32, :], ALU.add, [t1.b, t2.b], [st.b])
            for hh in range(4):
                P.dma(g.KA[8 + hh, 64:96, cs], st[0:32, :], [st.b], [g.bKA[8 + hh][c]])
            for half2 in range(2):
                pt = ptb[half2]
                ptv = pt[:].bitcast(BF)
                for k8 in range(8):
                    i16 = half2 * 8 + k8
                    P.tr(ptv[0:32, k8 * 128:(k8 + 1) * 128], mb[:, i16 * 32:(i16 + 1) * 32], ident[:], [mb.b, ident.b], [pt.b])
                P.cp(mbT[:, half2 * 1024:(half2 + 1) * 1024], ptv[0:32, :], [pt.b], [mbT.b], eng="act")
            for hh in range(4):
                P.dma(g.QA[4 + hh, 64:96, cs], mbT[:, hh * 512:(hh + 1) * 512], [mbT.b], [g.bQA[4 + hh][c]])
            if c + 1 < NCH:
                pre_tr(c + 1)
            values(3)


def phase3f(g, a_d, a16_d, b_d, c_d, d_d, gw_d, gb_d):
    nc, P, banks, NCH = g.nc, g.P, g.banks, g.NCH
    PI = math.pi
    esp = ExitStack()
    sb = lambda name, shape, dt=F32: Tile(esp.enter_context(nc.sbuf_tensor("p3_%d_" % g.uid + name, list(shape), dt)))
    g_sb_saved = sb

    def ld(name, shape, src, dt=F32):
        t = sb(name, shape, dt)
        P.dma(t[:], src, (), [t.b])
        return t

    A = ld("A", [128, 3, 16], a_d.rearrange("a p g -> p a g"))
    dcol = ld("dcol", [128, 2], d_d)
    gbc = ld("gbc", [128, 2], gb_d)
    gwf = ld("gwf", [128, 2, 256], gw_d.rearrange("(i p) o -> p i o", p=128))
    idf = ld("idf", [128, 128], g.idf_d)
    jsw = ld("jsw", [128, 128], g.jsw_d)
    gwb = sb("gwb", [128, 2, 256], BF)
    P.cp(gwb[:], gwf[:], [gwf.b], [gwb.b])

    def lamparts(pre, src, np_, nf):
        mk = lambda n: sb(pre + n, [np_, nf])
        dt, th, r, k, x1, sn, cs = mk("dt"), mk("th"), mk("r"), mk("k"), mk("x1"), mk("sn"), mk("cs")
        are, aim, ldt = src[:, 0, :], src[:, 1, :], src[:, 2, :]
        P.act(dt[:], ldt, AF.Exp, [src.b], [dt.b])
        P.tt(th[:], aim, dt[:], ALU.mult, [src.b, dt.b], [th.b])
        P.tt(r[:], are, dt[:], ALU.mult, [src.b, dt.b], [r.b])
        P.act(r[:], r[:], AF.Exp, [r.b], [r.b])
        for (dst, shift) in ((sn, 0.0), (cs, PI / 2)):
            P.ts(x1[:], th[:], shift, None, ALU.add, None, [th.b], [x1.b])
            P.ts(k[:], x1[:], 1.0 / (2 * PI), 12582912.0, ALU.mult, ALU.add, [x1.b], [k.b])
            P.ts(k[:], k[:], -12582912.0, None, ALU.add, None, [k.b], [k.b])
            P.stt(x1[:], k[:], -6.28125, x1[:], ALU.mult, ALU.add, [k.b, x1.b], [x1.b])
            P.stt(x1[:], k[:], -0.0019353071795864769, x1[:], ALU.mult, ALU.add, [k.b, x1.b], [x1.b])
            P.ts(x1[:], x1[:], 3.1415925, -3.1415925, ALU.min, ALU.max, [x1.b], [x1.b])
            P.act(dst[:], x1[:], AF.Sin, [x1.b], [dst.b])
        return r, cs, sn

    Ct, St = sb("Ct", [128, 16, 512], BF), sb("St", [128, 16, 512], BF)
    Mrot = sb("Mrot", [128, 16, 128])
    LB, LS = sb("LB", [16, 16, 128], BF), sb("LS", [16, 16, 128], BF)
    c1p, c2p = sb("c1p", [128, 16, 128], BF), sb("c2p", [128, 16, 128], BF)
    rkeep = sb("rkeep", [128, 16])
    Rdec = sb("Rdec", [128, 16, 512])
    xin = sb("xin", [128, 16])
    couts = [sb("cout%d" % i, [128, 16]) for i in range(2)]
    gsb = g_sb_saved
    es1 = ExitStack()
    sb = lambda name, shape, dt=F32: Tile(es1.enter_context(nc.sbuf_tensor("s1_%d_" % g.uid + name, list(shape), dt)))
    r, cm, sm = lamparts("a_", A, 128, 16)
    P.cp(rkeep[:], r[:], [r.b], [rkeep.b])
    for gi in range(16):
        P.act(Rdec[:, gi, :], g.ones_f[:], AF.Copy, [g.ones_f.b, r.b], [Rdec.b], scale=r[:, gi:gi + 1])
    tA, tB = sb("tA", [128, 16, 256]), sb("tB", [128, 16, 256])
    Ct32, St32 = sb("Ct32", [128, 16, 512]), sb("St32", [128, 16, 512])
    P.ms(Ct32[:, :, 0:1], 1.0, [Ct32.b])
    P.ms(St32[:, :, 0:1], 0.0, [St32.b])
    c3 = lambda t: t[:].rearrange("p (g o) -> p g o", o=1)
    P.cp(Ct32[:, :, 1:2], c3(cm), [cm.b], [Ct32.b])
    P.cp(St32[:, :, 1:2], c3(sm), [sm.b], [St32.b])
    m = 1
    q1, q2 = sb("q1", [128, 16]), sb("q2", [128, 16])
    cur = (cm, sm)
    nxt = (sb("cm2", [128, 16]), sb("sm2", [128, 16]))
    while m <= 256:
        c_, s_ = cur
        if m > 1 or True:
            pass
        if m >= 2 or m == 1:
            pass
        if m > 1:
            pass
        cb = c3(c_).to_broadcast([128, 16, m])
        sbb = c3(s_).to_broadcast([128, 16, m])
        if m >= 2:
            P.tt(tA[:, :, 0:m], St32[:, :, 0:m], sbb, ALU.mult, [St32.b, s_.b], [tA.b])
            P.tt(tB[:, :, 0:m], Ct32[:, :, 0:m], cb, ALU.mult, [Ct32.b, c_.b], [tB.b])
            P.tt(Ct32[:, :, m:2 * m], tB[:, :, 0:m], tA[:, :, 0:m], ALU.subtract, [tA.b, tB.b], [Ct32.b])
            P.tt(tA[:, :, 0:m], St32[:, :, 0:m], cb, ALU.mult, [St32.b, c_.b], [tA.b])
            P.tt(tB[:, :, 0:m], Ct32[:, :, 0:m], sbb, ALU.mult, [Ct32.b, s_.b], [tB.b])
            P.tt(St32[:, :, m:2 * m], tB[:, :, 0:m], tA[:, :, 0:m], ALU.add, [tA.b, tB.b], [St32.b])
        c2, s2 = nxt
        P.tt(q1[:], c_[:], c_[:], ALU.mult, [c_.b], [q1.b])
        P.tt(q2[:], s_[:], s_[:], ALU.mult, [s_.b], [q2.b])
        P.tt(c2[:], q1[:], q2[:], ALU.subtract, [q1.b, q2.b], [c2.b])
        P.tt(q1[:], c_[:], s_[:], ALU.mult, [c_.b, s_.b], [q1.b])
        P.ts(s2[:], q1[:], 2.0, None, ALU.mult, None, [q1.b], [s2.b])
        cur, nxt = (c2, s2), (c_, s_)
        m *= 2
    c512, s512 = cur
    P.cp(Ct[:], Ct32[:], [Ct32.b], [Ct.b])
    P.cp(St[:], St32[:], [St32.b], [St.b], eng="act")
    ssg = sb("ssg", [128, 16])
    P.cp(ssg[0:64, :], s512[0:64, :], [s512.b], [ssg.b])
    P.ts(ssg[64:128, :], s512[64:128, :], -1.0, None, ALU.mult, None, [s512.b], [ssg.b])
    for gi in range(16):
        P.ts(Mrot[:, gi, :], idf[:], c512[:, gi:gi + 1], None, ALU.mult, None, [idf.b, c512.b], [Mrot.b])
        P.stt(Mrot[:, gi, :], jsw[:], ssg[:, gi:gi + 1], Mrot[:, gi, :], ALU.mult, ALU.add, [jsw.b, ssg.b, Mrot.b], [Mrot.b])
    P.barrier()
    es1.close()
    es2 = ExitStack()
    sb = lambda name, shape, dt=F32: Tile(es2.enter_context(nc.sbuf_tensor("s2_%d_" % g.uid + name, list(shape), dt)))
    A16 = ld("A16", [16, 3, 1024], a16_d.rearrange("a p g -> p a g"))
    Bw = ld("Bw", [16, 2, 1024], b_d.rearrange("a p g -> p a g"))
    Cw = ld("Cw", [128, 2, 256], c_d.rearrange("a p g -> p a g"))
    r16, c16, s16 = lamparts("b_", A16, 16, 1024)
    mk16 = lambda n: sb("b_" + n, [16, 1024])
    lre, lim, den, cr, ci, w1, w2 = mk16("lre"), mk16("lim"), mk16("den"), mk16("cr"), mk16("ci"), mk16("w1"), mk16("w2")
    are, aim = A16[:, 0, :], A16[:, 1, :]
    P.tt(lre[:], r16[:], c16[:], ALU.mult, [r16.b, c16.b], [lre.b])
    P.tt(lim[:], r16[:], s16[:], ALU.mult, [r16.b, s16.b], [lim.b])
    P.ts(lre[:], lre[:], -1.0, None, ALU.add, None, [lre.b], [lre.b])
    P.tt(w1[:], are, are, ALU.mult, [A16.b], [w1.b])
    P.tt(w2[:], aim, aim, ALU.mult, [A16.b], [w2.b])
    P.tt(den[:], w1[:], w2[:], ALU.add, [w1.b, w2.b], [den.b])
    P.op("dve", lambda e: e.reciprocal(den[:], den[:]), [den.b], [den.b])
    P.tt(w1[:], lre[:], are, ALU.mult, [lre.b, A16.b], [w1.b])
    P.tt(w2[:], lim[:], aim, ALU.mult, [lim.b, A16.b], [w2.b])
    P.tt(cr[:], w1[:], w2[:], ALU.add, [w1.b, w2.b], [cr.b])
    P.tt(cr[:], cr[:], den[:], ALU.mult, [cr.b, den.b], [cr.b])
    P.tt(w1[:], lim[:], are, ALU.mult, [lim.b, A16.b], [w1.b])
    P.tt(w2[:], lre[:], aim, ALU.mult, [lre.b, A16.b], [w2.b])
    P.tt(ci[:], w1[:], w2[:], ALU.subtract, [w1.b, w2.b], [ci.b])
    P.tt(ci[:], ci[:], den[:], ALU.mult, [ci.b, den.b], [ci.b])
    bre, bim = Bw[:, 0, :], Bw[:, 1, :]
    Bre, Bim = mk16("Bre"), mk16("Bim")
    P.tt(w1[:], cr[:], bre, ALU.mult, [cr.b, Bw.b], [w1.b])
    P.tt(w2[:], ci[:], bim, ALU.mult, [ci.b, Bw.b], [w2.b])
    P.tt(Bre[:], w1[:], w2[:], ALU.subtract, [w1.b, w2.b], [Bre.b])
    P.tt(w1[:], cr[:], bim, ALU.mult, [cr.b, Bw.b], [w1.b])
    P.tt(w2[:], ci[:], bre, ALU.mult, [ci.b, Bw.b], [w2.b])
    P.tt(Bim[:], w1[:], w2[:], ALU.add, [w1.b, w2.b], [Bim.b])
    v3 = lambda t: t[:].rearrange("c (g p) -> c g p", g=16)
    P.cp(LB[:, :, 0:64], v3(Bre), [Bre.b], [LB.b])
    P.cp(LB[:, :, 64:128], v3(Bim), [Bim.b], [LB.b])
    P.cp(LS[:, :, 0:64], v3(Bim), [Bim.b], [LS.b])
    P.ts(LS[:, :, 64:128], v3(Bre), -1.0, None, ALU.mult, None, [Bre.b], [LS.b])
    c1f, c2f = sb("c1f", [128, 256]), sb("c2f", [128, 256])
    P.cp(c1f[0:64, :], Cw[0:64, 0, :], [Cw.b], [c1f.b])
    P.ts(c1f[64:128, :], Cw[64:128, 0, :], -1.0, None, ALU.mult, None, [Cw.b], [c1f.b])
    P.ts(c2f[:], Cw[:, 1, :], -1.0, None, ALU.mult, None, [Cw.b], [c2f.b])
    P.ms(c1p[:], 0.0, [c1p.b])
    P.ms(c2p[:], 0.0, [c2p.b])
    for gi in range(16):
        o = 16 * (gi % 8)
        P.cp(c1p[:, gi, o:o + 16], c1f[:, 16 * gi:16 * gi + 16], [c1f.b], [c1p.b])
        P.cp(c2p[:, gi, o:o + 16], c2f[:, 16 * gi:16 * gi + 16], [c2f.b], [c2p.b])
    P.barrier()
    es2.close()
    sb = gsb
    pbr = Rot(banks[0:4])
    yps = [banks[4], banks[5]]
    m1 = Rot([sb("m1_%d" % i, [128, 512], BF) for i in range(4)])
    m2 = Rot([sb("m2_%d" % i, [128, 512], BF) for i in range(4)])
    bt = Rot([sb("bt_%d" % i, [128, 512], BF) for i in range(4)])
    xs = Rot([sb("xs_%d" % i, [128, 512], BF) for i in range(4)])
    d1 = Rot([sb("d1_%d" % i, [128, 512], BF) for i in range(4)])
    d2 = Rot([sb("d2_%d" % i, [128, 512], BF) for i in range(4)])
    rdr = Rot([None])
    pbs = Rot([sb("pbs_%d" % i, [128, 512], BF) for i in range(4)])
    pws = Rot([sb("pws_%d" % i, [128, 512], BF) for i in range(4)])
    ep = {}
    for hf in range(2):
        ep["yv%d" % hf] = sb("yv%d" % hf, [128, 512])
        w_ = sb("wk%d" % hf, [128, 512])
        for n in ("sq", "in", "th", "sg", "o1"):
            ep["%s%d" % (n, hf)] = w_
        ep["o2%d" % hf] = sb("o2%d" % hf, [128, 512], BF)
    gf = [sb("gf%d" % i, [128, 512]) for i in range(2)]
    gb16 = [sb("gb%d" % i, [128, 512], BF) for i in range(2)]
    uTs = [sb("uT%d" % i, [16, 16, 512], BF) for i in range(2)]
    ufs = [sb("uf%d" % i, [128, 2, 512], BF) for i in range(2)]
    gss = [sb("gsT%d" % i, [128, 2, 512], BF) for i in range(2)]
    pc = banks[6]

    def load(c):
        cs_ = slice(c * 512, (c + 1) * 512)
        P.dma(uTs[c % 2][:], g.UT[:, cs_].rearrange("(g c) t -> c g t", c=16), (), [uTs[c % 2].b])
        P.dma(ufs[c % 2][:], g.UT[:, cs_].rearrange("(j p) t -> p j t", p=128), (), [ufs[c % 2].b])
        P.dma(gss[c % 2][:], g.GS[256:512, cs_].rearrange("(j p) t -> p j t", p=128), (), [gss[c % 2].b])

    def epi_a(c):
        uf = ufs[c % 2]
        for hf in range(2):
            yv = ep["yv%d" % hf]
            P.stt(yv[:], uf[:, hf, :], dcol[:, hf:hf + 1], yps[hf][:], ALU.mult, ALU.add, [uf.b, dcol.b, yps[hf].b], [yv.b])

    def epi_b(c):
        cs = slice(c * 512, (c + 1) * 512)
        gsT = gss[c % 2]
        for hf in range(2):
            yv, sq, inn, th = ep["yv%d" % hf], ep["sq%d" % hf], ep["in%d" % hf], ep["th%d" % hf]
            P.tt(sq[:], yv[:], yv[:], ALU.mult, [yv.b], [sq.b])
            P.ts(sq[:], sq[:], 0.044715, 1.0, ALU.mult, ALU.add, [sq.b], [sq.b])
            P.tt(inn[:], sq[:], yv[:], ALU.mult, [sq.b, yv.b], [inn.b])
            P.act(th[:], inn[:], AF.Tanh, [inn.b], [th.b], scale=0.7978845608028654)
            P.stt(gf[hf][:], th[:], 1.0, yv[:], ALU.add, ALU.mult, [th.b, yv.b], [gf[hf].b])
            P.ts(gf[hf][:], gf[hf][:], 0.5, None, ALU.mult, None, [gf[hf].b], [gf[hf].b])
            P.cp(gb16[hf][:], gf[hf][:], [gf[hf].b], [gb16[hf].b])
        for oh in range(2):
            ps = banks[7]
            for ih in range(2):
                P.mm(ps[:], gwb[:, ih, oh * 128:(oh + 1) * 128], gb16[ih][:], ih == 0, ih == 1, [gwb.b, gb16[ih].b], [ps.b])
            sg, o1, o2 = ep["sg%d" % oh], ep["o1%d" % oh], ep["o2%d" % oh]
            P.act(sg[:], ps[:], AF.Sigmoid, [ps.b, gbc.b], [sg.b], bias=gbc[:, oh:oh + 1], scale=1.0)
            P.tt(o1[:], gf[oh][:], sg[:], ALU.mult, [gf[oh].b, sg.b], [o1.b])
            P.tt(o2[:], o1[:], gsT[:, oh, :], ALU.mult, [o1.b, gsT.b], [o2.b])
            P.dma(g.MIX[256 + oh * 128:256 + (oh + 1) * 128, cs], o2[:], [o2.b], [Buf()])


    load(0)
    if NCH > 1:
        load(1)
    xins = [xin, sb("xin1", [128, 16])]
    P.ms(xins[0][:], 0.0, [xins[0].b])

    def in_mm(c, gp):
        uT = uTs[c % 2]
        T = {}
        for gi in (gp, gp + 1):
            pb, psw = pbr.next(), pbr.next()
            P.mm(pb[:], LB[:, gi, :], uT[:, gi, :], True, True, [LB.b, uT.b], [pb.b])
            P.mm(psw[:], LS[:, gi, :], uT[:, gi, :], True, True, [LS.b, uT.b], [psw.b])
            pb_s, pw_s = pbs.next(), pws.next()
            P.cp(pb_s[:], pb[:], [pb.b], [pb_s.b], eng="act")
            P.cp(pw_s[:], psw[:], [psw.b], [pw_s.b], eng="act")
            T[gi] = (pb_s, pw_s, m1.next(), m2.next(), bt.next(), xs.next(), d1.next(), d2.next(), rdr.next())
        return T

    pairs = [(c, gp) for c in range(NCH) for gp in range(0, 16, 2)]
    Tnext = in_mm(0, 0)
    for pi, (c, gp) in enumerate(pairs):
        cs = slice(c * 512, (c + 1) * 512)
        cout = couts[c % 2]
        xcur, xnxt = xins[c % 2], xins[(c + 1) % 2]
        gl = (gp, gp + 1)
        if gp == 4 and c > 0:
            epi_b(c - 1)
            if c + 1 < NCH:
                load(c + 1)
        T = Tnext
        for gi in gl:
            pb, psw, a1, a2, b_, x_, e1, e2, rd_ = T[gi]
            P.tt(a1[:], pb[:], Ct[:, gi, :], ALU.mult, [pb.b, Ct.b], [a1.b])
        for gi in gl:
            pb, psw, a1, a2, b_, x_, e1, e2, rd_ = T[gi]
            P.tt(a2[:], psw[:], St[:, gi, :], ALU.mult, [psw.b, St.b], [a2.b])
        if pi + 1 < len(pairs):
            Tnext = in_mm(*pairs[pi + 1])
        for gi in gl:
            pb, psw, a1, a2, b_, x_, e1, e2, rd_ = T[gi]
            P.tt(b_[:], a1[:], a2[:], ALU.add, [a1.b, a2.b], [b_.b])
        for gi in gl:
            pb, psw, a1, a2, b_, x_, e1, e2, rd_ = T[gi]
            P.op("dve", lambda e, x_=x_, b_=b_, gi=gi, xc=xcur: e.tensor_tensor_scan(x_[:], Rdec[:, gi, :], b_[:], xc[:, gi:gi + 1], ALU.mult, ALU.add),
                 [Rdec.b, b_.b, xcur.b], [x_.b])
        for gi in gl:
            x_ = T[gi][5]
            P.cp(cout[:, gi:gi + 1], x_[:, 511:512], [x_.b], [cout.b], eng="act")
        if c + 1 < NCH:
            for gi in gl:
                P.mm(pc[:, gi:gi + 1], Mrot[:, gi, :], cout[:, gi:gi + 1], True, True, [Mrot.b, cout.b], [pc.b])
        for gi in gl:
            pb, psw, a1, a2, b_, x_, e1, e2, rd_ = T[gi]
            P.tt(e1[:], x_[:], Ct[:, gi, :], ALU.mult, [x_.b, Ct.b], [e1.b])
        for gi in gl:
            pb, psw, a1, a2, b_, x_, e1, e2, rd_ = T[gi]
            P.tt(e2[:], x_[:], St[:, gi, :], ALU.mult, [x_.b, St.b], [e2.b])
        if c + 1 < NCH:
            P.cp(xnxt[:, gp:gp + 2], pc[:, gp:gp + 2], [pc.b], [xnxt.b], eng="act")
        for gi in gl:
            pb, psw, a1, a2, b_, x_, e1, e2, rd_ = T[gi]
            yp = yps[gi // 8]
            P.mm(yp[:], c1p[:, gi, :], e1[:], gi % 8 == 0, False, [c1p.b, e1.b], [yp.b])
            P.mm(yp[:], c2p[:, gi, :], e2[:], False, gi % 8 == 7, [c2p.b, e2.b], [yp.b])
        if gp == 14:
            epi_a(c)
    epi_b(NCH - 1)

    esp.close()
def _bf(a):
    return np.asarray(a, dtype=np.float32).astype(ml_dtypes.bfloat16)


def host_consts(S):
    pos = np.arange(S, dtype=np.float32)
    rope = np.zeros((2, 2, 128, S), np.float32)
    for bi, half in enumerate((32, 16)):
        inv = np.power(np.float32(10000.0), -np.arange(half, dtype=np.float32) / np.float32(half)).astype(np.float32)
        ang = (pos[None, :] * inv[:, None]).astype(np.float32)
        rows = np.arange(128) % half
        rope[0, bi] = np.cos(ang)[rows]
        rope[1, bi] = np.sin(ang)[rows]
    ident = _bf(np.eye(128))
    r = np.arange(128)
    negtri = _bf(np.where(r[:, None] <= r[None, :], 0.0, NEGB))
    blkind = _bf((np.arange(S)[None, :] // 256) == np.arange(32)[:, None])
    return {"rope": rope, "ident": ident, "negtri": negtri, "blkind": blkind}


def host_layout(inp, S, depth):
    f = lambda a: np.ascontiguousarray(np.asarray(a, dtype=np.float32))
    L = depth
    m = {}
    m["w_in"] = f(np.asarray(inp["w_in"])[:L])
    m["w_out"] = f(np.asarray(inp["w_out"])[:L])
    m["norm_g"] = f(np.asarray(inp["norm_g"])[:L].reshape(L, 8, 128).transpose(0, 2, 1))
    m["final_g"] = f(np.broadcast_to(np.asarray(inp["final_g"]).reshape(1, D), (128, D)))
    m["fox_fb"] = f(np.asarray(inp["fox_fb"])[:L].reshape(L, 4, 1))
    m["mla_w_uq"] = f(np.asarray(inp["mla_w_uq"])[:L])
    m["mla_q_norm"] = f(np.asarray(inp["mla_q_norm"])[:L].reshape(L, 3, 128).transpose(0, 2, 1))
    m["mla_w_ukv"] = f(np.asarray(inp["mla_w_ukv"])[:L])
    m["mla_kv_norm"] = f(np.asarray(inp["mla_kv_norm"])[:L].reshape(L, 128, 1))
    are, aim, ldt = (np.asarray(inp[k])[:L] for k in ("s5_a_re", "s5_a_im", "s5_log_dt"))
    ldtb = np.broadcast_to(ldt[:, :, None], (L, 16, 64))
    tT = lambda a: np.concatenate([a.transpose(0, 2, 1)] * 2, axis=1)
    m["s5_a"] = f(np.stack([tT(are), tT(aim), tT(ldtb)], axis=1))
    rep = lambda a: np.broadcast_to(a.reshape(L, 1, 1024), (L, 16, 1024))
    m["s5_a16"] = f(np.stack([rep(are), rep(aim), rep(ldtb)], axis=1))
    bre, bim = (np.asarray(inp[k])[:L] for k in ("s5_b_re", "s5_b_im"))
    tb = lambda a: a.transpose(0, 3, 1, 2).reshape(L, 16, 1024)
    m["s5_b"] = f(np.stack([tb(bre), tb(bim)], axis=1))
    cre, cim = (np.asarray(inp[k])[:L] for k in ("s5_c_re", "s5_c_im"))
    tc_ = lambda a: a.transpose(0, 3, 1, 2).reshape(L, 64, 256)
    m["s5_c"] = f(np.stack([np.concatenate([tc_(cre), tc_(cim)], axis=1),
                            np.concatenate([tc_(cim), tc_(cre)], axis=1)], axis=1))
    m["s5_d"] = f(np.asarray(inp["s5_d"])[:L].reshape(L, 2, 128).transpose(0, 2, 1))
    m["s5_glu_w"] = f(np.asarray(inp["s5_glu_w"])[:L])
    m["s5_glu_b"] = f(np.asarray(inp["s5_glu_b"])[:L].reshape(L, 2, 128).transpose(0, 2, 1))
    return m


def fused_consts(S):
    NCH = S // 512
    cst = host_consts(S)
    r = np.arange(128)
    negm = np.stack([np.where((r[:, None] + 128 * j) <= np.arange(512)[None, :], 0.0, NEGB) for j in range(4)], axis=1)
    cst["negm"] = _bf(negm)
    cst.pop("negtri", None)
    cst["identf"] = np.eye(128, dtype=np.float32)
    cst["swapj"] = np.roll(np.eye(128, dtype=np.float32), 64, axis=1).copy()
    cm = np.full((NCH, 128, 4, 4, 32), -1e30, np.float32)
    no = np.ones((NCH, 128, 4, 4, 32), np.float32)
    for c in range(NCH):
        cm[c, :, :, 0:2, 0:2 * c] = 0.0
        cm[c, :, :, 2:4, 0:2 * c + 1] = 0.0
        no[c, :, :, 0:2, 2 * c] = 0.0
        no[c, :, :, 2:4, 2 * c + 1] = 0.0
    cst["cmask"] = cm.reshape(NCH, 128, 512)
    cst["notown"] = no.reshape(NCH, 128, 512)
    return cst


def fused_maps(inputs, S, depth):
    m = host_layout(inputs, S, depth)
    m.update(fused_consts(S))
    return m


def kernel(**inputs):
    x = np.ascontiguousarray(np.asarray(inputs["x"], dtype=np.float32))
    B, S, _ = x.shape
    nc = build_fused(S, DEPTH)
    shared = fused_maps(inputs, S, DEPTH)
    in_maps = []
    for b in range(B):
        mp = dict(shared)
        mp["x"] = np.ascontiguousarray(x[b])
        in_maps.append(mp)
    res = run_bass_kernel_spmd(nc, in_maps, core_ids=list(range(B)))
    return np.stack([np.asarray(r["out"], dtype=np.float32) for r in res.results], axis=0)
```
